# Optimizing a Trainium2 kernel written in Bass

```python
import jax, jax.numpy as jnp
from jax import lax
import numpy as np

D_MODEL = 1024
BATCH = 4
SEQ = 8192
DEPTH = 2

HEAD_DIM = 64
ROPE_THETA = 10000.0
EPS = 1e-6
NEG_INF = -1e30
Q_BLOCK = 128

MLA_HEADS = 8
MLA_Q_RANK = 256
MLA_KV_RANK = 128
MLA_NOPE_DIM = 64
MLA_ROPE_DIM = 32
MLA_V_DIM = 64
DIL_PATTERNS = ((128, 1), (512, 4), (2048, 16))
DIL_HEADS_PER_GROUP = 4
DIL_HEADS = DIL_HEADS_PER_GROUP * len(DIL_PATTERNS)
BAND_BLOCK = 128
NA_HEADS = 8
NA_KH = 8
NA_KW = 16
GRID_W = 64
N_BRANCH = 3
D_FF = -(-(8 * D_MODEL) // (3 * 256)) * 256

A_COLS = MLA_Q_RANK + MLA_KV_RANK + MLA_ROPE_DIM
B_COLS = 3 * DIL_HEADS * HEAD_DIM
C_COLS = 3 * NA_HEADS * HEAD_DIM
G_COLS = N_BRANCH * D_MODEL
IN_COLS = A_COLS + B_COLS + C_COLS + G_COLS
IN_SPLITS = (MLA_Q_RANK, MLA_Q_RANK + MLA_KV_RANK, A_COLS, A_COLS + B_COLS, A_COLS + B_COLS + C_COLS)
A_OUT = MLA_HEADS * MLA_V_DIM
B_OUT = DIL_HEADS_PER_GROUP * HEAD_DIM
C_OUT = NA_HEADS * HEAD_DIM

kernel_name = 'hybrid_mla_dilated_neighbourhood_encoder'


def rms_norm(x, g):
    xf = x.astype(jnp.float32)
    y = xf * lax.rsqrt(jnp.mean(xf * xf, axis=-1, keepdims=True) + EPS)
    return (y * g.astype(jnp.float32)).astype(x.dtype)


def rope_tables(seq_len, dim):
    pos = jnp.arange(seq_len, dtype=jnp.float32)
    inv = jnp.power(ROPE_THETA, -jnp.arange(0, dim, 2, dtype=jnp.float32) / dim)
    ang = pos[:, None] * inv[None, :]
    return jnp.cos(ang), jnp.sin(ang)


def apply_rope(x, cos, sin):
    xf = x.astype(jnp.float32)
    x1, x2 = jnp.split(xf, 2, axis=-1)
    return jnp.concatenate([x1 * cos - x2 * sin, x2 * cos + x1 * sin], axis=-1).astype(x.dtype)


def mla_mixer(c_q, c_kv, k_rope, g_q, g_kv, w_uq, w_ukv):
    B, S = c_q.shape[0], c_q.shape[1]
    q = (rms_norm(c_q, g_q) @ w_uq).reshape(B, S, MLA_HEADS, MLA_NOPE_DIM + MLA_ROPE_DIM)
    kv = (rms_norm(c_kv, g_kv) @ w_ukv).reshape(B, S, MLA_HEADS, MLA_NOPE_DIM + MLA_V_DIM)
    cos, sin = rope_tables(S, MLA_ROPE_DIM)
    q_nope = q[..., :MLA_NOPE_DIM]
    q_rope = apply_rope(q[..., MLA_NOPE_DIM:], cos[:, None], sin[:, None])
    k_nope, v = kv[..., :MLA_NOPE_DIM], kv[..., MLA_NOPE_DIM:]
    k_rope = apply_rope(k_rope, cos, sin)
    scale = (MLA_NOPE_DIM + MLA_ROPE_DIM) ** -0.5
    nb = S // Q_BLOCK

    def to_blocks(t):
        return t.reshape(B, nb, Q_BLOCK, *t.shape[2:]).swapaxes(0, 1)

    def attend(blk):
        qn, qr = blk
        s = (jnp.einsum('bqhd,bkhd->bhqk', qn, k_nope, preferred_element_type=jnp.float32)
             + jnp.einsum('bqhr,bkr->bhqk', qr, k_rope, preferred_element_type=jnp.float32)) * scale
        p = jax.nn.softmax(s, axis=-1).astype(v.dtype)
        return jnp.einsum('bhqk,bkhd->bqhd', p, v)

    o = lax.map(attend, (to_blocks(q_nope), to_blocks(q_rope)))
    return o.swapaxes(0, 1).reshape(B, S, A_OUT)


def banded_attention(q, k, v, radius):
    N, L, H, D = q.shape
    nb = -(-L // BAND_BLOCK)
    Lp = nb * BAND_BLOCK
    kw = BAND_BLOCK + 2 * radius
    qb = jnp.pad(q, ((0, 0), (0, Lp - L), (0, 0), (0, 0))).reshape(N, nb, BAND_BLOCK, H, D)
    pad_k = ((0, 0), (radius, Lp - L + radius), (0, 0), (0, 0))
    idx = jnp.arange(nb)[:, None] * BAND_BLOCK + jnp.arange(kw)[None, :]
    kb = jnp.pad(k, pad_k)[:, idx]
    vb = jnp.pad(v, pad_k)[:, idx]
    s = jnp.einsum('ncqhd,nckhd->nchqk', qb, kb, preferred_element_type=jnp.float32) * (D ** -0.5)
    qpos = jnp.arange(nb)[:, None] * BAND_BLOCK + jnp.arange(BAND_BLOCK)[None, :]
    kpos = (idx - radius)[:, None, :]
    mask = (jnp.abs(qpos[:, :, None] - kpos) <= radius) & (kpos >= 0) & (kpos < L)
    s = jnp.where(mask[None, :, None], s, NEG_INF)
    m = jnp.max(s, axis=-1, keepdims=True)
    e = jnp.exp(s - m)
    den = jnp.sum(e, axis=-1, keepdims=True)
    p = (e / den).astype(v.dtype)
    lse = (m + jnp.log(den))[..., 0]
    o = jnp.einsum('nchqk,nckhd->ncqhd', p, vb).reshape(N, Lp, H, D)[:, :L]
    lse = lse.transpose(0, 1, 3, 2).reshape(N, Lp, H)[:, :L]
    return o, lse


def dilated_mixer(q, k, v):
    B, S = q.shape[0], q.shape[1]
    H, D = DIL_HEADS_PER_GROUP, HEAD_DIM
    outs, lses = [], []
    for g, (window, dilation) in enumerate(DIL_PATTERNS):
        hs = slice(g * H, (g + 1) * H)
        L = S // dilation

        def split(t):
            return t[:, :, hs].reshape(B, L, dilation, H, D).swapaxes(1, 2).reshape(B * dilation, L, H, D)

        o, lse = banded_attention(split(q), split(k), split(v), window // (2 * dilation))
        outs.append(o.reshape(B, dilation, L, H, D).swapaxes(1, 2).reshape(B, S, H, D))
        lses.append(lse.reshape(B, dilation, L, H).swapaxes(1, 2).reshape(B, S, H))
    alpha = jax.nn.softmax(jnp.stack(lses), axis=0).astype(q.dtype)
    o = jnp.einsum('gbsh,gbshd->bshd', alpha, jnp.stack(outs))
    return o.reshape(B, S, B_OUT)


def neighbourhood_mixer(q, k, v, rpb):
    B, S, H, D = q.shape
    rows = S // GRID_W
    kh = min(NA_KH, rows)
    r = jnp.arange(rows)
    c = jnp.arange(GRID_W)
    rs = jnp.clip(r - kh // 2, 0, rows - kh)
    cs = jnp.clip(c - NA_KW // 2, 0, GRID_W - NA_KW)
    key_cols = cs[:, None] + jnp.arange(NA_KW)[None, :]
    dc = key_cols - c[:, None] + (NA_KW - 1)
    scale = D ** -0.5

    def attend_row(inp):
        q_row, r0, r_i = inp
        key_rows = r0 + jnp.arange(kh)
        idx = key_rows[None, :, None] * GRID_W + key_cols[:, None, :]
        kg = k[:, idx].reshape(B, GRID_W, kh * NA_KW, H, D)
        vg = v[:, idx].reshape(B, GRID_W, kh * NA_KW, H, D)
        dr = key_rows - r_i + (NA_KH - 1)
        bias = rpb[:, dr[None, :, None], dc[:, None, :]].reshape(H, GRID_W, kh * NA_KW)
        s = jnp.einsum('bchd,bcnhd->bhcn', q_row, kg, preferred_element_type=jnp.float32) * scale
        s = s + bias.astype(jnp.float32)[None]
        p = jax.nn.softmax(s, axis=-1).astype(v.dtype)
        return jnp.einsum('bhcn,bcnhd->bchd', p, vg)

    q_rows = q.reshape(B, rows, GRID_W, H, D).swapaxes(0, 1)
    o = lax.map(attend_row, (q_rows, rs, r))
    return o.swapaxes(0, 1).reshape(B, S, C_OUT)


def setup_inputs(seed: int = 0) -> dict:
    key = jax.random.key(seed)
    ks = jax.random.split(key, 17)

    def normal(k, shape, fan_in):
        return jax.random.normal(k, shape, jnp.float32) * (fan_in ** -0.5)

    def gain(k, shape):
        return 1.0 + 0.01 * jax.random.normal(k, shape, jnp.float32)

    return {
        'x': jax.random.normal(ks[0], (BATCH, SEQ, D_MODEL), jnp.float32),
        'w_in': normal(ks[1], (DEPTH, D_MODEL, IN_COLS), D_MODEL),
        'g_mix': gain(ks[2], (DEPTH, D_MODEL)),
        'g_q': gain(ks[3], (DEPTH, MLA_Q_RANK)),
        'g_kv': gain(ks[4], (DEPTH, MLA_KV_RANK)),
        'w_uq': normal(ks[5], (DEPTH, MLA_Q_RANK, MLA_HEADS * (MLA_NOPE_DIM + MLA_ROPE_DIM)), MLA_Q_RANK),
        'w_ukv': normal(ks[6], (DEPTH, MLA_KV_RANK, MLA_HEADS * (MLA_NOPE_DIM + MLA_V_DIM)), MLA_KV_RANK),
        'rpb': 0.1 * jax.random.normal(ks[7], (DEPTH, NA_HEADS, 2 * NA_KH - 1, 2 * NA_KW - 1), jnp.float32),
        'w_pa': normal(ks[8], (DEPTH, A_OUT, D_MODEL), A_OUT),
        'w_pb': normal(ks[9], (DEPTH, B_OUT, D_MODEL), B_OUT),
        'w_pc': normal(ks[10], (DEPTH, C_OUT, D_MODEL), C_OUT),
        'w_o': normal(ks[11], (DEPTH, D_MODEL, D_MODEL), D_MODEL),
        'g_ffn': gain(ks[12], (DEPTH, D_MODEL)),
        'w1': normal(ks[13], (DEPTH, D_MODEL, D_FF), D_MODEL),
        'w3': normal(ks[14], (DEPTH, D_MODEL, D_FF), D_MODEL),
        'w2': normal(ks[15], (DEPTH, D_FF, D_MODEL), D_FF),
        'g_final': gain(ks[16], (D_MODEL,)),
    }


def reference(x, w_in, g_mix, g_q, g_kv, w_uq, w_ukv, rpb, w_pa, w_pb, w_pc, w_o, g_ffn, w1, w3, w2, g_final):
    B, S = x.shape[0], x.shape[1]
    cos_b, sin_b = rope_tables(S, HEAD_DIM)
    for l in range(DEPTH):
        h = rms_norm(x, g_mix[l])
        proj = h @ w_in[l]
        c_q, c_kv, k_r, qkv_b, qkv_c, gate_logits = jnp.split(proj, IN_SPLITS, axis=-1)
        y_a = mla_mixer(c_q, c_kv, k_r, g_q[l], g_kv[l], w_uq[l], w_ukv[l])
        qkv_b = qkv_b.reshape(B, S, 3, DIL_HEADS, HEAD_DIM)
        q_b = apply_rope(qkv_b[:, :, 0], cos_b[:, None], sin_b[:, None])
        k_b = apply_rope(qkv_b[:, :, 1], cos_b[:, None], sin_b[:, None])
        y_b = dilated_mixer(q_b, k_b, qkv_b[:, :, 2])
        qkv_c = qkv_c.reshape(B, S, 3, NA_HEADS, HEAD_DIM)
        y_c = neighbourhood_mixer(qkv_c[:, :, 0], qkv_c[:, :, 1], qkv_c[:, :, 2], rpb[l])
        gates = jax.nn.sigmoid(gate_logits.astype(jnp.float32)).astype(x.dtype).reshape(B, S, N_BRANCH, D_MODEL)
        merged = (gates[:, :, 0] * (y_a @ w_pa[l])
                  + gates[:, :, 1] * (y_b @ w_pb[l])
                  + gates[:, :, 2] * (y_c @ w_pc[l]))
        x = x + merged @ w_o[l]
        h = rms_norm(x, g_ffn[l])
        x = x + (jax.nn.silu(h @ w1[l]) * (h @ w3[l])) @ w2[l]
    return rms_norm(x, g_final)
```

```python
import numpy as np
import concourse.bass as bass
import concourse.mybir as mybir
from concourse.bass_utils import run_bass_kernel_spmd

F32 = mybir.dt.float32
BF16 = mybir.dt.bfloat16
U8 = mybir.dt.uint8
AF = mybir.ActivationFunctionType
ALU = mybir.AluOpType

D = 1024
S = 8192
OWN = 4096
NB = 8
EPS = 1e-6
IN_COLS = 7328
DFF = 2816
EXTB = 6144
EXTC = 5120
ISZ = {F32: 4, BF16: 2, U8: 1}
ENGS = ['tensor', 'vector', 'scalar', 'gpsimd', 'sync']
CH = 16000
DMAK = 12


class Prog:
    def __init__(self, nc):
        self.nc = nc
        self.ops = []
        self.lastw = {}
        self.readers = {}
        self.barrier = {e: None for e in ENGS}

    def add(self, eng, fn, reads=(), writes=(), dma=False):
        i = len(self.ops)
        deps = set()
        for r in reads:
            j = self.lastw.get(r)
            if j is not None:
                deps.add(j)
        for w in writes:
            j = self.lastw.get(w)
            if j is not None:
                deps.add(j)
            deps.update(self.readers.get(w, ()))
        if self.barrier[eng] is not None:
            deps.update(self.barrier[eng])
            self.barrier[eng] = None
        for r in reads:
            self.readers.setdefault(r, []).append(i)
        for w in writes:
            self.lastw[w] = i
            self.readers[w] = []
        self.ops.append(dict(eng=eng, fn=fn, deps=deps, dma=dma))
        return i

    def phase_barrier(self):
        last = set()
        seen_c = set()
        seen_d = {}
        for i in range(len(self.ops) - 1, -1, -1):
            o = self.ops[i]
            if o['dma']:
                c = seen_d.get(o['eng'], 0)
                if c < DMAK:
                    last.add(i)
                    seen_d[o['eng']] = c + 1
            elif o['eng'] not in seen_c:
                seen_c.add(o['eng'])
                last.add(i)
            if len(seen_c) >= 4 and all(seen_d.get(e, 0) >= DMAK for e in ('sync', 'gpsimd')):
                break
        for e in ENGS:
            self.barrier[e] = set(last)
        self.lastw = {}
        self.readers = {}

    def emit(self):
        nc = self.nc
        ops = self.ops
        n = len(ops)
        needed = [False] * n
        for o in ops:
            for j in o['deps']:
                oj = ops[j]
                if oj['eng'] == 'tensor' and o['eng'] == 'tensor' and not oj['dma'] and not o['dma']:
                    continue
                needed[j] = True
        sig = [None] * n
        ccount = {e: 0 for e in ENGS}
        dcount = {e: 0 for e in ENGS}
        csems = {e: [] for e in ENGS}
        dsems = {e: [] for e in ENGS}
        dprev = [None] * n
        for i, o in enumerate(ops):
            e = o['eng']
            if o['dma']:
                k = dcount[e]
                dcount[e] += 1
                slot = k % DMAK
                if slot >= len(dsems[e]):
                    dsems[e].append(nc.alloc_semaphore('d_%s_%d' % (e, slot)))
                val = 16 * (k // DMAK + 1)
                sig[i] = (dsems[e][slot], val)
                if val > 16:
                    dprev[i] = (dsems[e][slot], val - 16)
            elif needed[i]:
                k = ccount[e]
                ccount[e] += 1
                si = k // CH
                if si >= len(csems[e]):
                    csems[e].append(nc.alloc_semaphore('c_%s_%d' % (e, si)))
                sig[i] = (csems[e][si], k % CH + 1)
        per = {e: [] for e in ENGS}
        for i, o in enumerate(ops):
            per[o['eng']].append(i)
        finals = []
        for e in ENGS:
            k = dcount[e]
            for slot in range(min(k, DMAK)):
                cnt = (k - 1 - slot) // DMAK + 1
                finals.append((dsems[e][slot], 16 * cnt))

        def run(eng_name, e):
            waited = {}

            def wait(sem, val):
                key = sem.num
                if waited.get(key, 0) < val:
                    e.wait_ge(sem, val)
                    waited[key] = val
            for i in per[eng_name]:
                o = ops[i]
                for j in sorted(o['deps']):
                    if sig[j] is None:
                        continue
                    oj = ops[j]
                    if oj['eng'] == 'tensor' and eng_name == 'tensor' and not oj['dma'] and not o['dma']:
                        continue
                    wait(*sig[j])
                if dprev[i] is not None:
                    wait(*dprev[i])
                ins = o['fn'](e)
                if sig[i] is not None:
                    ins.then_inc(sig[i][0], 16 if o['dma'] else 1)
            if eng_name == 'sync':
                for sem, val in finals:
                    wait(sem, val)

        with nc.Block() as block:
            @block.tensor
            def _(e):
                run('tensor', e)

            @block.vector
            def _(e):
                run('vector', e)

            @block.scalar
            def _(e):
                run('scalar', e)

            @block.gpsimd
            def _(e):
                run('gpsimd', e)

            @block.sync
            def _(e):
                run('sync', e)


class Arena:
    def __init__(self, nc, nbytes):
        self.t = nc.alloc_sbuf_tensor('arena', [128, nbytes], U8)
        self.cap = nbytes
        self.off = 0
        self.n = 0

    def alloc(self, parts, elems, dt, p0=0):
        size = elems * ISZ[dt]
        size = (size + 63) // 64 * 64
        assert self.off + size <= self.cap, ('SBUF overflow', self.off, size)
        ap = self.t[p0:p0 + parts, self.off:self.off + elems * ISZ[dt]].bitcast(dt)
        self.off += size
        self.n += 1
        return ap

    def mark(self):
        return self.off

    def release(self, m):
        self.off = m


class Buf:
    _cnt = [0]

    def __init__(self, arena, parts, elems, dt, nslot=1, p0=0):
        Buf._cnt[0] += 1
        self.id = Buf._cnt[0]
        self.aps = [arena.alloc(parts, elems, dt, p0) for _ in range(nslot)]
        self.nslot = nslot
        self.i = -1

    def next(self):
        self.i += 1
        return self.i % self.nslot

    def ap(self, s):
        return self.aps[s]

    def key(self, s, sub=None):
        return ('b', self.id, s, sub)


class Builder:
    def __init__(self, nc, dbg=None):
        self.nc = nc
        self.p = Prog(nc)
        self.arena = Arena(nc, 206 * 1024)
        self.ps = nc.alloc_psum_tensor('ps', [128, 4096], F32)
        self.psi = -1
        self.dr = {}
        self.dbg = dbg
        self.did = 0

    def dram_in(self, name, shape, dt=F32):
        t = self.nc.dram_tensor(name, list(shape), dt, kind="ExternalInput").ap()
        self.dr[name] = t
        return t

    def dram_out(self, name, shape, dt=F32):
        t = self.nc.dram_tensor(name, list(shape), dt, kind="ExternalOutput").ap()
        self.dr[name] = t
        return t

    def dram_scr(self, name, shape, dt=BF16):
        kind = "ExternalOutput" if (self.dbg and name in self.dbg) else "Internal"
        t = self.nc.dram_tensor(name, list(shape), dt, kind=kind).ap()
        self.dr[name] = t
        return t

    def bank(self, b):
        return self.ps[:, b * 512:(b + 1) * 512]

    def pk(self, b):
        return ('ps', b)

    def dma(self, out, in_, reads, writes, q='sync'):
        self.p.add(q, lambda e, o=out, i=in_: e.dma_start(out=o, in_=i), reads, writes, dma=True)

    def mm(self, out, lhsT, rhs, start, stop, reads, writes, skip=False):
        def f(e, o=out, l=lhsT, r=rhs, s=start, t=stop, k=skip):
            if k:
                return e.matmul(o, l, r, start=s, stop=t, skip_group_check=True)
            return e.matmul(o, l, r, start=s, stop=t)
        self.p.add('tensor', f, reads, writes)

    def act(self, out, in_, func, reads, writes, scale=None, bias=None, accum=None):
        def f(e, o=out, i=in_, fn=func, sc=scale, bi=bias, ac=accum):
            kw = {}
            if sc is not None:
                kw['scale'] = sc
            if bi is not None:
                kw['bias'] = bi
            if ac is not None:
                kw['accum_out'] = ac
            return e.activation(o, i, fn, **kw)
        self.p.add('scalar', f, reads, writes)

    def tt(self, eng, out, in0, in1, op, reads, writes):
        self.p.add(eng, lambda e, o=out, a=in0, b=in1, p=op: e.tensor_tensor(o, a, b, p), reads, writes)

    def ts(self, eng, out, in0, s1, s2, op0, op1, reads, writes):
        def f(e, o=out, a=in0, x=s1, y=s2, p=op0, q=op1):
            if q is None:
                return e.tensor_scalar(o, a, x, None, p)
            return e.tensor_scalar(o, a, x, y, p, q)
        self.p.add(eng, f, reads, writes)

    def stt(self, out, in0, scalar, in1, op0, op1, reads, writes):
        self.p.add('vector', lambda e, o=out, a=in0, s=scalar, b=in1, p=op0, q=op1:
                   e.scalar_tensor_tensor(o, a, s, b, p, q), reads, writes)

    def copy(self, eng, out, in_, reads, writes):
        if eng == 'scalar':
            self.act(out, in_, AF.Copy, reads, writes)
        else:
            self.p.add(eng, lambda e, o=out, i=in_: e.tensor_copy(o, i), reads, writes)

    def memset(self, eng, ap, val, writes):
        self.p.add(eng, lambda e, a=ap, v=val: e.memset(a, v), (), writes)

    def recip(self, out, in_, reads, writes):
        self.p.add('vector', lambda e, o=out, i=in_: e.reciprocal(o, i), reads, writes)

    def transpose(self, out, in_, ident, reads, writes):
        self.p.add('tensor', lambda e, o=out, i=in_, d=ident: e.transpose(o, i, d), reads, writes)


def _perm_local(h):
    own = np.arange(h * OWN, (h + 1) * OWN)
    par = np.arange((1 - h) * OWN, (2 - h) * OWN)
    return np.concatenate([own, par])


class Layer(Builder):
    def nb(self):
        self.psi = (self.psi + 1) % 8
        return self.psi

    def setup_consts(self):
        a = self.arena
        d = self.dr
        self.ident = a.alloc(128, 128, BF16)
        self.ones = a.alloc(128, 128, BF16)
        self.epsA = a.alloc(128, 1, F32)
        stage = a.alloc(128, 128, F32)
        self.dma(stage, d['ident'], [], ['c_stage'])
        self.copy('vector', self.ident, stage, ['c_stage'], ['ident'])
        self.memset('vector', self.ones, 1.0, ['ones'])
        self.memset('vector', self.epsA, EPS, ['eps'])
        self.onescol = a.alloc(128, 12 * 2, BF16)
        self.memset('vector', self.onescol, 1.0, ['onescol'])
        self.valB = a.alloc(128, 48, F32)
        self.valOne = a.alloc(128, 64, F32)
        self.memset('vector', self.valOne, 1.0, ['valOne'])

    def begin_pass(self, L, cs, xsrc, swap, out_raw, final):
        self.L, self.cs, self.xsrc, self.swap, self.out_raw, self.final = L, cs, xsrc, swap, out_raw, final
        self.dma(self.valB, self.dr['valB' + cs], [], ['valB'])
        self.p.phase_barrier()

    def xrow(self, u):
        return (u + OWN) % S if self.swap else u

    def norm_transpose(self, xt_ap, xt_key, gbc, gkey, hT_dst, hT_key, bufs):
        junk, ss, xn = bufs['junk'], bufs['ss'], bufs['xn']
        sj = junk.next()
        s1 = ss.next()
        self.act(junk.ap(sj), xt_ap, AF.Square, [xt_key], [junk.key(sj), ss.key(s1, 'a')],
                 accum=ss.ap(s1)[:, 0:1])
        self.act(ss.ap(s1)[:, 1:2], ss.ap(s1)[:, 0:1], AF.Sqrt, [ss.key(s1, 'a'), 'eps'], [ss.key(s1, 'b')],
                 scale=1.0 / D, bias=self.epsA[:, 0:1])
        self.recip(ss.ap(s1)[:, 2:3], ss.ap(s1)[:, 1:2], [ss.key(s1, 'b')], [ss.key(s1, 'c')])
        sx = xn.next()
        self.stt(xn.ap(sx), xt_ap, ss.ap(s1)[:, 2:3], gbc, ALU.mult, ALU.mult,
                 [xt_key, ss.key(s1, 'c'), gkey], [xn.key(sx)])
        b = self.nb()
        pb = self.bank(b).bitcast(BF16)
        for c in range(8):
            self.transpose(pb[:, c * 128:(c + 1) * 128], xn.ap(sx)[:, c * 128:(c + 1) * 128], self.ident,
                           [xn.key(sx), 'ident'], [self.pk(b)])
        self.copy('vector', hT_dst, pb.rearrange('p (c t) -> p c t', c=8), [self.pk(b)], [hT_key])

    def norm_bufs(self):
        a = self.arena
        return dict(junk=Buf(a, 128, D, BF16, 1), ss=Buf(a, 128, 4, F32, 4), xn=Buf(a, 128, D, BF16, 2))

    def load_w(self, src, kc, ncols, wst, wbf):
        s = wst.next()
        st = wst.ap(s)[:, 0:kc * ncols].rearrange('p (c n) -> p c n', c=kc)
        if kc == 1:
            self.dma(wst.ap(s)[:, 0:ncols], src, [], [wst.key(s)])
        else:
            self.dma(st, src.rearrange('(c p) n -> p c n', p=128), [], [wst.key(s)])
        t = wbf.next()
        wb = wbf.ap(t)[:, 0:kc * ncols].rearrange('p (c n) -> p c n', c=kc)
        self.copy('scalar', wbf.ap(t)[:, 0:kc * ncols], wst.ap(s)[:, 0:kc * ncols], [wst.key(s)], [wbf.key(t)])
        return wb, wbf.key(t)

    def load_w_res(self, src, kc, ncols, wst):
        dst = self.arena.alloc(128, kc * ncols, BF16)
        key = ('wres', self.arena.n)
        done = 0
        per = max(1, (wst.aps[0].shape[1]) // ncols)
        while done < kc:
            k = min(per, kc - done)
            s = wst.next()
            st = wst.ap(s)[:, 0:k * ncols].rearrange('p (c n) -> p c n', c=k)
            self.dma(st, src[done * 128:(done + k) * 128, :].rearrange('(c p) n -> p c n', p=128), [], [wst.key(s)])
            self.copy('scalar', dst[:, done * ncols:(done + k) * ncols], wst.ap(s)[:, 0:k * ncols],
                      [wst.key(s)], [key + (done,)])
            done += k
        keys = [key + (i,) for i in range(0, kc, per)]
        return dst.rearrange('p (c n) -> p c n', c=kc), keys

    def fm_norm(self, banks, nch, gcol, gkey, nfeat, t):
        raw, sq, sd, out = t['raw'], t['sq'], t['sd'], t['cn']
        rs, qs = [], []
        for c in range(nch):
            r = raw.next()
            q = sq.next()
            self.copy('scalar', raw.ap(r), self.bank(banks[c]), [self.pk(banks[c])], [raw.key(r)])
            self.act(sq.ap(q), self.bank(banks[c]), AF.Square, [self.pk(banks[c])], [sq.key(q)])
            rs.append(r)
            qs.append(q)
        b = self.nb()
        for c in range(nch):
            self.mm(self.bank(b), self.ones, sq.ap(qs[c]), c == 0, c == nch - 1,
                    ['ones', sq.key(qs[c])], [self.pk(b)])
        s = sd.next()
        self.act(sd.ap(s), self.bank(b), AF.Sqrt, [self.pk(b), 'eps'], [sd.key(s, 'a')],
                 scale=1.0 / nfeat, bias=self.epsA[:, 0:1])
        self.recip(sd.ap(s), sd.ap(s), [sd.key(s, 'a')], [sd.key(s, 'a')])
        outs = []
        for c in range(nch):
            o = out.next()
            self.stt(out.ap(o), raw.ap(rs[c]), gcol[:, c:c + 1], sd.ap(s), ALU.mult, ALU.mult,
                     [raw.key(rs[c]), sd.key(s, 'a'), gkey], [out.key(o)])
            outs.append(o)
        return outs

    def rope(self, bA, bB, rows, cosap, sinap, tkeys, t, outbuf):
        t1, t2 = t['r1'], t['r2']
        s1 = t1.next()
        s2 = t2.next()
        self.tt('vector', t1.ap(s1)[0:rows], self.bank(bB)[0:rows], sinap, ALU.mult,
                [self.pk(bB)] + tkeys, [t1.key(s1)])
        self.tt('vector', t2.ap(s2)[0:rows], self.bank(bA)[0:rows], cosap, ALU.mult,
                [self.pk(bA)] + tkeys, [t2.key(s2)])
        o = outbuf.next()
        self.tt('gpsimd', outbuf.ap(o)[0:rows], t1.ap(s1)[0:rows], t2.ap(s2)[0:rows], ALU.add,
                [t1.key(s1), t2.key(s2)], [outbuf.key(o)])
        return o

    def vtok(self, lhs_fn, nk, w, wkeys, col0, nh, val, valkey, dst, tile_idx, hd0, t, lkeys):
        vs = t['vst']
        b = self.nb()
        ncol = nh * 64
        for k in range(nk):
            self.mm(self.bank(b)[:, 0:ncol], lhs_fn(k), w[:, k, col0:col0 + ncol], k == 0, k == nk - 1,
                    lkeys + wkeys, [self.pk(b)])
        s = vs.next()
        st = vs.ap(s)[:, 0:nh * 66].rearrange('p (h d) -> p h d', h=nh)
        self.ts('vector', st[:, :, 0:64], self.bank(b)[:, 0:ncol].rearrange('p (h d) -> p h d', h=nh),
                val, None, ALU.mult, None, [self.pk(b), valkey], [vs.key(s, 'v')])
        self.ts('vector', st[:, :, 64:66], self.onescol[:, 0:nh * 2].rearrange('p (h d) -> p h d', h=nh),
                val, None, ALU.mult, None, ['onescol', valkey], [vs.key(s, 'o')])
        self.dma(dst[:, tile_idx, hd0:hd0 + nh, :], st, [vs.key(s, 'v'), vs.key(s, 'o')],
                 [('scr', id(dst), tile_idx, hd0)], q='gpsimd')

    def phase_A(self, L, own):
        a = self.arena
        d = self.dr
        m0 = a.mark()
        ubase = 0 if own else OWN
        cs = self.cs
        hT = a.alloc(128, 8 * OWN, BF16).rearrange('p (c t) -> p c t', c=8)
        gbc = a.alloc(128, D, F32)
        self.dma(gbc, d['g_mix%d' % L], [], ['gmix'])
        m1 = a.mark()
        nbufs = self.norm_bufs()
        xt = Buf(a, 128, D, F32, 3)
        for ti in range(32):
            s = xt.next()
            r0 = self.xrow(ubase + ti * 128)
            self.dma(xt.ap(s), self.xsrc[r0:r0 + 128, :], [], [xt.key(s)])
            self.norm_transpose(xt.ap(s), xt.key(s), gbc, 'gmix', hT[:, :, ti * 128:(ti + 1) * 128],
                                ('hT', ti // 4, ti % 4), nbufs)
        self.p.phase_barrier()
        a.release(m1)
        hkeys = lambda tb: []
        wst = Buf(a, 128, 8 * 256, F32, 4)
        wbf = Buf(a, 128, 8 * 256, BF16, 4)
        wrs = Buf(a, 128, 2048, F32, 2)
        t = dict(raw=Buf(a, 128, 512, F32, 3), sq=Buf(a, 128, 512, BF16, 3), sd=Buf(a, 128, 512, F32, 2),
                 cn=Buf(a, 128, 512, BF16, 4), r1=Buf(a, 128, 512, F32, 2), r2=Buf(a, 128, 512, F32, 2),
                 vst=Buf(a, 128, 8 * 66, BF16, 3))
        ob = Buf(a, 128, 512, BF16, 4)
        tab = Buf(a, 128, 2 * 512, F32, 3)
        gq = a.alloc(128, 2, F32)
        gkv = a.alloc(128, 1, F32)
        self.dma(gq, d['g_q%d' % L], [], ['gq'])
        self.dma(gkv, d['g_kv%d' % L], [], ['gkv'])
        w_in = d['w_in%d' % L]
        w_sw = d['w_sw%d' % L]
        blocks = list(range(8))

        def proj_fm(b, w, wk, c0, m, tb):
            for c in range(8):
                self.mm(self.bank(b)[0:m], w[:, c, c0:c0 + m], hT[:, c, tb * 512:(tb + 1) * 512],
                        c == 0, c == 7, [wk] + hkeys(tb), [self.pk(b)])

        def load_tab(cname, sname, rows, u0, n=512):
            s = tab.next()
            self.dma(tab.ap(s)[0:rows, 0:n], d[cname + cs][:, u0:u0 + n], [], [tab.key(s, 'c')])
            self.dma(tab.ap(s)[0:rows, 512:512 + n], d[sname + cs][:, u0:u0 + n], [], [tab.key(s, 's')])
            return tab.ap(s)[0:rows, 0:n], tab.ap(s)[0:rows, 512:512 + n], [tab.key(s, 'c'), tab.key(s, 's')]

        groups = []
        if own:
            wuq, kuq = self.load_w_res(d['w_uq%d' % L], 2, 768, wrs)
            wuqs, kuqs = self.load_w_res(d['w_uq_sw%d' % L], 2, 768, wrs)

            def g_cq(ws, hook):
                (w, wk), = ws
                for tb in blocks:
                    if tb == 4:
                        hook()
                    bs = []
                    for c in range(2):
                        b = self.nb()
                        proj_fm(b, w, wk, c * 128, 128, tb)
                        bs.append(b)
                    cn = self.fm_norm(bs, 2, gq, 'gq', 256, t)
                    cosap, sinap, tk = load_tab('CA', 'SA', 96, tb * 512)
                    for h in range(8):
                        bA = self.nb()
                        bB = self.nb()
                        for (bb, ww, kk) in ((bA, wuq, kuq), (bB, wuqs, kuqs)):
                            for c in range(2):
                                self.mm(self.bank(bb)[0:96], ww[:, c, h * 96:(h + 1) * 96], t['cn'].ap(cn[c]),
                                        c == 0, c == 1, kk + [t['cn'].key(cn[c])], [self.pk(bb)])
                        o = self.rope(bA, bB, 96, cosap, sinap, tk, t, ob)
                        self.dma(d['S_qA'][h * 96:(h + 1) * 96, tb * 512:(tb + 1) * 512], ob.ap(o)[0:96],
                                 [ob.key(o)], [('sqa', h, tb)], q='gpsimd')
            groups.append(([(w_in[:, 0:256], 8, 256)], g_cq))
        wuk, kuk = self.load_w_res(d['w_uk%d' % L], 1, 512, wrs)
        wuv, kuv = self.load_w_res(d['w_uv%d' % L], 1, 512, wrs)

        def g_ckv(ws, hook):
            (w, wk), (wsw, wswk) = ws
            for tb in blocks:
                if tb == 4:
                    hook()
                u0 = ubase + tb * 512
                b = self.nb()
                proj_fm(b, w, wk, 0, 128, tb)
                cn = self.fm_norm([b], 1, gkv, 'gkv', 128, t)
                ckvn = t['cn'].ap(cn[0])
                ckey = t['cn'].key(cn[0])
                for ch in range(4):
                    b2 = self.nb()
                    self.mm(self.bank(b2), wuk[:, 0, ch * 128:(ch + 1) * 128], ckvn, True, True, kuk + [ckey], [self.pk(b2)])
                    o = ob.next()
                    self.copy('scalar' if ch % 2 else 'vector', ob.ap(o), self.bank(b2), [self.pk(b2)], [ob.key(o)])
                    self.dma(d['S_kA'][ch * 128:(ch + 1) * 128, u0:u0 + 512], ob.ap(o), [ob.key(o)], [('ska', ch, u0)], q='gpsimd')
                for sub in range(4):
                    self.vtok(lambda k, sub=sub, ckvn=ckvn: ckvn[:, sub * 128:(sub + 1) * 128], 1, wuv, kuv, 0, 8,
                              self.valOne[:, 0:1], 'valOne', d['S_vA'], u0 // 128 + sub, 0, t, [ckey])
                bA = self.nb()
                bB = self.nb()
                proj_fm(bA, w, wk, 128, 32, tb)
                proj_fm(bB, wsw, wswk, 0, 32, tb)
                cosap, sinap, tk = load_tab('Ck', 'Sk', 32, u0)
                o = self.rope(bA, bB, 32, cosap, sinap, tk, t, ob)
                self.dma(d['S_kr'][0:32, u0:u0 + 512], ob.ap(o)[0:32], [ob.key(o)], [('skr', u0)], q='gpsimd')
        if not self.swap:
            groups.append(([(w_in[:, 256:416], 8, 160), (w_sw[:, 0:32], 8, 32)], g_ckv))

        def extB(tb):
            if own:
                return 1024 + tb * 512
            return {0: 5120, 1: 5632, 6: 0, 7: 512}[tb]

        def extC(tb):
            if own:
                return 512 + tb * 512
            return {0: 4608, 7: 0}[tb]
        own_ext = lambda tb: tb * 512
        bB_blocks = blocks if own else [0, 1, 6, 7]
        bC_blocks = blocks if own else [0, 7]

        def rope_group(col0, sw0, dst, row0, blks, extf):
            def run(ws, hook):
                (w, wk), (ws_, wsk) = ws
                for bi_, tb in enumerate(blks):
                    if bi_ == len(blks) // 2:
                        hook()
                    u0 = ubase + tb * 512
                    cosap, sinap, tk = load_tab('cosB', 'sinB', 128, u0)
                    for c in range(2):
                        bA = self.nb()
                        bBk = self.nb()
                        proj_fm(bA, w, wk, c * 128, 128, tb)
                        proj_fm(bBk, ws_, wsk, c * 128, 128, tb)
                        o = self.rope(bA, bBk, 128, cosap, sinap, tk, t, ob)
                        e0 = extf(tb)
                        self.dma(dst[row0 + c * 128: row0 + (c + 1) * 128, e0:e0 + 512], ob.ap(o),
                                 [ob.key(o)], [('sr', id(dst), row0 + c, e0)], q='gpsimd')
            groups.append(([(w_in[:, col0:col0 + 256], 8, 256), (w_sw[:, sw0:sw0 + 256], 8, 256)], run))

        def plain_group(col0, ncols, dst, row0, blks, extf, func):
            def run(ws, hook):
                (w, wk), = ws
                for bi_, tb in enumerate(blks):
                    if bi_ == len(blks) // 2:
                        hook()
                    for c in range(ncols // 128):
                        b = self.nb()
                        proj_fm(b, w, wk, c * 128, 128, tb)
                        o = ob.next()
                        if func is None and c % 2 == 0:
                            self.copy('vector', ob.ap(o), self.bank(b), [self.pk(b)], [ob.key(o)])
                        else:
                            self.act(ob.ap(o), self.bank(b), AF.Copy if func is None else func, [self.pk(b)], [ob.key(o)])
                        e0 = extf(tb)
                        self.dma(dst[row0 + c * 128: row0 + (c + 1) * 128, e0:e0 + 512], ob.ap(o),
                                 [ob.key(o)], [('sp', id(dst), row0 + c, e0)], q='gpsimd')
            groups.append(([(w_in[:, col0:col0 + ncols], 8, ncols)], run))

        def vtok_group(col0, nh, dst, hd0, blks, extf, val, valkey, per_tile_val):
            def run(ws, hook):
                (w, wk), = ws
                for bi_, tb in enumerate(blks):
                    if bi_ == len(blks) // 2:
                        hook()
                    for sub in range(4):
                        et = extf(tb) // 128 + sub
                        v = val[:, et:et + 1] if per_tile_val else val[:, 0:1]
                        self.vtok(lambda k, tb=tb, sub=sub: hT[:, k, tb * 512 + sub * 128: tb * 512 + (sub + 1) * 128],
                                  8, w, [wk], 0, nh, v, valkey, dst, et, hd0, t, hkeys(tb))
            groups.append(([(w_in[:, col0:col0 + nh * 64], 8, nh * 64)], run))

        QB, KB, VB = 416, 1184, 1952
        QC, KC, VC, G0 = 2720, 3232, 3744, 4256
        if own:
            for g in range(3):
                rope_group(QB + g * 256, 32 + g * 256, d['S_qb'], g * 256, blocks, own_ext)
        for g in range(3):
            rope_group(KB + g * 256, 32 + 768 + g * 256, d['S_kb'], g * 256, bB_blocks, extB)
        for g in range(3):
            vtok_group(VB + g * 256, 4, d['S_vb'], g * 4, bB_blocks, extB, self.valB, 'valB', True)
        if own:
            for g in range(2):
                plain_group(QC + g * 256, 256, d['S_qc'], g * 256, blocks, own_ext, None)
        for g in range(2):
            plain_group(KC + g * 256, 256, d['S_kc'], g * 256, bC_blocks, extC, None)
        for g in range(2):
            vtok_group(VC + g * 256, 4, d['S_vc'], g * 4, bC_blocks, extC, self.valOne, 'valOne', False)
        if own:
            for g in range(12):
                plain_group(G0 + g * 256, 256, d['S_gate'], g * 256, blocks, own_ext, AF.Sigmoid)

        def do_loads(g):
            return [self.load_w(src, kc, n, wst, wbf) for (src, kc, n) in g[0]]
        cur = do_loads(groups[0])
        for i, g in enumerate(groups):
            box = {}

            def hook(i=i, box=box):
                if i + 1 < len(groups):
                    box['n'] = do_loads(groups[i + 1])
            g[1](cur, hook)
            cur = box.get('n')
        a.release(m0)
        self.p.phase_barrier()

    def attn_heads(self, heads, nslot=4, nkmax=S):
        a = self.arena
        Qb = Buf(a, 128, OWN, BF16, nslot)
        Kb = Buf(a, 128, nkmax, BF16, nslot)
        dk0 = heads[0]['dk']
        if dk0 == 64:
            for s_ in range(nslot):
                self.memset('vector', Qb.ap(s_)[64:128, :], 0.0, [Qb.key(s_, 'z')])
                self.memset('gpsimd', Kb.ap(s_)[64:128, :], 0.0, [Kb.key(s_, 'z')])
            self.p.phase_barrier()
        dkp = 128 if dk0 == 64 else dk0
        Vb = Buf(a, 128, (nkmax // 128) * 66, BF16, nslot)
        Pb = Buf(a, 128, 512, BF16, 7)
        Osb = Buf(a, 65, 512, F32, 2)
        r32 = Buf(a, 65, 512, F32, 2)
        rhi = Buf(a, 65, 512, BF16, 2)
        rlo = Buf(a, 65, 512, BF16, 2)
        yb = Buf(a, 64, 512, BF16, 3)
        LA = 4
        sbank = [0, 1, 2, 6, 7]
        ucount = 0
        qcount = 0
        if heads[0].get('pre'):
            heads[0]['pre']()
        for hi, hd in enumerate(heads):
            dk, scale = hd['dk'], hd['scale']
            loaded = []
            for src in hd['srcs']:
                sq, sk, sv = Qb.next(), Kb.next(), Vb.next()
                scr, r0 = src['Q']
                self.dma(Qb.ap(sq)[0:dk, :], scr[r0:r0 + dk, :], [], [Qb.key(sq)])
                nkeys = src['nkeys']
                for (scr, r0, rows, dst0) in src['K']:
                    self.dma(Kb.ap(sk)[dst0:dst0 + rows, 0:nkeys], scr[r0:r0 + rows, 0:nkeys], [], [Kb.key(sk, dst0)])
                kkeys = [Kb.key(sk, x[3]) for x in src['K']]
                scr, hidx = src['V']
                nkt = nkeys // 128
                vv = Vb.ap(sv)[:, 0:nkt * 66].rearrange('p (k d) -> p k d', d=66)
                self.dma(vv, scr[:, 0:nkt, hidx, :], [], [Vb.key(sv)])
                loaded.append(dict(Q=Qb.ap(sq), Qk=Qb.key(sq), K=Kb.ap(sk), Kk=kkeys, V=vv, Vk=Vb.key(sv)))
            if hi + 1 < len(heads) and heads[hi + 1].get('pre'):
                heads[hi + 1]['pre']()
            units = []
            for qb in range(NB):
                ul = hd['units'](qb)
                for i, un in enumerate(ul):
                    si, kt, mask, mkey = un[:4]
                    c0, c1 = (un[4], un[5]) if len(un) > 4 else (0, 512)
                    units.append((qb, si, kt, mask, mkey, i == 0, i == len(ul) - 1, c0, c1))
            n = len(units)
            pend = []
            pslots = {}

            def fin_pe(qb, ob_, so, sr):
                bc = 5
                self.mm(self.bank(bc)[0:64], self.ones[64:65, 0:64], rhi.ap(sr)[64:65, :], True, False,
                        ['ones', rhi.key(sr)], [self.pk(bc)])
                self.mm(self.bank(bc)[0:64], self.ones[64:65, 0:64], rlo.ap(sr)[64:65, :], False, True,
                        ['ones', rlo.key(sr)], [self.pk(bc)])
                sy = yb.next()
                self.tt('vector', yb.ap(sy), Osb.ap(so)[0:64], self.bank(bc)[0:64], ALU.mult,
                        [Osb.key(so), self.pk(bc)], [yb.key(sy)])
                scr, row0 = hd['out']
                self.dma(scr[row0:row0 + 64, qb * 512:(qb + 1) * 512], yb.ap(sy), [yb.key(sy)],
                         [('y', id(scr), row0, qb)], q='gpsimd')

            for u in range(n + LA):
                if u < n:
                    qb, si, kt, mask, mkey, first, last, c0, c1 = units[u]
                    L_ = loaded[si]
                    nq = c1 - c0
                    sb = sbank[ucount % 5]
                    sp = Pb.next()
                    pslots[u] = (sb, sp)
                    ucount += 1
                    self.mm(self.bank(sb)[:, 0:nq], L_['K'][0:dkp, kt * 128:(kt + 1) * 128],
                            L_['Q'][0:dkp, qb * 512 + c0:qb * 512 + c1], True, True,
                            L_['Kk'] + [L_['Qk']], [self.pk(sb)])
                    self.act(Pb.ap(sp)[:, 0:nq], self.bank(sb)[:, 0:nq], AF.Exp, [self.pk(sb)], [Pb.key(sp)], scale=scale)
                    if mask is not None:
                        self.tt('vector', Pb.ap(sp)[:, 0:nq], Pb.ap(sp)[:, 0:nq], mask[:, c0:c1], ALU.mult,
                                [Pb.key(sp)] + list(mkey or []), [Pb.key(sp)])
                if u >= LA:
                    v = u - LA
                    qb, si, kt, mask, mkey, first, last, c0, c1 = units[v]
                    L_ = loaded[si]
                    sb, sp = pslots.pop(v)
                    ob_ = 3 + (qb % 2)
                    self.mm(self.bank(ob_)[0:65, c0:c1], L_['V'][:, kt, 0:65], Pb.ap(sp)[:, 0:c1 - c0], first, last,
                            [L_['Vk'], Pb.key(sp)], [self.pk(ob_)], skip=True)
                    if last:
                        so = Osb.next()
                        sr = r32.next()
                        self.copy('vector', Osb.ap(so), self.bank(ob_)[0:65], [self.pk(ob_)], [Osb.key(so)])
                        self.act(r32.ap(sr)[64:65], Osb.ap(so)[64:65], AF.Ln, [Osb.key(so)], [r32.key(sr)])
                        self.act(r32.ap(sr)[64:65], r32.ap(sr)[64:65], AF.Exp, [r32.key(sr)], [r32.key(sr)], scale=-1.0)
                        self.copy('gpsimd', rhi.ap(sr)[64:65], r32.ap(sr)[64:65], [r32.key(sr)], [rhi.key(sr)])
                        self.tt('gpsimd', rlo.ap(sr)[64:65], r32.ap(sr)[64:65], rhi.ap(sr)[64:65], ALU.subtract,
                                [r32.key(sr), rhi.key(sr)], [rlo.key(sr)])
                        pend.append((u, qb, ob_, so, sr))
                while pend and (u - pend[0][0] >= 3 or u == n + LA - 1):
                    _, qb, ob_, so, sr = pend.pop(0)
                    fin_pe(qb, ob_, so, sr)

    def phase_M(self, L):
        a = self.arena
        d = self.dr
        m0 = a.mark()
        heads = []
        for h in range(8):
            src = dict(Q=(d['S_qA'], h * 96), K=[(d['S_kA'], h * 64, 64, 0), (d['S_kr'], 0, 32, 64)],
                       V=(d['S_vA'], h), nkeys=S)
            heads.append(dict(dk=96, scale=96 ** -0.5, srcs=[src], out=(d['S_ya'], h * 64),
                              units=lambda qb: [(0, kt, None, None) for kt in range(64)]))
        self.attn_heads(heads)
        a.release(m0)
        self.p.phase_barrier()

    def load_masks(self, name, n):
        a = self.arena
        d = self.dr
        mk = a.alloc(128, n * 512, BF16).rearrange('p (n f) -> p n f', n=n)
        m = a.mark()
        st = Buf(a, 128, 4 * 512, F32, 2)
        i = 0
        while i < n:
            k = min(4, n - i)
            s = st.next()
            self.dma(st.ap(s)[:, 0:k * 512].rearrange('p (n f) -> p n f', n=k),
                     d[name][i:i + k].rearrange('n p f -> p n f'), [], [st.key(s)])
            self.copy('scalar', mk[:, i:i + k, :], st.ap(s)[:, 0:k * 512].rearrange('p (n f) -> p n f', n=k),
                      [st.key(s)], [(name, i)])
            i += k
        self.p.phase_barrier()
        a.release(m)
        return mk

    def phase_B(self, L):
        a = self.arena
        d = self.dr
        m0 = a.mark()
        mk = self.load_masks('Bmask', 34)
        rels = [list(range(-1, 5)), list(range(-2, 6)), list(range(-8, 12))]
        offs = [0, 6, 14]
        heads = []
        for j in range(4):
            srcs = []
            for g in range(3):
                hh = g * 4 + j
                srcs.append(dict(Q=(d['S_qb'], hh * 64), K=[(d['S_kb'], hh * 64, 64, 0)], V=(d['S_vb'], hh), nkeys=EXTB))

            def units(qb, rels=rels, offs=offs):
                ul = []
                for g in (2, 1, 0):
                    reach = 64 * (1, 4, 16)[g]
                    for ri, rel in enumerate(rels[g]):
                        c0 = max(0, 128 * rel - reach)
                        c1 = min(512, 128 * rel + 128 + reach)
                        ul.append((g, 8 + 4 * qb + rel, mk[:, offs[g] + ri, :], None, c0, c1))
                ul.sort(key=lambda x: 0 if (x[4] == 0 and x[5] == 512) else 1)
                return ul
            heads.append(dict(dk=64, scale=0.125, srcs=srcs, out=(d['S_yb'], j * 64), units=units))
        self.attn_heads(heads, nslot=4, nkmax=EXTB)
        a.release(m0)
        self.p.phase_barrier()

    def phase_C(self, L):
        a = self.arena
        d = self.dr
        m0 = a.mark()
        nav = self.load_masks('NAvalid' + self.cs, 24)
        NS = 24 * 64
        ebst = Buf(a, 128, 2 * NS, F32, 2)
        EF = Buf(a, 128, 2 * NS, BF16, 2)
        T = Buf(a, 128, 16 * 512, BF16, 2)
        heads = []
        for h in range(8):
            state = {}

            def pre(h=h, state=state):
                s = ebst.next()
                self.dma(ebst.ap(s)[:, 0:NS], d['ebias%d' % L][h], [], [ebst.key(s, 'f')])
                self.dma(ebst.ap(s)[:, NS:2 * NS], d['ebiasz%d' % L][h], [], [ebst.key(s, 'z')])
                se = EF.next()
                self.act(EF.ap(se), ebst.ap(s), AF.Exp, [ebst.key(s, 'f'), ebst.key(s, 'z')], [EF.key(se)])
                ef = EF.ap(se)[:, 0:NS].rearrange('p (s c) -> p s c', c=64)
                st_ = T.next()
                tt_ = T.ap(st_).rearrange('p (n f) -> p n f', n=16)
                for e_, ty in enumerate((0, 2)):
                    for j in range(8):
                        w0 = 15 - 2 * j
                        self.tt('gpsimd', tt_[:, e_ * 8 + j, :].rearrange('p (r c) -> p r c', c=64),
                                ef[:, w0:w0 + 8, :],
                                nav[:, ty * 8 + j, :].rearrange('p (r c) -> p r c', c=64),
                                ALU.mult, [EF.key(se)], [T.key(st_, (e_, j))])
                state['T'] = tt_
                state['k'] = st_
                state['EZ'] = EF.ap(se)[:, NS:2 * NS]
                state['ek'] = EF.key(se)

            def units(qb, state=state):
                ul = []
                for j in range(8):
                    w0 = 15 - 2 * j
                    if qb == 0 or qb == 7:
                        e_ = 0 if qb == 0 else 1
                        ul.append((0, 4 * qb + 2 + j, state['T'][:, e_ * 8 + j, :], [T.key(state['k'], (e_, j))]))
                    else:
                        r0 = max(0, 2 * j - 7)
                        r1 = min(7, 2 * j + 1)
                        ul.append((0, 4 * qb + 2 + j, state['EZ'][:, w0 * 64:(w0 + 8) * 64], [state['ek']],
                                   r0 * 64, (r1 + 1) * 64))
                ul.sort(key=lambda x: 0 if (len(x) < 5 or (x[4] == 0 and x[5] == 512)) else 1)
                return ul
            src = dict(Q=(d['S_qc'], h * 64), K=[(d['S_kc'], h * 64, 64, 0)], V=(d['S_vc'], h), nkeys=EXTC)
            heads.append(dict(dk=64, scale=0.125, srcs=[src], out=(d['S_yc'], h * 64), units=units, pre=pre,
                              tkeys=state))
        self.attn_heads(heads, nslot=3, nkmax=EXTC)
        a.release(m0)
        self.p.phase_barrier()

    def phase_G(self, L):
        a = self.arena
        d = self.dr
        self.h2T = a.alloc(128, 8 * OWN, BF16).rearrange('p (c t) -> p c t', c=8)
        self.mG = a.mark()
        wst = Buf(a, 128, 2048, F32, 2)
        wpa, kpa = self.load_w_res(d['w_pa%d' % L], 4, 1024, wst)
        wpb, kpb = self.load_w_res(d['w_pb%d' % L], 2, 1024, wst)
        wpc, kpc = self.load_w_res(d['w_pc%d' % L], 4, 1024, wst)
        wo, ko = self.load_w_res(d['w_o%d' % L], 8, 1024, wst)
        gbc = a.alloc(128, D, F32)
        self.dma(gbc, d['g_ffn%d' % L], [], ['gffn'])
        ya = Buf(a, 128, 4 * 512, BF16, 2)
        yb = Buf(a, 128, 2 * 512, BF16, 2)
        yc = Buf(a, 128, 4 * 512, BF16, 2)
        gt = Buf(a, 128, 3 * 512, BF16, 3)
        mg = Buf(a, 128, 8 * 512, BF16, 2)
        tf = Buf(a, 128, 512, F32, 5)
        xt = Buf(a, 128, D, F32, 3)
        nbufs = self.norm_bufs()
        def stage1(tb):
            t0 = tb * 512
            sa, sb_, sc = ya.next(), yb.next(), yc.next()
            self.dma(ya.ap(sa).rearrange('p (c t) -> p c t', c=4),
                     d['S_ya'][:, t0:t0 + 512].rearrange('(c p) t -> p c t', p=128), [], [ya.key(sa)])
            self.dma(yb.ap(sb_).rearrange('p (c t) -> p c t', c=2),
                     d['S_yb'][:, t0:t0 + 512].rearrange('(c p) t -> p c t', p=128), [], [yb.key(sb_)])
            self.dma(yc.ap(sc).rearrange('p (c t) -> p c t', c=4),
                     d['S_yc'][:, t0:t0 + 512].rearrange('(c p) t -> p c t', p=128), [], [yc.key(sc)])
            yA = ya.ap(sa).rearrange('p (c t) -> p c t', c=4)
            yB = yb.ap(sb_).rearrange('p (c t) -> p c t', c=2)
            yC = yc.ap(sc).rearrange('p (c t) -> p c t', c=4)
            sm = mg.next()
            mgv = mg.ap(sm).rearrange('p (c t) -> p c t', c=8)
            for oc in range(8):
                sg = gt.next()
                gv = gt.ap(sg).rearrange('p (c t) -> p c t', c=3)
                self.dma(gv, d['S_gate'].rearrange('(b c p) t -> c p b t', b=3, p=128)[oc][:, :, t0:t0 + 512],
                         [], [gt.key(sg)])
                ts_ = []
                for (y, yk, w, wk, nk, bi) in ((yA, ya.key(sa), wpa, kpa, 4, 0), (yB, yb.key(sb_), wpb, kpb, 2, 1),
                                               (yC, yc.key(sc), wpc, kpc, 4, 2)):
                    b = self.nb()
                    for c in range(nk):
                        self.mm(self.bank(b), w[:, c, oc * 128:(oc + 1) * 128], y[:, c, :], c == 0, c == nk - 1,
                                wk + [yk], [self.pk(b)])
                    s = tf.next()
                    self.tt('vector', tf.ap(s), self.bank(b), gv[:, bi, :], ALU.mult, [self.pk(b), gt.key(sg)], [tf.key(s)])
                    ts_.append(s)
                s4 = tf.next()
                self.tt('gpsimd', tf.ap(s4), tf.ap(ts_[0]), tf.ap(ts_[1]), ALU.add, [tf.key(ts_[0]), tf.key(ts_[1])], [tf.key(s4)])
                self.tt('gpsimd', mgv[:, oc, :], tf.ap(s4), tf.ap(ts_[2]), ALU.add, [tf.key(s4), tf.key(ts_[2])],
                        [mg.key(sm, oc)])
            mkeys = [mg.key(sm, oc) for oc in range(8)]
            return mgv, mkeys

        def stage2(tb, mgv, mkeys):
            t0 = tb * 512
            for sub in range(4):
                sx = xt.next()
                r0 = t0 + sub * 128
                rx = self.xrow(r0)
                self.dma(xt.ap(sx), self.xsrc[rx:rx + 128, :], [], [xt.key(sx)])
                for half in range(2):
                    b = self.nb()
                    for c in range(8):
                        self.mm(self.bank(b), mgv[:, c, sub * 128:(sub + 1) * 128], wo[:, c, half * 512:(half + 1) * 512],
                                c == 0, c == 7, mkeys + ko, [self.pk(b)])
                    self.tt('vector', xt.ap(sx)[:, half * 512:(half + 1) * 512], xt.ap(sx)[:, half * 512:(half + 1) * 512],
                            self.bank(b), ALU.add, [xt.key(sx), self.pk(b)], [xt.key(sx)])
                self.dma(d['S_x1'][r0:r0 + 128, :], xt.ap(sx), [xt.key(sx)], [('x1', r0)], q='gpsimd')
                ti = tb * 4 + sub
                self.norm_transpose(xt.ap(sx), xt.key(sx), gbc, 'gffn', self.h2T[:, :, ti * 128:(ti + 1) * 128],
                                    ('h2T', tb, sub), nbufs)
        pend = {0: stage1(0)}
        for tb in range(NB):
            if tb + 1 < NB:
                pend[tb + 1] = stage1(tb + 1)
            stage2(tb, *pend.pop(tb))
        a.release(self.mG)
        self.p.phase_barrier()

    def phase_F(self, L, final):
        a = self.arena
        d = self.dr
        h2T = self.h2T
        m1 = a.mark()
        wst = Buf(a, 128, 8 * 256, F32, 4)
        wbf = Buf(a, 128, 8 * 256, BF16, 4)
        sgb = Buf(a, 128, 512, F32, 3)
        ob = Buf(a, 128, 512, BF16, 4)
        cols = list(range(0, DFF, 256))

        def f_loads(col):
            return (self.load_w(d['w1%d' % L][:, col:col + 256], 8, 256, wst, wbf),
                    self.load_w(d['w3%d' % L][:, col:col + 256], 8, 256, wst, wbf))
        cur = f_loads(cols[0])
        for gi, col in enumerate(cols):
            nxt = f_loads(cols[gi + 1]) if gi + 1 < len(cols) else None
            (w1, k1), (w3, k3) = cur
            for tb in range(NB):
                for ch in range(2):
                    b1 = self.nb()
                    b3 = self.nb()
                    for (bb, ww, kk) in ((b1, w1, k1), (b3, w3, k3)):
                        for c in range(8):
                            self.mm(self.bank(bb), ww[:, c, ch * 128:(ch + 1) * 128], h2T[:, c, tb * 512:(tb + 1) * 512],
                                    c == 0, c == 7, [kk], [self.pk(bb)])
                    s = sgb.next()
                    self.act(sgb.ap(s), self.bank(b1), AF.Silu, [self.pk(b1)], [sgb.key(s)])
                    o = ob.next()
                    self.tt('vector', ob.ap(o), sgb.ap(s), self.bank(b3), ALU.mult, [sgb.key(s), self.pk(b3)], [ob.key(o)])
                    f = col // 128 + ch
                    self.dma(d['S_act'][f * 128:(f + 1) * 128, tb * 512:(tb + 1) * 512], ob.ap(o), [ob.key(o)],
                             [('sact', f, tb)], q='gpsimd')
            cur = nxt
        a.release(m1)
        self.p.phase_barrier()
        a.release(self.mG)
        a.off = 0 + self._const_end
        wst = Buf(a, 128, 4096, F32, 2)
        w2, k2 = self.load_w_res(d['w2%d' % L], 22, 1024, wst)
        gfin = a.alloc(128, D, F32)
        self.dma(gfin, d['g_final'], [], ['gfin'])
        at = Buf(a, 128, 22 * 512, BF16, 2)
        xt = Buf(a, 128, D, F32, 3)
        yn = Buf(a, 128, D, F32, 2)
        junk = Buf(a, 128, D, BF16, 1)
        ss = Buf(a, 128, 4, F32, 4)
        for tb in range(NB):
            t0 = tb * 512
            sa = at.next()
            av = at.ap(sa).rearrange('p (f t) -> p f t', f=22)
            self.dma(av, d['S_act'][:, t0:t0 + 512].rearrange('(f p) t -> p f t', p=128), [], [at.key(sa)])
            for sub in range(4):
                r0 = t0 + sub * 128
                sx = xt.next()
                self.dma(xt.ap(sx), d['S_x1'][r0:r0 + 128, :], [], [xt.key(sx)])
                for half in range(2):
                    b = self.nb()
                    for f in range(22):
                        self.mm(self.bank(b), av[:, f, sub * 128:(sub + 1) * 128], w2[:, f, half * 512:(half + 1) * 512],
                                f == 0, f == 21, [at.key(sa)] + k2, [self.pk(b)])
                    self.tt('vector', xt.ap(sx)[:, half * 512:(half + 1) * 512], xt.ap(sx)[:, half * 512:(half + 1) * 512],
                            self.bank(b), ALU.add, [xt.key(sx), self.pk(b)], [xt.key(sx)])
                self.dma(self.out_raw[r0:r0 + 128, :], xt.ap(sx), [xt.key(sx)], [('yraw', r0)], q='gpsimd')
                if final:
                    sj = junk.next()
                    s1 = ss.next()
                    self.act(junk.ap(sj), xt.ap(sx), AF.Square, [xt.key(sx)], [junk.key(sj), ss.key(s1, 'a')],
                             accum=ss.ap(s1)[:, 0:1])
                    self.act(ss.ap(s1)[:, 1:2], ss.ap(s1)[:, 0:1], AF.Sqrt, [ss.key(s1, 'a'), 'eps'], [ss.key(s1, 'b')],
                             scale=1.0 / D, bias=self.epsA[:, 0:1])
                    self.recip(ss.ap(s1)[:, 2:3], ss.ap(s1)[:, 1:2], [ss.key(s1, 'b')], [ss.key(s1, 'c')])
                    sy = yn.next()
                    self.stt(yn.ap(sy), xt.ap(sx), ss.ap(s1)[:, 2:3], gfin, ALU.mult, ALU.mult,
                             [xt.key(sx), ss.key(s1, 'c'), 'gfin'], [yn.key(sy)])
                    self.dma(d['y_norm'][r0:r0 + 128, :], yn.ap(sy), [yn.key(sy)], [('ynorm', r0)], q='gpsimd')
        a.off = self._const_end
        self.p.phase_barrier()


W_SHAPES = dict(w_in=(1024, IN_COLS), w_sw=(1024, 1568), w_uq=(256, 768), w_uq_sw=(256, 768), w_uk=(128, 512),
                w_uv=(128, 512), g_mix=(128, D), g_q=(128, 2), g_kv=(128, 1), ebias=(8, 128, 24 * 64), ebiasz=(8, 128, 24 * 64),
                w_pa=(512, D), w_pb=(256, D), w_pc=(512, D), w_o=(D, D), g_ffn=(128, D), w1=(D, DFF), w3=(D, DFF),
                w2=(DFF, D))
C_SHAPES = dict(ident=(128, 128), valB=(128, 48), CA=(96, OWN), SA=(96, OWN), Ck=(32, S), Sk=(32, S),
                cosB=(128, S), sinB=(128, S), Bmask=(34, 128, 512), NAvalid=(24, 128, 512), g_final=(128, D))
SCR = dict(S_qA=((768, OWN), BF16), S_kA=((512, S), BF16), S_kr=((32, S), BF16), S_vA=((128, 64, 8, 66), BF16),
           S_qb=((768, OWN), BF16), S_kb=((768, EXTB), BF16), S_vb=((128, 48, 12, 66), BF16),
           S_qc=((512, OWN), BF16), S_kc=((512, EXTC), BF16), S_vc=((128, 40, 8, 66), BF16),
           S_gate=((3072, OWN), BF16), S_ya=((512, OWN), BF16), S_yb=((256, OWN), BF16), S_yc=((512, OWN), BF16),
           S_x1=((OWN, D), F32), S_act=((DFF, OWN), BF16))


PC_NAMES = ('valB', 'CA', 'SA', 'Ck', 'Sk', 'cosB', 'sinB', 'NAvalid')


def build_nc(fused=True, dbg=None, upto=99):
    nc = bass.Bass("TRN2", target_bir_lowering=False)
    B = Layer(nc, dbg)
    B.dram_in('xs', (S, D))
    sets = ('', '_p') if fused else ('',)
    for k, shp in C_SHAPES.items():
        if k in PC_NAMES:
            for cs in sets:
                B.dram_in(k + cs, shp)
        else:
            B.dram_in(k, shp)
    for l in range(2 if fused else 1):
        for k, shp in W_SHAPES.items():
            B.dram_in(k + str(l), shp)
    B.dram_out('y_raw', (OWN, D))
    B.dram_out('y_norm', (OWN, D))
    for k, (shp, dt) in SCR.items():
        B.dram_scr(k, shp, dt)
    B.setup_consts()
    B._const_end = B.arena.off
    B.p.phase_barrier()

    def run_pass(L, cs, xsrc, swap, out_raw, final):
        B.begin_pass(L, cs, xsrc, swap, out_raw, final)
        phs = [lambda: B.phase_A(L, True), lambda: B.phase_A(L, False), lambda: B.phase_M(L), lambda: B.phase_B(L),
               lambda: B.phase_C(L), lambda: B.phase_G(L), lambda: B.phase_F(L, final)]
        for ph in phs[:upto]:
            ph()
    if fused:
        X1 = B.dram_scr('X1', (S, D), F32)
        run_pass(0, '', B.dr['xs'], False, X1[0:OWN], False)
        run_pass(0, '_p', B.dr['xs'], True, X1[OWN:S], False)
        run_pass(1, '', X1, False, B.dr['y_raw'], True)
    else:
        run_pass(0, '', B.dr['xs'], False, B.dr['y_raw'], True)
    B.p.emit()
    return nc, B


def _rope_tabs(pos, dim):
    inv = np.power(np.float32(10000.0), -np.arange(0, dim, 2, dtype=np.float32) / np.float32(dim)).astype(np.float32)
    ang = pos.astype(np.float32)[:, None] * inv[None, :]
    return np.cos(ang).astype(np.float32), np.sin(ang).astype(np.float32)


def core_consts(h):
    perm = _perm_local(h)
    c = {}
    c['ident'] = np.eye(128, dtype=np.float32)
    e = np.arange(EXTB)
    t = h * OWN - 1024 + e
    valid = ((t >= 0) & (t < S)).astype(np.float32)
    c['valB'] = np.ascontiguousarray(valid.reshape(48, 128).T)
    cb, sb = _rope_tabs(perm, 64)
    f = np.arange(128)
    sign = np.where((f % 64) < 32, -1.0, 1.0).astype(np.float32)
    c['cosB'] = np.ascontiguousarray(cb[:, f % 32].T)
    c['sinB'] = np.ascontiguousarray((sb[:, f % 32] * sign[None, :]).T)
    ca, sa = _rope_tabs(perm, 32)
    j = np.arange(32)
    sgn = np.where(j < 16, -1.0, 1.0).astype(np.float32)
    c['Ck'] = np.ascontiguousarray(ca[:, j % 16].T)
    c['Sk'] = np.ascontiguousarray((sa[:, j % 16] * sgn[None, :]).T)
    CA = np.ones((96, OWN), np.float32)
    SA = np.zeros((96, OWN), np.float32)
    CA[64:96] = c['Ck'][:, :OWN]
    SA[64:96] = c['Sk'][:, :OWN]
    c['CA'], c['SA'] = CA, SA
    masks = []
    ii = np.arange(128)[:, None]
    jj = np.arange(512)[None, :]
    for dil, rels in ((1, range(-1, 5)), (4, range(-2, 6)), (16, range(-8, 12))):
        for rel in rels:
            dlt = 128 * rel + ii - jj
            masks.append(((dlt % dil == 0) & (np.abs(dlt) <= 64 * dil)).astype(np.float32))
    c['Bmask'] = np.stack(masks)
    nav = np.zeros((24, 128, 512), np.float32)
    for ty, i in enumerate((0, 1, 7)):
        for jx in range(8):
            for a_ in range(2):
                for rho in range(8):
                    Rr = 64 * h + 8 * i + rho
                    KR = 64 * h + 8 * i - 4 + 2 * jx + a_
                    rs = min(max(Rr - 4, 0), 120)
                    ok = (0 <= KR < 128) and (rs <= KR < rs + 8)
                    if ok:
                        nav[ty * 8 + jx, a_ * 64:(a_ + 1) * 64, rho * 64:(rho + 1) * 64] = 1.0
    c['NAvalid'] = nav
    return c


def layer_weights(inp, l):
    w = {}
    w_in = np.asarray(inp['w_in'][l], np.float32)
    w['w_in'] = w_in

    def swap_heads(cols, nh, hd):
        half = hd // 2
        parts = []
        for hh in range(nh):
            b = hh * hd
            parts += [cols[:, b + half:b + hd], cols[:, b:b + half]]
        return np.concatenate(parts, axis=1)
    w['w_sw'] = np.ascontiguousarray(np.concatenate([swap_heads(w_in[:, 384:416], 1, 32), swap_heads(w_in[:, 416:1184], 12, 64),
                                                     swap_heads(w_in[:, 1184:1952], 12, 64)], axis=1))
    wuq = np.asarray(inp['w_uq'][l], np.float32)
    w['w_uq'] = wuq
    sw = wuq.copy()
    for hh in range(8):
        b = hh * 96 + 64
        sw[:, b:b + 16] = wuq[:, b + 16:b + 32]
        sw[:, b + 16:b + 32] = wuq[:, b:b + 16]
    w['w_uq_sw'] = sw
    wukv = np.asarray(inp['w_ukv'][l], np.float32).reshape(128, 8, 128)
    w['w_uk'] = np.ascontiguousarray(wukv[:, :, :64].reshape(128, 512))
    w['w_uv'] = np.ascontiguousarray(wukv[:, :, 64:].reshape(128, 512))
    w['g_mix'] = np.ascontiguousarray(np.broadcast_to(np.asarray(inp['g_mix'][l], np.float32)[None, :], (128, D)))
    w['g_ffn'] = np.ascontiguousarray(np.broadcast_to(np.asarray(inp['g_ffn'][l], np.float32)[None, :], (128, D)))
    w['g_q'] = np.ascontiguousarray(np.asarray(inp['g_q'][l], np.float32).reshape(2, 128).T)
    w['g_kv'] = np.ascontiguousarray(np.asarray(inp['g_kv'][l], np.float32).reshape(128, 1))
    rpb = np.asarray(inp['rpb'][l], np.float32)
    kc = np.arange(64)[:, None]
    cc = np.arange(64)[None, :]
    cs = np.clip(cc - 8, 0, 48)
    colok = (kc >= cs) & (kc < cs + 16)
    dc = np.clip(kc - cc + 15, 0, 30)
    NEG = np.float32(-30000.0)
    ebf = np.full((8, 64, 23, 64), NEG, np.float32)
    ebz = np.full((8, 64, 23, 64), NEG, np.float32)
    for slot in range(23):
        dr = 18 - slot
        if 0 <= dr <= 14:
            vals = np.where(colok[None], rpb[:, dr][:, dc], NEG)
            ebf[:, :, slot, :] = vals
            if 3 <= dr <= 10:
                ebz[:, :, slot, :] = vals

    def shifted(e):
        pad = np.full((8, 64, 1, 64), NEG, np.float32)
        lo = np.concatenate([e, pad], axis=2)
        hi = np.concatenate([pad, e], axis=2)
        return np.ascontiguousarray(np.concatenate([lo, hi], axis=1).reshape(8, 128, 24 * 64))
    w['ebias'] = shifted(ebf)
    w['ebiasz'] = shifted(ebz)
    for k in ('w_pa', 'w_pb', 'w_pc', 'w_o', 'w1', 'w3', 'w2'):
        w[k] = np.ascontiguousarray(np.asarray(inp[k][l], np.float32))
    return w


_NC_CACHE = {}


def kernel(**inputs):
    x = np.asarray(inputs['x'], np.float32)
    gfin_bc = np.ascontiguousarray(np.broadcast_to(np.asarray(inputs['g_final'], np.float32)[None, :], (128, D)))
    consts = [core_consts(0), core_consts(1)]
    if 'f' not in _NC_CACHE:
        _NC_CACHE['f'] = build_nc(True)
    nc, B = _NC_CACHE['f']
    ws = [layer_weights(inputs, l) for l in range(2)]
    in_maps = []
    for c in range(8):
        b, h = c // 2, c % 2
        m = {'xs': np.ascontiguousarray(x[b][_perm_local(h)]), 'g_final': gfin_bc}
        for k, v in consts[h].items():
            m[k] = v
        for k in PC_NAMES:
            m[k + '_p'] = consts[1 - h][k]
        for l in range(2):
            for k, v in ws[l].items():
                m[k + str(l)] = v
        in_maps.append(m)
    res = run_bass_kernel_spmd(nc, in_maps, core_ids=list(range(8)))
    out = np.empty_like(x)
    for c in range(8):
        b, h = c // 2, c % 2
        out[b, h * OWN:(h + 1) * OWN] = res.results[c]['y_norm']
    return out
```

```python
import numpy as np
import concourse.bass as bass
import concourse.mybir as mybir
from concourse.bass_utils import run_bass_kernel_spmd

F32 = mybir.dt.float32
BF16 = mybir.dt.bfloat16
U8 = mybir.dt.uint8
AF = mybir.ActivationFunctionType
ALU = mybir.AluOpType

D = 1024
S = 8192
OWN = 4096
NB = 8
EPS = 1e-6
IN_COLS = 7328
DFF = 2816
EXTB = 6144
EXTC = 5120
ISZ = {F32: 4, BF16: 2, U8: 1}
ENGS = ['tensor', 'vector', 'scalar', 'gpsimd', 'sync']
CH = 16000
DMAK = 12


class Prog:
    def __init__(self, nc):
        self.nc = nc
        self.ops = []
        self.lastw = {}
        self.readers = {}
        self.barrier = {e: None for e in ENGS}

    def add(self, eng, fn, reads=(), writes=(), dma=False):
        i = len(self.ops)
        deps = set()
        for r in reads:
            j = self.lastw.get(r)
            if j is not None:
                deps.add(j)
        for w in writes:
            j = self.lastw.get(w)
            if j is not None:
                deps.add(j)
            deps.update(self.readers.get(w, ()))
        if self.barrier[eng] is not None:
            deps.update(self.barrier[eng])
            self.barrier[eng] = None
        for r in reads:
            self.readers.setdefault(r, []).append(i)
        for w in writes:
            self.lastw[w] = i
            self.readers[w] = []
        self.ops.append(dict(eng=eng, fn=fn, deps=deps, dma=dma))
        return i

    def phase_barrier(self):
        last = set()
        seen_c = set()
        seen_d = {}
        for i in range(len(self.ops) - 1, -1, -1):
            o = self.ops[i]
            if o['dma']:
                c = seen_d.get(o['eng'], 0)
                if c < DMAK:
                    last.add(i)
                    seen_d[o['eng']] = c + 1
            elif o['eng'] not in seen_c:
                seen_c.add(o['eng'])
                last.add(i)
            if len(seen_c) >= 4 and all(seen_d.get(e, 0) >= DMAK for e in ('sync', 'gpsimd')):
                break
        for e in ENGS:
            self.barrier[e] = set(last)
        self.lastw = {}
        self.readers = {}

    def emit(self):
        nc = self.nc
        ops = self.ops
        n = len(ops)
        needed = [False] * n
        for o in ops:
            for j in o['deps']:
                oj = ops[j]
                if oj['eng'] == 'tensor' and o['eng'] == 'tensor' and not oj['dma'] and not o['dma']:
                    continue
                needed[j] = True
        sig = [None] * n
        ccount = {e: 0 for e in ENGS}
        dcount = {e: 0 for e in ENGS}
        csems = {e: [] for e in ENGS}
        dsems = {e: [] for e in ENGS}
        dprev = [None] * n
        for i, o in enumerate(ops):
            e = o['eng']
            if o['dma']:
                k = dcount[e]
                dcount[e] += 1
                slot = k % DMAK
                if slot >= len(dsems[e]):
                    dsems[e].append(nc.alloc_semaphore('d_%s_%d' % (e, slot)))
                val = 16 * (k // DMAK + 1)
                sig[i] = (dsems[e][slot], val)
                if val > 16:
                    dprev[i] = (dsems[e][slot], val - 16)
            elif needed[i]:
                k = ccount[e]
                ccount[e] += 1
                si = k // CH
                if si >= len(csems[e]):
                    csems[e].append(nc.alloc_semaphore('c_%s_%d' % (e, si)))
                sig[i] = (csems[e][si], k % CH + 1)
        per = {e: [] for e in ENGS}
        for i, o in enumerate(ops):
            per[o['eng']].append(i)
        finals = []
        for e in ENGS:
            k = dcount[e]
            for slot in range(min(k, DMAK)):
                cnt = (k - 1 - slot) // DMAK + 1
                finals.append((dsems[e][slot], 16 * cnt))

        def run(eng_name, e):
            waited = {}

            def wait(sem, val):
                key = sem.num
                if waited.get(key, 0) < val:
                    e.wait_ge(sem, val)
                    waited[key] = val
            for i in per[eng_name]:
                o = ops[i]
                for j in sorted(o['deps']):
                    if sig[j] is None:
                        continue
                    oj = ops[j]
                    if oj['eng'] == 'tensor' and eng_name == 'tensor' and not oj['dma'] and not o['dma']:
                        continue
                    wait(*sig[j])
                if dprev[i] is not None:
                    wait(*dprev[i])
                ins = o['fn'](e)
                if sig[i] is not None:
                    ins.then_inc(sig[i][0], 16 if o['dma'] else 1)
            if eng_name == 'sync':
                for sem, val in finals:
                    wait(sem, val)

        with nc.Block() as block:
            @block.tensor
            def _(e):
                run('tensor', e)

            @block.vector
            def _(e):
                run('vector', e)

            @block.scalar
            def _(e):
                run('scalar', e)

            @block.gpsimd
            def _(e):
                run('gpsimd', e)

            @block.sync
            def _(e):
                run('sync', e)


class Arena:
    def __init__(self, nc, nbytes):
        self.t = nc.alloc_sbuf_tensor('arena', [128, nbytes], U8)
        self.cap = nbytes
        self.off = 0
        self.n = 0

    def alloc(self, parts, elems, dt, p0=0):
        size = elems * ISZ[dt]
        size = (size + 63) // 64 * 64
        assert self.off + size <= self.cap, ('SBUF overflow', self.off, size)
        ap = self.t[p0:p0 + parts, self.off:self.off + elems * ISZ[dt]].bitcast(dt)
        self.off += size
        self.n += 1
        return ap

    def mark(self):
        return self.off

    def release(self, m):
        self.off = m


class Buf:
    _cnt = [0]

    def __init__(self, arena, parts, elems, dt, nslot=1, p0=0):
        Buf._cnt[0] += 1
        self.id = Buf._cnt[0]
        self.aps = [arena.alloc(parts, elems, dt, p0) for _ in range(nslot)]
        self.nslot = nslot
        self.i = -1

    def next(self):
        self.i += 1
        return self.i % self.nslot

    def ap(self, s):
        return self.aps[s]

    def key(self, s, sub=None):
        return ('b', self.id, s, sub)


class Builder:
    def __init__(self, nc, dbg=None):
        self.nc = nc
        self.p = Prog(nc)
        self.arena = Arena(nc, 206 * 1024)
        self.ps = nc.alloc_psum_tensor('ps', [128, 4096], F32)
        self.psi = -1
        self.dr = {}
        self.dbg = dbg
        self.did = 0

    def dram_in(self, name, shape, dt=F32):
        t = self.nc.dram_tensor(name, list(shape), dt, kind="ExternalInput").ap()
        self.dr[name] = t
        return t

    def dram_out(self, name, shape, dt=F32):
        t = self.nc.dram_tensor(name, list(shape), dt, kind="ExternalOutput").ap()
        self.dr[name] = t
        return t

    def dram_scr(self, name, shape, dt=BF16):
        kind = "ExternalOutput" if (self.dbg and name in self.dbg) else "Internal"
        t = self.nc.dram_tensor(name, list(shape), dt, kind=kind).ap()
        self.dr[name] = t
        return t

    def bank(self, b):
        return self.ps[:, b * 512:(b + 1) * 512]

    def pk(self, b):
        return ('ps', b)

    def dma(self, out, in_, reads, writes, q='sync'):
        self.p.add(q, lambda e, o=out, i=in_: e.dma_start(out=o, in_=i), reads, writes, dma=True)

    def mm(self, out, lhsT, rhs, start, stop, reads, writes, skip=False):
        def f(e, o=out, l=lhsT, r=rhs, s=start, t=stop, k=skip):
            if k:
                return e.matmul(o, l, r, start=s, stop=t, skip_group_check=True)
            return e.matmul(o, l, r, start=s, stop=t)
        self.p.add('tensor', f, reads, writes)

    def act(self, out, in_, func, reads, writes, scale=None, bias=None, accum=None):
        def f(e, o=out, i=in_, fn=func, sc=scale, bi=bias, ac=accum):
            kw = {}
            if sc is not None:
                kw['scale'] = sc
            if bi is not None:
                kw['bias'] = bi
            if ac is not None:
                kw['accum_out'] = ac
            return e.activation(o, i, fn, **kw)
        self.p.add('scalar', f, reads, writes)

    def tt(self, eng, out, in0, in1, op, reads, writes):
        self.p.add(eng, lambda e, o=out, a=in0, b=in1, p=op: e.tensor_tensor(o, a, b, p), reads, writes)

    def ts(self, eng, out, in0, s1, s2, op0, op1, reads, writes):
        def f(e, o=out, a=in0, x=s1, y=s2, p=op0, q=op1):
            if q is None:
                return e.tensor_scalar(o, a, x, None, p)
            return e.tensor_scalar(o, a, x, y, p, q)
        self.p.add(eng, f, reads, writes)

    def stt(self, out, in0, scalar, in1, op0, op1, reads, writes):
        self.p.add('vector', lambda e, o=out, a=in0, s=scalar, b=in1, p=op0, q=op1:
                   e.scalar_tensor_tensor(o, a, s, b, p, q), reads, writes)

    def copy(self, eng, out, in_, reads, writes):
        if eng == 'scalar':
            self.act(out, in_, AF.Copy, reads, writes)
        else:
            self.p.add(eng, lambda e, o=out, i=in_: e.tensor_copy(o, i), reads, writes)

    def memset(self, eng, ap, val, writes):
        self.p.add(eng, lambda e, a=ap, v=val: e.memset(a, v), (), writes)

    def recip(self, out, in_, reads, writes):
        self.p.add('vector', lambda e, o=out, i=in_: e.reciprocal(o, i), reads, writes)

    def transpose(self, out, in_, ident, reads, writes):
        self.p.add('tensor', lambda e, o=out, i=in_, d=ident: e.transpose(o, i, d), reads, writes)


def _perm_local(h):
    own = np.arange(h * OWN, (h + 1) * OWN)
    par = np.arange((1 - h) * OWN, (2 - h) * OWN)
    return np.concatenate([own, par])


class Layer(Builder):
    def nb(self):
        self.psi = (self.psi + 1) % 8
        return self.psi

    def setup_consts(self):
        a = self.arena
        d = self.dr
        self.ident = a.alloc(128, 128, BF16)
        self.ones = a.alloc(128, 128, BF16)
        self.epsA = a.alloc(128, 1, F32)
        stage = a.alloc(128, 128, F32)
        self.dma(stage, d['ident'], [], ['c_stage'])
        self.copy('vector', self.ident, stage, ['c_stage'], ['ident'])
        self.memset('vector', self.ones, 1.0, ['ones'])
        self.memset('vector', self.epsA, EPS, ['eps'])
        self.onescol = a.alloc(128, 12 * 2, BF16)
        self.memset('vector', self.onescol, 1.0, ['onescol'])
        self.valB = a.alloc(128, 48, F32)
        self.valOne = a.alloc(128, 64, F32)
        self.memset('vector', self.valOne, 1.0, ['valOne'])

    def begin_pass(self, L, cs, xsrc, swap, out_raw, final):
        self.L, self.cs, self.xsrc, self.swap, self.out_raw, self.final = L, cs, xsrc, swap, out_raw, final
        self.dma(self.valB, self.dr['valB' + cs], [], ['valB'])
        self.p.phase_barrier()

    def xrow(self, u):
        return (u + OWN) % S if self.swap else u

    def norm_transpose(self, xt_ap, xt_key, gbc, gkey, hT_dst, hT_key, bufs):
        junk, ss, xn = bufs['junk'], bufs['ss'], bufs['xn']
        sj = junk.next()
        s1 = ss.next()
        self.act(junk.ap(sj), xt_ap, AF.Square, [xt_key], [junk.key(sj), ss.key(s1, 'a')],
                 accum=ss.ap(s1)[:, 0:1])
        self.act(ss.ap(s1)[:, 1:2], ss.ap(s1)[:, 0:1], AF.Sqrt, [ss.key(s1, 'a'), 'eps'], [ss.key(s1, 'b')],
                 scale=1.0 / D, bias=self.epsA[:, 0:1])
        self.recip(ss.ap(s1)[:, 2:3], ss.ap(s1)[:, 1:2], [ss.key(s1, 'b')], [ss.key(s1, 'c')])
        sx = xn.next()
        self.stt(xn.ap(sx), xt_ap, ss.ap(s1)[:, 2:3], gbc, ALU.mult, ALU.mult,
                 [xt_key, ss.key(s1, 'c'), gkey], [xn.key(sx)])
        b = self.nb()
        pb = self.bank(b).bitcast(BF16)
        for c in range(8):
            self.transpose(pb[:, c * 128:(c + 1) * 128], xn.ap(sx)[:, c * 128:(c + 1) * 128], self.ident,
                           [xn.key(sx), 'ident'], [self.pk(b)])
        self.copy('vector', hT_dst, pb.rearrange('p (c t) -> p c t', c=8), [self.pk(b)], [hT_key])

    def norm_bufs(self):
        a = self.arena
        return dict(junk=Buf(a, 128, D, BF16, 1), ss=Buf(a, 128, 4, F32, 4), xn=Buf(a, 128, D, BF16, 2))

    def load_w(self, src, kc, ncols, wst, wbf):
        s = wst.next()
        st = wst.ap(s)[:, 0:kc * ncols].rearrange('p (c n) -> p c n', c=kc)
        if kc == 1:
            self.dma(wst.ap(s)[:, 0:ncols], src, [], [wst.key(s)])
        else:
            self.dma(st, src.rearrange('(c p) n -> p c n', p=128), [], [wst.key(s)])
        t = wbf.next()
        wb = wbf.ap(t)[:, 0:kc * ncols].rearrange('p (c n) -> p c n', c=kc)
        self.copy('scalar', wbf.ap(t)[:, 0:kc * ncols], wst.ap(s)[:, 0:kc * ncols], [wst.key(s)], [wbf.key(t)])
        return wb, wbf.key(t)

    def load_w_res(self, src, kc, ncols, wst):
        dst = self.arena.alloc(128, kc * ncols, BF16)
        key = ('wres', self.arena.n)
        done = 0
        per = max(1, (wst.aps[0].shape[1]) // ncols)
        while done < kc:
            k = min(per, kc - done)
            s = wst.next()
            st = wst.ap(s)[:, 0:k * ncols].rearrange('p (c n) -> p c n', c=k)
            self.dma(st, src[done * 128:(done + k) * 128, :].rearrange('(c p) n -> p c n', p=128), [], [wst.key(s)])
            self.copy('scalar', dst[:, done * ncols:(done + k) * ncols], wst.ap(s)[:, 0:k * ncols],
                      [wst.key(s)], [key + (done,)])
            done += k
        keys = [key + (i,) for i in range(0, kc, per)]
        return dst.rearrange('p (c n) -> p c n', c=kc), keys

    def fm_norm(self, banks, nch, gcol, gkey, nfeat, t):
        raw, sq, sd, out = t['raw'], t['sq'], t['sd'], t['cn']
        rs, qs = [], []
        for c in range(nch):
            r = raw.next()
            q = sq.next()
            self.copy('scalar', raw.ap(r), self.bank(banks[c]), [self.pk(banks[c])], [raw.key(r)])
            self.act(sq.ap(q), self.bank(banks[c]), AF.Square, [self.pk(banks[c])], [sq.key(q)])
            rs.append(r)
            qs.append(q)
        b = self.nb()
        for c in range(nch):
            self.mm(self.bank(b), self.ones, sq.ap(qs[c]), c == 0, c == nch - 1,
                    ['ones', sq.key(qs[c])], [self.pk(b)])
        s = sd.next()
        self.act(sd.ap(s), self.bank(b), AF.Sqrt, [self.pk(b), 'eps'], [sd.key(s, 'a')],
                 scale=1.0 / nfeat, bias=self.epsA[:, 0:1])
        self.recip(sd.ap(s), sd.ap(s), [sd.key(s, 'a')], [sd.key(s, 'a')])
        outs = []
        for c in range(nch):
            o = out.next()
            self.stt(out.ap(o), raw.ap(rs[c]), gcol[:, c:c + 1], sd.ap(s), ALU.mult, ALU.mult,
                     [raw.key(rs[c]), sd.key(s, 'a'), gkey], [out.key(o)])
            outs.append(o)
        return outs

    def rope(self, bA, bB, rows, cosap, sinap, tkeys, t, outbuf):
        t1, t2 = t['r1'], t['r2']
        s1 = t1.next()
        s2 = t2.next()
        self.tt('vector', t1.ap(s1)[0:rows], self.bank(bB)[0:rows], sinap, ALU.mult,
                [self.pk(bB)] + tkeys, [t1.key(s1)])
        self.tt('vector', t2.ap(s2)[0:rows], self.bank(bA)[0:rows], cosap, ALU.mult,
                [self.pk(bA)] + tkeys, [t2.key(s2)])
        o = outbuf.next()
        self.tt('gpsimd', outbuf.ap(o)[0:rows], t1.ap(s1)[0:rows], t2.ap(s2)[0:rows], ALU.add,
                [t1.key(s1), t2.key(s2)], [outbuf.key(o)])
        return o

    def vtok(self, lhs_fn, nk, w, wkeys, col0, nh, val, valkey, dst, tile_idx, hd0, t, lkeys):
        vs = t['vst']
        b = self.nb()
        ncol = nh * 64
        for k in range(nk):
            self.mm(self.bank(b)[:, 0:ncol], lhs_fn(k), w[:, k, col0:col0 + ncol], k == 0, k == nk - 1,
                    lkeys + wkeys, [self.pk(b)])
        s = vs.next()
        st = vs.ap(s)[:, 0:nh * 66].rearrange('p (h d) -> p h d', h=nh)
        self.ts('vector', st[:, :, 0:64], self.bank(b)[:, 0:ncol].rearrange('p (h d) -> p h d', h=nh),
                val, None, ALU.mult, None, [self.pk(b), valkey], [vs.key(s, 'v')])
        self.ts('vector', st[:, :, 64:66], self.onescol[:, 0:nh * 2].rearrange('p (h d) -> p h d', h=nh),
                val, None, ALU.mult, None, ['onescol', valkey], [vs.key(s, 'o')])
        self.dma(dst[:, tile_idx, hd0:hd0 + nh, :], st, [vs.key(s, 'v'), vs.key(s, 'o')],
                 [('scr', id(dst), tile_idx, hd0)], q='gpsimd')

    def phase_A(self, L, own):
        a = self.arena
        d = self.dr
        m0 = a.mark()
        ubase = 0 if own else OWN
        cs = self.cs
        hT = a.alloc(128, 8 * OWN, BF16).rearrange('p (c t) -> p c t', c=8)
        gbc = a.alloc(128, D, F32)
        self.dma(gbc, d['g_mix%d' % L], [], ['gmix'])
        m1 = a.mark()
        nbufs = self.norm_bufs()
        xt = Buf(a, 128, D, F32, 3)
        for ti in range(32):
            s = xt.next()
            r0 = self.xrow(ubase + ti * 128)
            self.dma(xt.ap(s), self.xsrc[r0:r0 + 128, :], [], [xt.key(s)])
            self.norm_transpose(xt.ap(s), xt.key(s), gbc, 'gmix', hT[:, :, ti * 128:(ti + 1) * 128],
                                ('hT', ti // 4, ti % 4), nbufs)
        self.p.phase_barrier()
        a.release(m1)
        hkeys = lambda tb: []
        wst = Buf(a, 128, 8 * 256, F32, 4)
        wbf = Buf(a, 128, 8 * 256, BF16, 4)
        wrs = Buf(a, 128, 2048, F32, 2)
        t = dict(raw=Buf(a, 128, 512, F32, 3), sq=Buf(a, 128, 512, BF16, 3), sd=Buf(a, 128, 512, F32, 2),
                 cn=Buf(a, 128, 512, BF16, 4), r1=Buf(a, 128, 512, F32, 2), r2=Buf(a, 128, 512, F32, 2),
                 vst=Buf(a, 128, 8 * 66, BF16, 3))
        ob = Buf(a, 128, 512, BF16, 4)
        tab = Buf(a, 128, 2 * 512, F32, 3)
        gq = a.alloc(128, 2, F32)
        gkv = a.alloc(128, 1, F32)
        self.dma(gq, d['g_q%d' % L], [], ['gq'])
        self.dma(gkv, d['g_kv%d' % L], [], ['gkv'])
        w_in = d['w_in%d' % L]
        w_sw = d['w_sw%d' % L]
        blocks = list(range(8))

        def proj_fm(b, w, wk, c0, m, tb):
            for c in range(8):
                self.mm(self.bank(b)[0:m], w[:, c, c0:c0 + m], hT[:, c, tb * 512:(tb + 1) * 512],
                        c == 0, c == 7, [wk] + hkeys(tb), [self.pk(b)])

        def load_tab(cname, sname, rows, u0, n=512):
            s = tab.next()
            self.dma(tab.ap(s)[0:rows, 0:n], d[cname + cs][:, u0:u0 + n], [], [tab.key(s, 'c')])
            self.dma(tab.ap(s)[0:rows, 512:512 + n], d[sname + cs][:, u0:u0 + n], [], [tab.key(s, 's')])
            return tab.ap(s)[0:rows, 0:n], tab.ap(s)[0:rows, 512:512 + n], [tab.key(s, 'c'), tab.key(s, 's')]

        groups = []
        if own:
            wuq, kuq = self.load_w_res(d['w_uq%d' % L], 2, 768, wrs)
            wuqs, kuqs = self.load_w_res(d['w_uq_sw%d' % L], 2, 768, wrs)

            def g_cq(ws, hook):
                (w, wk), = ws
                def s1(tb):
                    bs = []
                    for c in range(2):
                        b = self.nb()
                        proj_fm(b, w, wk, c * 128, 128, tb)
                        bs.append(b)
                    return self.fm_norm(bs, 2, gq, 'gq', 256, t)
                cns = {0: s1(0)}
                for tb in blocks:
                    if tb == 4:
                        hook()
                    if tb + 1 < 8:
                        cns[tb + 1] = s1(tb + 1)
                    cn = cns.pop(tb)
                    cosap, sinap, tk = load_tab('CA', 'SA', 96, tb * 512)
                    for h in range(8):
                        bA = self.nb()
                        bB = self.nb()
                        for (bb, ww, kk) in ((bA, wuq, kuq), (bB, wuqs, kuqs)):
                            for c in range(2):
                                self.mm(self.bank(bb)[0:96], ww[:, c, h * 96:(h + 1) * 96], t['cn'].ap(cn[c]),
                                        c == 0, c == 1, kk + [t['cn'].key(cn[c])], [self.pk(bb)])
                        o = self.rope(bA, bB, 96, cosap, sinap, tk, t, ob)
                        self.dma(d['S_qA'][h * 96:(h + 1) * 96, tb * 512:(tb + 1) * 512], ob.ap(o)[0:96],
                                 [ob.key(o)], [('sqa', h, tb)], q='gpsimd')
            groups.append(([(w_in[:, 0:256], 8, 256)], g_cq))
        wuk, kuk = self.load_w_res(d['w_uk%d' % L], 1, 512, wrs)
        wuv, kuv = self.load_w_res(d['w_uv%d' % L], 1, 512, wrs)

        def g_ckv(ws, hook):
            (w, wk), (wsw, wswk) = ws
            def s1(tb):
                b = self.nb()
                proj_fm(b, w, wk, 0, 128, tb)
                return self.fm_norm([b], 1, gkv, 'gkv', 128, t)
            cns = {0: s1(0)}
            for tb in blocks:
                if tb == 4:
                    hook()
                if tb + 1 < 8:
                    cns[tb + 1] = s1(tb + 1)
                u0 = ubase + tb * 512
                cn = cns.pop(tb)
                ckvn = t['cn'].ap(cn[0])
                ckey = t['cn'].key(cn[0])
                for ch in range(4):
                    b2 = self.nb()
                    self.mm(self.bank(b2), wuk[:, 0, ch * 128:(ch + 1) * 128], ckvn, True, True, kuk + [ckey], [self.pk(b2)])
                    o = ob.next()
                    self.copy('scalar' if ch % 2 else 'vector', ob.ap(o), self.bank(b2), [self.pk(b2)], [ob.key(o)])
                    self.dma(d['S_kA'][ch * 128:(ch + 1) * 128, u0:u0 + 512], ob.ap(o), [ob.key(o)], [('ska', ch, u0)], q='gpsimd')
                for sub in range(4):
                    self.vtok(lambda k, sub=sub, ckvn=ckvn: ckvn[:, sub * 128:(sub + 1) * 128], 1, wuv, kuv, 0, 8,
                              self.valOne[:, 0:1], 'valOne', d['S_vA'], u0 // 128 + sub, 0, t, [ckey])
                bA = self.nb()
                bB = self.nb()
                proj_fm(bA, w, wk, 128, 32, tb)
                proj_fm(bB, wsw, wswk, 0, 32, tb)
                cosap, sinap, tk = load_tab('Ck', 'Sk', 32, u0)
                o = self.rope(bA, bB, 32, cosap, sinap, tk, t, ob)
                self.dma(d['S_kr'][0:32, u0:u0 + 512], ob.ap(o)[0:32], [ob.key(o)], [('skr', u0)], q='gpsimd')
        if not self.swap:
            groups.append(([(w_in[:, 256:416], 8, 160), (w_sw[:, 0:32], 8, 32)], g_ckv))

        def extB(tb):
            if own:
                return 1024 + tb * 512
            return {0: 5120, 1: 5632, 6: 0, 7: 512}[tb]

        def extC(tb):
            if own:
                return 512 + tb * 512
            return {0: 4608, 7: 0}[tb]
        own_ext = lambda tb: tb * 512
        bB_blocks = blocks if own else [0, 1, 6, 7]
        bC_blocks = blocks if own else [0, 7]

        def rope_group(col0, sw0, dst, row0, blks, extf):
            def run(ws, hook):
                (w, wk), (ws_, wsk) = ws
                for bi_, tb in enumerate(blks):
                    if bi_ == len(blks) // 2:
                        hook()
                    u0 = ubase + tb * 512
                    cosap, sinap, tk = load_tab('cosB', 'sinB', 128, u0)
                    for c in range(2):
                        bA = self.nb()
                        bBk = self.nb()
                        proj_fm(bA, w, wk, c * 128, 128, tb)
                        proj_fm(bBk, ws_, wsk, c * 128, 128, tb)
                        o = self.rope(bA, bBk, 128, cosap, sinap, tk, t, ob)
                        e0 = extf(tb)
                        self.dma(dst[row0 + c * 128: row0 + (c + 1) * 128, e0:e0 + 512], ob.ap(o),
                                 [ob.key(o)], [('sr', id(dst), row0 + c, e0)], q='gpsimd')
            groups.append(([(w_in[:, col0:col0 + 256], 8, 256), (w_sw[:, sw0:sw0 + 256], 8, 256)], run))

        def plain_group(col0, ncols, dst, row0, blks, extf, func):
            def run(ws, hook):
                (w, wk), = ws
                for bi_, tb in enumerate(blks):
                    if bi_ == len(blks) // 2:
                        hook()
                    for c in range(ncols // 128):
                        b = self.nb()
                        proj_fm(b, w, wk, c * 128, 128, tb)
                        o = ob.next()
                        if func is None and c % 2 == 0:
                            self.copy('vector', ob.ap(o), self.bank(b), [self.pk(b)], [ob.key(o)])
                        else:
                            self.act(ob.ap(o), self.bank(b), AF.Copy if func is None else func, [self.pk(b)], [ob.key(o)])
                        e0 = extf(tb)
                        self.dma(dst[row0 + c * 128: row0 + (c + 1) * 128, e0:e0 + 512], ob.ap(o),
                                 [ob.key(o)], [('sp', id(dst), row0 + c, e0)], q='gpsimd')
            groups.append(([(w_in[:, col0:col0 + ncols], 8, ncols)], run))

        def vtok_group(col0, nh, dst, hd0, blks, extf, val, valkey, per_tile_val):
            def run(ws, hook):
                (w, wk), = ws
                for bi_, tb in enumerate(blks):
                    if bi_ == len(blks) // 2:
                        hook()
                    for sub in range(4):
                        et = extf(tb) // 128 + sub
                        v = val[:, et:et + 1] if per_tile_val else val[:, 0:1]
                        self.vtok(lambda k, tb=tb, sub=sub: hT[:, k, tb * 512 + sub * 128: tb * 512 + (sub + 1) * 128],
                                  8, w, [wk], 0, nh, v, valkey, dst, et, hd0, t, hkeys(tb))
            groups.append(([(w_in[:, col0:col0 + nh * 64], 8, nh * 64)], run))

        QB, KB, VB = 416, 1184, 1952
        QC, KC, VC, G0 = 2720, 3232, 3744, 4256
        if own:
            for g in range(3):
                rope_group(QB + g * 256, 32 + g * 256, d['S_qb'], g * 256, blocks, own_ext)
        for g in range(3):
            rope_group(KB + g * 256, 32 + 768 + g * 256, d['S_kb'], g * 256, bB_blocks, extB)
        for g in range(3):
            vtok_group(VB + g * 256, 4, d['S_vb'], g * 4, bB_blocks, extB, self.valB, 'valB', True)
        if own:
            for g in range(2):
                plain_group(QC + g * 256, 256, d['S_qc'], g * 256, blocks, own_ext, None)
        for g in range(2):
            plain_group(KC + g * 256, 256, d['S_kc'], g * 256, bC_blocks, extC, None)
        for g in range(2):
            vtok_group(VC + g * 256, 4, d['S_vc'], g * 4, bC_blocks, extC, self.valOne, 'valOne', False)
        if own:
            for g in range(12):
                plain_group(G0 + g * 256, 256, d['S_gate'], g * 256, blocks, own_ext, AF.Sigmoid)

        def do_loads(g):
            return [self.load_w(src, kc, n, wst, wbf) for (src, kc, n) in g[0]]
        cur = do_loads(groups[0])
        for i, g in enumerate(groups):
            box = {}

            def hook(i=i, box=box):
                if i + 1 < len(groups):
                    box['n'] = do_loads(groups[i + 1])
            g[1](cur, hook)
            cur = box.get('n')
        a.release(m0)
        self.p.phase_barrier()

    def attn_heads(self, heads, nslot=4, nkmax=S):
        a = self.arena
        Qb = Buf(a, 128, OWN, BF16, nslot)
        Kb = Buf(a, 128, nkmax, BF16, nslot)
        dk0 = heads[0]['dk']
        if dk0 == 64:
            for s_ in range(nslot):
                self.memset('vector', Qb.ap(s_)[64:128, :], 0.0, [Qb.key(s_, 'z')])
                self.memset('gpsimd', Kb.ap(s_)[64:128, :], 0.0, [Kb.key(s_, 'z')])
            self.p.phase_barrier()
        dkp = 128 if dk0 == 64 else dk0
        Vb = Buf(a, 128, (nkmax // 128) * 66, BF16, nslot)
        Pb = Buf(a, 128, 512, BF16, 7)
        Osb = Buf(a, 65, 512, F32, 2)
        r32 = Buf(a, 65, 512, F32, 2)
        rhi = Buf(a, 65, 512, BF16, 2)
        rlo = Buf(a, 65, 512, BF16, 2)
        yb = Buf(a, 64, 512, BF16, 3)
        LA = 4
        sbank = [0, 1, 2, 6, 7]
        ucount = 0
        qcount = 0
        if heads[0].get('pre'):
            heads[0]['pre']()
        for hi, hd in enumerate(heads):
            dk, scale = hd['dk'], hd['scale']
            loaded = []
            for src in hd['srcs']:
                sq, sk, sv = Qb.next(), Kb.next(), Vb.next()
                scr, r0 = src['Q']
                self.dma(Qb.ap(sq)[0:dk, :], scr[r0:r0 + dk, :], [], [Qb.key(sq)])
                nkeys = src['nkeys']
                for (scr, r0, rows, dst0) in src['K']:
                    self.dma(Kb.ap(sk)[dst0:dst0 + rows, 0:nkeys], scr[r0:r0 + rows, 0:nkeys], [], [Kb.key(sk, dst0)])
                kkeys = [Kb.key(sk, x[3]) for x in src['K']]
                scr, hidx = src['V']
                nkt = nkeys // 128
                vv = Vb.ap(sv)[:, 0:nkt * 66].rearrange('p (k d) -> p k d', d=66)
                self.dma(vv, scr[:, 0:nkt, hidx, :], [], [Vb.key(sv)])
                loaded.append(dict(Q=Qb.ap(sq), Qk=Qb.key(sq), K=Kb.ap(sk), Kk=kkeys, V=vv, Vk=Vb.key(sv)))
            if hi + 1 < len(heads) and heads[hi + 1].get('pre'):
                heads[hi + 1]['pre']()
            units = []
            for qb in range(NB):
                ul = hd['units'](qb)
                for i, un in enumerate(ul):
                    si, kt, mask, mkey = un[:4]
                    c0, c1 = (un[4], un[5]) if len(un) > 4 else (0, 512)
                    units.append((qb, si, kt, mask, mkey, i == 0, i == len(ul) - 1, c0, c1))
            n = len(units)
            pend = []
            pslots = {}

            def fin_pe(qb, ob_, so, sr):
                bc = 5
                self.mm(self.bank(bc)[0:64], self.ones[64:65, 0:64], rhi.ap(sr)[64:65, :], True, False,
                        ['ones', rhi.key(sr)], [self.pk(bc)])
                self.mm(self.bank(bc)[0:64], self.ones[64:65, 0:64], rlo.ap(sr)[64:65, :], False, True,
                        ['ones', rlo.key(sr)], [self.pk(bc)])
                sy = yb.next()
                self.tt('vector', yb.ap(sy), Osb.ap(so)[0:64], self.bank(bc)[0:64], ALU.mult,
                        [Osb.key(so), self.pk(bc)], [yb.key(sy)])
                scr, row0 = hd['out']
                self.dma(scr[row0:row0 + 64, qb * 512:(qb + 1) * 512], yb.ap(sy), [yb.key(sy)],
                         [('y', id(scr), row0, qb)], q='gpsimd')

            for u in range(n + LA):
                if u < n:
                    qb, si, kt, mask, mkey, first, last, c0, c1 = units[u]
                    L_ = loaded[si]
                    nq = c1 - c0
                    sb = sbank[ucount % 5]
                    sp = Pb.next()
                    pslots[u] = (sb, sp)
                    ucount += 1
                    self.mm(self.bank(sb)[:, 0:nq], L_['K'][0:dkp, kt * 128:(kt + 1) * 128],
                            L_['Q'][0:dkp, qb * 512 + c0:qb * 512 + c1], True, True,
                            L_['Kk'] + [L_['Qk']], [self.pk(sb)])
                    self.act(Pb.ap(sp)[:, 0:nq], self.bank(sb)[:, 0:nq], AF.Exp, [self.pk(sb)], [Pb.key(sp)], scale=scale)
                    if mask is not None:
                        self.tt('vector', Pb.ap(sp)[:, 0:nq], Pb.ap(sp)[:, 0:nq], mask[:, c0:c1], ALU.mult,
                                [Pb.key(sp)] + list(mkey or []), [Pb.key(sp)])
                if u >= LA:
                    v = u - LA
                    qb, si, kt, mask, mkey, first, last, c0, c1 = units[v]
                    L_ = loaded[si]
                    sb, sp = pslots.pop(v)
                    ob_ = 3 + (qb % 2)
                    self.mm(self.bank(ob_)[0:65, c0:c1], L_['V'][:, kt, 0:65], Pb.ap(sp)[:, 0:c1 - c0], first, last,
                            [L_['Vk'], Pb.key(sp)], [self.pk(ob_)], skip=True)
                    if last:
                        so = Osb.next()
                        sr = r32.next()
                        self.copy('vector', Osb.ap(so), self.bank(ob_)[0:65], [self.pk(ob_)], [Osb.key(so)])
                        self.act(r32.ap(sr)[64:65], Osb.ap(so)[64:65], AF.Ln, [Osb.key(so)], [r32.key(sr)])
                        self.act(r32.ap(sr)[64:65], r32.ap(sr)[64:65], AF.Exp, [r32.key(sr)], [r32.key(sr)], scale=-1.0)
                        self.copy('gpsimd', rhi.ap(sr)[64:65], r32.ap(sr)[64:65], [r32.key(sr)], [rhi.key(sr)])
                        self.tt('gpsimd', rlo.ap(sr)[64:65], r32.ap(sr)[64:65], rhi.ap(sr)[64:65], ALU.subtract,
                                [r32.key(sr), rhi.key(sr)], [rlo.key(sr)])
                        pend.append((u, qb, ob_, so, sr))
                while pend and (u - pend[0][0] >= 3 or u == n + LA - 1):
                    _, qb, ob_, so, sr = pend.pop(0)
                    fin_pe(qb, ob_, so, sr)

    def phase_M(self, L):
        a = self.arena
        d = self.dr
        m0 = a.mark()
        heads = []
        for h in range(8):
            src = dict(Q=(d['S_qA'], h * 96), K=[(d['S_kA'], h * 64, 64, 0), (d['S_kr'], 0, 32, 64)],
                       V=(d['S_vA'], h), nkeys=S)
            heads.append(dict(dk=96, scale=96 ** -0.5, srcs=[src], out=(d['S_ya'], h * 64),
                              units=lambda qb: [(0, kt, None, None) for kt in range(64)]))
        self.attn_heads(heads)
        a.release(m0)
        self.p.phase_barrier()

    def load_masks(self, name, n):
        a = self.arena
        d = self.dr
        mk = a.alloc(128, n * 512, BF16).rearrange('p (n f) -> p n f', n=n)
        m = a.mark()
        st = Buf(a, 128, 4 * 512, F32, 2)
        i = 0
        while i < n:
            k = min(4, n - i)
            s = st.next()
            self.dma(st.ap(s)[:, 0:k * 512].rearrange('p (n f) -> p n f', n=k),
                     d[name][i:i + k].rearrange('n p f -> p n f'), [], [st.key(s)])
            self.copy('scalar', mk[:, i:i + k, :], st.ap(s)[:, 0:k * 512].rearrange('p (n f) -> p n f', n=k),
                      [st.key(s)], [(name, i)])
            i += k
        self.p.phase_barrier()
        a.release(m)
        return mk

    def phase_B(self, L):
        a = self.arena
        d = self.dr
        m0 = a.mark()
        mk = self.load_masks('Bmask', 34)
        rels = [list(range(-1, 5)), list(range(-2, 6)), list(range(-8, 12))]
        offs = [0, 6, 14]
        heads = []
        for j in range(4):
            srcs = []
            for g in range(3):
                hh = g * 4 + j
                srcs.append(dict(Q=(d['S_qb'], hh * 64), K=[(d['S_kb'], hh * 64, 64, 0)], V=(d['S_vb'], hh), nkeys=EXTB))

            def units(qb, rels=rels, offs=offs):
                ul = []
                for g in (2, 1, 0):
                    reach = 64 * (1, 4, 16)[g]
                    for ri, rel in enumerate(rels[g]):
                        c0 = max(0, 128 * rel - reach)
                        c1 = min(512, 128 * rel + 128 + reach)
                        ul.append((g, 8 + 4 * qb + rel, mk[:, offs[g] + ri, :], None, c0, c1))
                ul.sort(key=lambda x: 0 if (x[4] == 0 and x[5] == 512) else 1)
                return ul
            heads.append(dict(dk=64, scale=0.125, srcs=srcs, out=(d['S_yb'], j * 64), units=units))
        self.attn_heads(heads, nslot=4, nkmax=EXTB)
        a.release(m0)
        self.p.phase_barrier()

    def phase_C(self, L):
        a = self.arena
        d = self.dr
        m0 = a.mark()
        nav = self.load_masks('NAvalid' + self.cs, 24)
        NS = 24 * 64
        ebst = Buf(a, 128, 2 * NS, F32, 2)
        EF = Buf(a, 128, 2 * NS, BF16, 2)
        T = Buf(a, 128, 16 * 512, BF16, 2)
        heads = []
        for h in range(8):
            state = {}

            def pre(h=h, state=state):
                s = ebst.next()
                self.dma(ebst.ap(s)[:, 0:NS], d['ebias%d' % L][h], [], [ebst.key(s, 'f')])
                self.dma(ebst.ap(s)[:, NS:2 * NS], d['ebiasz%d' % L][h], [], [ebst.key(s, 'z')])
                se = EF.next()
                self.act(EF.ap(se), ebst.ap(s), AF.Exp, [ebst.key(s, 'f'), ebst.key(s, 'z')], [EF.key(se)])
                ef = EF.ap(se)[:, 0:NS].rearrange('p (s c) -> p s c', c=64)
                st_ = T.next()
                tt_ = T.ap(st_).rearrange('p (n f) -> p n f', n=16)
                for e_, ty in enumerate((0, 2)):
                    for j in range(8):
                        w0 = 15 - 2 * j
                        self.tt('gpsimd', tt_[:, e_ * 8 + j, :].rearrange('p (r c) -> p r c', c=64),
                                ef[:, w0:w0 + 8, :],
                                nav[:, ty * 8 + j, :].rearrange('p (r c) -> p r c', c=64),
                                ALU.mult, [EF.key(se)], [T.key(st_, (e_, j))])
                state['T'] = tt_
                state['k'] = st_
                state['EZ'] = EF.ap(se)[:, NS:2 * NS]
                state['ek'] = EF.key(se)

            def units(qb, state=state):
                ul = []
                for j in range(8):
                    w0 = 15 - 2 * j
                    if qb == 0 or qb == 7:
                        e_ = 0 if qb == 0 else 1
                        ul.append((0, 4 * qb + 2 + j, state['T'][:, e_ * 8 + j, :], [T.key(state['k'], (e_, j))]))
                    else:
                        r0 = max(0, 2 * j - 7)
                        r1 = min(7, 2 * j + 1)
                        ul.append((0, 4 * qb + 2 + j, state['EZ'][:, w0 * 64:(w0 + 8) * 64], [state['ek']],
                                   r0 * 64, (r1 + 1) * 64))
                ul.sort(key=lambda x: 0 if (len(x) < 5 or (x[4] == 0 and x[5] == 512)) else 1)
                return ul
            src = dict(Q=(d['S_qc'], h * 64), K=[(d['S_kc'], h * 64, 64, 0)], V=(d['S_vc'], h), nkeys=EXTC)
            heads.append(dict(dk=64, scale=0.125, srcs=[src], out=(d['S_yc'], h * 64), units=units, pre=pre,
                              tkeys=state))
        self.attn_heads(heads, nslot=3, nkmax=EXTC)
        a.release(m0)
        self.p.phase_barrier()

    def phase_G(self, L):
        a = self.arena
        d = self.dr
        self.h2T = a.alloc(128, 8 * OWN, BF16).rearrange('p (c t) -> p c t', c=8)
        self.mG = a.mark()
        wst = Buf(a, 128, 2048, F32, 2)
        wpa, kpa = self.load_w_res(d['w_pa%d' % L], 4, 1024, wst)
        wpb, kpb = self.load_w_res(d['w_pb%d' % L], 2, 1024, wst)
        wpc, kpc = self.load_w_res(d['w_pc%d' % L], 4, 1024, wst)
        wo, ko = self.load_w_res(d['w_o%d' % L], 8, 1024, wst)
        gbc = a.alloc(128, D, F32)
        self.dma(gbc, d['g_ffn%d' % L], [], ['gffn'])
        ya = Buf(a, 128, 4 * 512, BF16, 2)
        yb = Buf(a, 128, 2 * 512, BF16, 2)
        yc = Buf(a, 128, 4 * 512, BF16, 2)
        gt = Buf(a, 128, 3 * 512, BF16, 3)
        mg = Buf(a, 128, 8 * 512, BF16, 2)
        tf = Buf(a, 128, 512, F32, 5)
        xt = Buf(a, 128, D, F32, 3)
        nbufs = self.norm_bufs()
        def stage1(tb):
            t0 = tb * 512
            sa, sb_, sc = ya.next(), yb.next(), yc.next()
            self.dma(ya.ap(sa).rearrange('p (c t) -> p c t', c=4),
                     d['S_ya'][:, t0:t0 + 512].rearrange('(c p) t -> p c t', p=128), [], [ya.key(sa)])
            self.dma(yb.ap(sb_).rearrange('p (c t) -> p c t', c=2),
                     d['S_yb'][:, t0:t0 + 512].rearrange('(c p) t -> p c t', p=128), [], [yb.key(sb_)])
            self.dma(yc.ap(sc).rearrange('p (c t) -> p c t', c=4),
                     d['S_yc'][:, t0:t0 + 512].rearrange('(c p) t -> p c t', p=128), [], [yc.key(sc)])
            yA = ya.ap(sa).rearrange('p (c t) -> p c t', c=4)
            yB = yb.ap(sb_).rearrange('p (c t) -> p c t', c=2)
            yC = yc.ap(sc).rearrange('p (c t) -> p c t', c=4)
            sm = mg.next()
            mgv = mg.ap(sm).rearrange('p (c t) -> p c t', c=8)
            for oc in range(8):
                sg = gt.next()
                gv = gt.ap(sg).rearrange('p (c t) -> p c t', c=3)
                self.dma(gv, d['S_gate'].rearrange('(b c p) t -> c p b t', b=3, p=128)[oc][:, :, t0:t0 + 512],
                         [], [gt.key(sg)])
                ts_ = []
                for (y, yk, w, wk, nk, bi) in ((yA, ya.key(sa), wpa, kpa, 4, 0), (yB, yb.key(sb_), wpb, kpb, 2, 1),
                                               (yC, yc.key(sc), wpc, kpc, 4, 2)):
                    b = self.nb()
                    for c in range(nk):
                        self.mm(self.bank(b), w[:, c, oc * 128:(oc + 1) * 128], y[:, c, :], c == 0, c == nk - 1,
                                wk + [yk], [self.pk(b)])
                    s = tf.next()
                    self.tt('vector', tf.ap(s), self.bank(b), gv[:, bi, :], ALU.mult, [self.pk(b), gt.key(sg)], [tf.key(s)])
                    ts_.append(s)
                s4 = tf.next()
                self.tt('gpsimd', tf.ap(s4), tf.ap(ts_[0]), tf.ap(ts_[1]), ALU.add, [tf.key(ts_[0]), tf.key(ts_[1])], [tf.key(s4)])
                self.tt('gpsimd', mgv[:, oc, :], tf.ap(s4), tf.ap(ts_[2]), ALU.add, [tf.key(s4), tf.key(ts_[2])],
                        [mg.key(sm, oc)])
            mkeys = [mg.key(sm, oc) for oc in range(8)]
            return mgv, mkeys

        def stage2(tb, mgv, mkeys):
            t0 = tb * 512
            for sub in range(4):
                sx = xt.next()
                r0 = t0 + sub * 128
                rx = self.xrow(r0)
                self.dma(xt.ap(sx), self.xsrc[rx:rx + 128, :], [], [xt.key(sx)])
                for half in range(2):
                    b = self.nb()
                    for c in range(8):
                        self.mm(self.bank(b), mgv[:, c, sub * 128:(sub + 1) * 128], wo[:, c, half * 512:(half + 1) * 512],
                                c == 0, c == 7, mkeys + ko, [self.pk(b)])
                    self.tt('vector', xt.ap(sx)[:, half * 512:(half + 1) * 512], xt.ap(sx)[:, half * 512:(half + 1) * 512],
                            self.bank(b), ALU.add, [xt.key(sx), self.pk(b)], [xt.key(sx)])
                self.dma(d['S_x1'][r0:r0 + 128, :], xt.ap(sx), [xt.key(sx)], [('x1', r0)], q='gpsimd')
                ti = tb * 4 + sub
                self.norm_transpose(xt.ap(sx), xt.key(sx), gbc, 'gffn', self.h2T[:, :, ti * 128:(ti + 1) * 128],
                                    ('h2T', tb, sub), nbufs)
        pend = {0: stage1(0)}
        for tb in range(NB):
            if tb + 1 < NB:
                pend[tb + 1] = stage1(tb + 1)
            stage2(tb, *pend.pop(tb))
        a.release(self.mG)
        self.p.phase_barrier()

    def phase_F(self, L, final):
        a = self.arena
        d = self.dr
        h2T = self.h2T
        m1 = a.mark()
        wst = Buf(a, 128, 8 * 256, F32, 4)
        wbf = Buf(a, 128, 8 * 256, BF16, 4)
        sgb = Buf(a, 128, 512, F32, 3)
        ob = Buf(a, 128, 512, BF16, 4)
        cols = list(range(0, DFF, 256))

        def f_loads(col):
            return (self.load_w(d['w1%d' % L][:, col:col + 256], 8, 256, wst, wbf),
                    self.load_w(d['w3%d' % L][:, col:col + 256], 8, 256, wst, wbf))
        cur = f_loads(cols[0])
        for gi, col in enumerate(cols):
            nxt = f_loads(cols[gi + 1]) if gi + 1 < len(cols) else None
            (w1, k1), (w3, k3) = cur
            for tb in range(NB):
                for ch in range(2):
                    b1 = self.nb()
                    b3 = self.nb()
                    for (bb, ww, kk) in ((b1, w1, k1), (b3, w3, k3)):
                        for c in range(8):
                            self.mm(self.bank(bb), ww[:, c, ch * 128:(ch + 1) * 128], h2T[:, c, tb * 512:(tb + 1) * 512],
                                    c == 0, c == 7, [kk], [self.pk(bb)])
                    s = sgb.next()
                    self.act(sgb.ap(s), self.bank(b1), AF.Silu, [self.pk(b1)], [sgb.key(s)])
                    o = ob.next()
                    self.tt('vector', ob.ap(o), sgb.ap(s), self.bank(b3), ALU.mult, [sgb.key(s), self.pk(b3)], [ob.key(o)])
                    f = col // 128 + ch
                    self.dma(d['S_act'][f * 128:(f + 1) * 128, tb * 512:(tb + 1) * 512], ob.ap(o), [ob.key(o)],
                             [('sact', f, tb)], q='gpsimd')
            cur = nxt
        a.release(m1)
        self.p.phase_barrier()
        a.release(self.mG)
        a.off = 0 + self._const_end
        wst = Buf(a, 128, 4096, F32, 2)
        w2, k2 = self.load_w_res(d['w2%d' % L], 22, 1024, wst)
        gfin = a.alloc(128, D, F32)
        self.dma(gfin, d['g_final'], [], ['gfin'])
        at = Buf(a, 128, 22 * 512, BF16, 2)
        xt = Buf(a, 128, D, F32, 3)
        yn = Buf(a, 128, D, F32, 2)
        junk = Buf(a, 128, D, BF16, 1)
        ss = Buf(a, 128, 4, F32, 4)
        for tb in range(NB):
            t0 = tb * 512
            sa = at.next()
            av = at.ap(sa).rearrange('p (f t) -> p f t', f=22)
            self.dma(av, d['S_act'][:, t0:t0 + 512].rearrange('(f p) t -> p f t', p=128), [], [at.key(sa)])
            for sub in range(4):
                r0 = t0 + sub * 128
                sx = xt.next()
                self.dma(xt.ap(sx), d['S_x1'][r0:r0 + 128, :], [], [xt.key(sx)])
                for half in range(2):
                    b = self.nb()
                    for f in range(22):
                        self.mm(self.bank(b), av[:, f, sub * 128:(sub + 1) * 128], w2[:, f, half * 512:(half + 1) * 512],
                                f == 0, f == 21, [at.key(sa)] + k2, [self.pk(b)])
                    self.tt('vector', xt.ap(sx)[:, half * 512:(half + 1) * 512], xt.ap(sx)[:, half * 512:(half + 1) * 512],
                            self.bank(b), ALU.add, [xt.key(sx), self.pk(b)], [xt.key(sx)])
                self.dma(self.out_raw[r0:r0 + 128, :], xt.ap(sx), [xt.key(sx)], [('yraw', r0)], q='gpsimd')
                if final:
                    sj = junk.next()
                    s1 = ss.next()
                    self.act(junk.ap(sj), xt.ap(sx), AF.Square, [xt.key(sx)], [junk.key(sj), ss.key(s1, 'a')],
                             accum=ss.ap(s1)[:, 0:1])
                    self.act(ss.ap(s1)[:, 1:2], ss.ap(s1)[:, 0:1], AF.Sqrt, [ss.key(s1, 'a'), 'eps'], [ss.key(s1, 'b')],
                             scale=1.0 / D, bias=self.epsA[:, 0:1])
                    self.recip(ss.ap(s1)[:, 2:3], ss.ap(s1)[:, 1:2], [ss.key(s1, 'b')], [ss.key(s1, 'c')])
                    sy = yn.next()
                    self.stt(yn.ap(sy), xt.ap(sx), ss.ap(s1)[:, 2:3], gfin, ALU.mult, ALU.mult,
                             [xt.key(sx), ss.key(s1, 'c'), 'gfin'], [yn.key(sy)])
                    self.dma(d['y_norm'][r0:r0 + 128, :], yn.ap(sy), [yn.key(sy)], [('ynorm', r0)], q='gpsimd')
        a.off = self._const_end
        self.p.phase_barrier()


W_SHAPES = dict(w_in=(1024, IN_COLS), w_sw=(1024, 1568), w_uq=(256, 768), w_uq_sw=(256, 768), w_uk=(128, 512),
                w_uv=(128, 512), g_mix=(128, D), g_q=(128, 2), g_kv=(128, 1), ebias=(8, 128, 24 * 64), ebiasz=(8, 128, 24 * 64),
                w_pa=(512, D), w_pb=(256, D), w_pc=(512, D), w_o=(D, D), g_ffn=(128, D), w1=(D, DFF), w3=(D, DFF),
                w2=(DFF, D))
C_SHAPES = dict(ident=(128, 128), valB=(128, 48), CA=(96, OWN), SA=(96, OWN), Ck=(32, S), Sk=(32, S),
                cosB=(128, S), sinB=(128, S), Bmask=(34, 128, 512), NAvalid=(24, 128, 512), g_final=(128, D))
SCR = dict(S_qA=((768, OWN), BF16), S_kA=((512, S), BF16), S_kr=((32, S), BF16), S_vA=((128, 64, 8, 66), BF16),
           S_qb=((768, OWN), BF16), S_kb=((768, EXTB), BF16), S_vb=((128, 48, 12, 66), BF16),
           S_qc=((512, OWN), BF16), S_kc=((512, EXTC), BF16), S_vc=((128, 40, 8, 66), BF16),
           S_gate=((3072, OWN), BF16), S_ya=((512, OWN), BF16), S_yb=((256, OWN), BF16), S_yc=((512, OWN), BF16),
           S_x1=((OWN, D), F32), S_act=((DFF, OWN), BF16))


PC_NAMES = ('valB', 'CA', 'SA', 'Ck', 'Sk', 'cosB', 'sinB', 'NAvalid')


def build_nc(fused=True, dbg=None, upto=99):
    nc = bass.Bass("TRN2", target_bir_lowering=False)
    B = Layer(nc, dbg)
    B.dram_in('xs', (S, D))
    sets = ('', '_p') if fused else ('',)
    for k, shp in C_SHAPES.items():
        if k in PC_NAMES:
            for cs in sets:
                B.dram_in(k + cs, shp)
        else:
            B.dram_in(k, shp)
    for l in range(2 if fused else 1):
        for k, shp in W_SHAPES.items():
            B.dram_in(k + str(l), shp)
    B.dram_out('y_raw', (OWN, D))
    B.dram_out('y_norm', (OWN, D))
    for k, (shp, dt) in SCR.items():
        B.dram_scr(k, shp, dt)
    B.setup_consts()
    B._const_end = B.arena.off
    B.p.phase_barrier()

    def run_pass(L, cs, xsrc, swap, out_raw, final):
        B.begin_pass(L, cs, xsrc, swap, out_raw, final)
        phs = [lambda: B.phase_A(L, True), lambda: B.phase_A(L, False), lambda: B.phase_M(L), lambda: B.phase_B(L),
               lambda: B.phase_C(L), lambda: B.phase_G(L), lambda: B.phase_F(L, final)]
        for ph in phs[:upto]:
            ph()
    if fused:
        X1 = B.dram_scr('X1', (S, D), F32)
        run_pass(0, '', B.dr['xs'], False, X1[0:OWN], False)
        run_pass(0, '_p', B.dr['xs'], True, X1[OWN:S], False)
        run_pass(1, '', X1, False, B.dr['y_raw'], True)
    else:
        run_pass(0, '', B.dr['xs'], False, B.dr['y_raw'], True)
    B.p.emit()
    return nc, B


def _rope_tabs(pos, dim):
    inv = np.power(np.float32(10000.0), -np.arange(0, dim, 2, dtype=np.float32) / np.float32(dim)).astype(np.float32)
    ang = pos.astype(np.float32)[:, None] * inv[None, :]
    return np.cos(ang).astype(np.float32), np.sin(ang).astype(np.float32)


def core_consts(h):
    perm = _perm_local(h)
    c = {}
    c['ident'] = np.eye(128, dtype=np.float32)
    e = np.arange(EXTB)
    t = h * OWN - 1024 + e
    valid = ((t >= 0) & (t < S)).astype(np.float32)
    c['valB'] = np.ascontiguousarray(valid.reshape(48, 128).T)
    cb, sb = _rope_tabs(perm, 64)
    f = np.arange(128)
    sign = np.where((f % 64) < 32, -1.0, 1.0).astype(np.float32)
    c['cosB'] = np.ascontiguousarray(cb[:, f % 32].T)
    c['sinB'] = np.ascontiguousarray((sb[:, f % 32] * sign[None, :]).T)
    ca, sa = _rope_tabs(perm, 32)
    j = np.arange(32)
    sgn = np.where(j < 16, -1.0, 1.0).astype(np.float32)
    c['Ck'] = np.ascontiguousarray(ca[:, j % 16].T)
    c['Sk'] = np.ascontiguousarray((sa[:, j % 16] * sgn[None, :]).T)
    CA = np.ones((96, OWN), np.float32)
    SA = np.zeros((96, OWN), np.float32)
    CA[64:96] = c['Ck'][:, :OWN]
    SA[64:96] = c['Sk'][:, :OWN]
    c['CA'], c['SA'] = CA, SA
    masks = []
    ii = np.arange(128)[:, None]
    jj = np.arange(512)[None, :]
    for dil, rels in ((1, range(-1, 5)), (4, range(-2, 6)), (16, range(-8, 12))):
        for rel in rels:
            dlt = 128 * rel + ii - jj
            masks.append(((dlt % dil == 0) & (np.abs(dlt) <= 64 * dil)).astype(np.float32))
    c['Bmask'] = np.stack(masks)
    nav = np.zeros((24, 128, 512), np.float32)
    for ty, i in enumerate((0, 1, 7)):
        for jx in range(8):
            for a_ in range(2):
                for rho in range(8):
                    Rr = 64 * h + 8 * i + rho
                    KR = 64 * h + 8 * i - 4 + 2 * jx + a_
                    rs = min(max(Rr - 4, 0), 120)
                    ok = (0 <= KR < 128) and (rs <= KR < rs + 8)
                    if ok:
                        nav[ty * 8 + jx, a_ * 64:(a_ + 1) * 64, rho * 64:(rho + 1) * 64] = 1.0
    c['NAvalid'] = nav
    return c


def layer_weights(inp, l):
    w = {}
    w_in = np.asarray(inp['w_in'][l], np.float32)
    w['w_in'] = w_in

    def swap_heads(cols, nh, hd):
        half = hd // 2
        parts = []
        for hh in range(nh):
            b = hh * hd
            parts += [cols[:, b + half:b + hd], cols[:, b:b + half]]
        return np.concatenate(parts, axis=1)
    w['w_sw'] = np.ascontiguousarray(np.concatenate([swap_heads(w_in[:, 384:416], 1, 32), swap_heads(w_in[:, 416:1184], 12, 64),
                                                     swap_heads(w_in[:, 1184:1952], 12, 64)], axis=1))
    wuq = np.asarray(inp['w_uq'][l], np.float32)
    w['w_uq'] = wuq
    sw = wuq.copy()
    for hh in range(8):
        b = hh * 96 + 64
        sw[:, b:b + 16] = wuq[:, b + 16:b + 32]
        sw[:, b + 16:b + 32] = wuq[:, b:b + 16]
    w['w_uq_sw'] = sw
    wukv = np.asarray(inp['w_ukv'][l], np.float32).reshape(128, 8, 128)
    w['w_uk'] = np.ascontiguousarray(wukv[:, :, :64].reshape(128, 512))
    w['w_uv'] = np.ascontiguousarray(wukv[:, :, 64:].reshape(128, 512))
    w['g_mix'] = np.ascontiguousarray(np.broadcast_to(np.asarray(inp['g_mix'][l], np.float32)[None, :], (128, D)))
    w['g_ffn'] = np.ascontiguousarray(np.broadcast_to(np.asarray(inp['g_ffn'][l], np.float32)[None, :], (128, D)))
    w['g_q'] = np.ascontiguousarray(np.asarray(inp['g_q'][l], np.float32).reshape(2, 128).T)
    w['g_kv'] = np.ascontiguousarray(np.asarray(inp['g_kv'][l], np.float32).reshape(128, 1))
    rpb = np.asarray(inp['rpb'][l], np.float32)
    kc = np.arange(64)[:, None]
    cc = np.arange(64)[None, :]
    cs = np.clip(cc - 8, 0, 48)
    colok = (kc >= cs) & (kc < cs + 16)
    dc = np.clip(kc - cc + 15, 0, 30)
    NEG = np.float32(-30000.0)
    ebf = np.full((8, 64, 23, 64), NEG, np.float32)
    ebz = np.full((8, 64, 23, 64), NEG, np.float32)
    for slot in range(23):
        dr = 18 - slot
        if 0 <= dr <= 14:
            vals = np.where(colok[None], rpb[:, dr][:, dc], NEG)
            ebf[:, :, slot, :] = vals
            if 3 <= dr <= 10:
                ebz[:, :, slot, :] = vals

    def shifted(e):
        pad = np.full((8, 64, 1, 64), NEG, np.float32)
        lo = np.concatenate([e, pad], axis=2)
        hi = np.concatenate([pad, e], axis=2)
        return np.ascontiguousarray(np.concatenate([lo, hi], axis=1).reshape(8, 128, 24 * 64))
    w['ebias'] = shifted(ebf)
    w['ebiasz'] = shifted(ebz)
    for k in ('w_pa', 'w_pb', 'w_pc', 'w_o', 'w1', 'w3', 'w2'):
        w[k] = np.ascontiguousarray(np.asarray(inp[k][l], np.float32))
    return w


_NC_CACHE = {}


def kernel(**inputs):
    x = np.asarray(inputs['x'], np.float32)
    gfin_bc = np.ascontiguousarray(np.broadcast_to(np.asarray(inputs['g_final'], np.float32)[None, :], (128, D)))
    consts = [core_consts(0), core_consts(1)]
    if 'f' not in _NC_CACHE:
        _NC_CACHE['f'] = build_nc(True)
    nc, B = _NC_CACHE['f']
    ws = [layer_weights(inputs, l) for l in range(2)]
    in_maps = []
    for c in range(8):
        b, h = c // 2, c % 2
        m = {'xs': np.ascontiguousarray(x[b][_perm_local(h)]), 'g_final': gfin_bc}
        for k, v in consts[h].items():
            m[k] = v
        for k in PC_NAMES:
            m[k + '_p'] = consts[1 - h][k]
        for l in range(2):
            for k, v in ws[l].items():
                m[k + str(l)] = v
        in_maps.append(m)
    res = run_bass_kernel_spmd(nc, in_maps, core_ids=list(range(8)))
    out = np.empty_like(x)
    for c in range(8):
        b, h = c // 2, c % 2
        out[b, h * OWN:(h + 1) * OWN] = res.results[c]['y_norm']
    return out
```

```python
import numpy as np
import concourse.bass as bass
import concourse.mybir as mybir
from concourse.bass_utils import run_bass_kernel_spmd

F32 = mybir.dt.float32
BF16 = mybir.dt.bfloat16
U8 = mybir.dt.uint8
AF = mybir.ActivationFunctionType
ALU = mybir.AluOpType

D = 1024
S = 8192
OWN = 4096
NB = 8
EPS = 1e-6
IN_COLS = 7328
DFF = 2816
EXTB = 6144
EXTC = 5120
ISZ = {F32: 4, BF16: 2, U8: 1}
ENGS = ['tensor', 'vector', 'scalar', 'gpsimd', 'sync']
CH = 16000
DMAK = 12


class Prog:
    def __init__(self, nc):
        self.nc = nc
        self.ops = []
        self.lastw = {}
        self.readers = {}
        self.barrier = {e: None for e in ENGS}

    def add(self, eng, fn, reads=(), writes=(), dma=False):
        i = len(self.ops)
        deps = set()
        for r in reads:
            j = self.lastw.get(r)
            if j is not None:
                deps.add(j)
        for w in writes:
            j = self.lastw.get(w)
            if j is not None:
                deps.add(j)
            deps.update(self.readers.get(w, ()))
        if self.barrier[eng] is not None:
            deps.update(self.barrier[eng])
            self.barrier[eng] = None
        for r in reads:
            self.readers.setdefault(r, []).append(i)
        for w in writes:
            self.lastw[w] = i
            self.readers[w] = []
        self.ops.append(dict(eng=eng, fn=fn, deps=deps, dma=dma))
        return i

    def phase_barrier(self):
        last = set()
        seen_c = set()
        seen_d = {}
        for i in range(len(self.ops) - 1, -1, -1):
            o = self.ops[i]
            if o['dma']:
                c = seen_d.get(o['eng'], 0)
                if c < DMAK:
                    last.add(i)
                    seen_d[o['eng']] = c + 1
            elif o['eng'] not in seen_c:
                seen_c.add(o['eng'])
                last.add(i)
            if len(seen_c) >= 4 and all(seen_d.get(e, 0) >= DMAK for e in ('sync', 'gpsimd')):
                break
        for e in ENGS:
            self.barrier[e] = set(last)
        self.lastw = {}
        self.readers = {}

    def emit(self):
        nc = self.nc
        ops = self.ops
        n = len(ops)
        needed = [False] * n
        for o in ops:
            for j in o['deps']:
                oj = ops[j]
                if oj['eng'] == 'tensor' and o['eng'] == 'tensor' and not oj['dma'] and not o['dma']:
                    continue
                needed[j] = True
        sig = [None] * n
        ccount = {e: 0 for e in ENGS}
        dcount = {e: 0 for e in ENGS}
        csems = {e: [] for e in ENGS}
        dsems = {e: [] for e in ENGS}
        dprev = [None] * n
        for i, o in enumerate(ops):
            e = o['eng']
            if o['dma']:
                k = dcount[e]
                dcount[e] += 1
                slot = k % DMAK
                if slot >= len(dsems[e]):
                    dsems[e].append(nc.alloc_semaphore('d_%s_%d' % (e, slot)))
                val = 16 * (k // DMAK + 1)
                sig[i] = (dsems[e][slot], val)
                if val > 16:
                    dprev[i] = (dsems[e][slot], val - 16)
            elif needed[i]:
                k = ccount[e]
                ccount[e] += 1
                si = k // CH
                if si >= len(csems[e]):
                    csems[e].append(nc.alloc_semaphore('c_%s_%d' % (e, si)))
                sig[i] = (csems[e][si], k % CH + 1)
        per = {e: [] for e in ENGS}
        for i, o in enumerate(ops):
            per[o['eng']].append(i)
        finals = []
        for e in ENGS:
            k = dcount[e]
            for slot in range(min(k, DMAK)):
                cnt = (k - 1 - slot) // DMAK + 1
                finals.append((dsems[e][slot], 16 * cnt))

        def run(eng_name, e):
            waited = {}

            def wait(sem, val):
                key = sem.num
                if waited.get(key, 0) < val:
                    e.wait_ge(sem, val)
                    waited[key] = val
            for i in per[eng_name]:
                o = ops[i]
                for j in sorted(o['deps']):
                    if sig[j] is None:
                        continue
                    oj = ops[j]
                    if oj['eng'] == 'tensor' and eng_name == 'tensor' and not oj['dma'] and not o['dma']:
                        continue
                    wait(*sig[j])
                if dprev[i] is not None:
                    wait(*dprev[i])
                ins = o['fn'](e)
                if sig[i] is not None:
                    ins.then_inc(sig[i][0], 16 if o['dma'] else 1)
            if eng_name == 'sync':
                for sem, val in finals:
                    wait(sem, val)

        with nc.Block() as block:
            @block.tensor
            def _(e):
                run('tensor', e)

            @block.vector
            def _(e):
                run('vector', e)

            @block.scalar
            def _(e):
                run('scalar', e)

            @block.gpsimd
            def _(e):
                run('gpsimd', e)

            @block.sync
            def _(e):
                run('sync', e)


class Arena:
    def __init__(self, nc, nbytes):
        self.t = nc.alloc_sbuf_tensor('arena', [128, nbytes], U8)
        self.cap = nbytes
        self.off = 0
        self.n = 0

    def alloc(self, parts, elems, dt, p0=0):
        size = elems * ISZ[dt]
        size = (size + 63) // 64 * 64
        assert self.off + size <= self.cap, ('SBUF overflow', self.off, size)
        ap = self.t[p0:p0 + parts, self.off:self.off + elems * ISZ[dt]].bitcast(dt)
        self.off += size
        self.n += 1
        return ap

    def mark(self):
        return self.off

    def release(self, m):
        self.off = m


class Buf:
    _cnt = [0]

    def __init__(self, arena, parts, elems, dt, nslot=1, p0=0):
        Buf._cnt[0] += 1
        self.id = Buf._cnt[0]
        self.aps = [arena.alloc(parts, elems, dt, p0) for _ in range(nslot)]
        self.nslot = nslot
        self.i = -1

    def next(self):
        self.i += 1
        return self.i % self.nslot

    def ap(self, s):
        return self.aps[s]

    def key(self, s, sub=None):
        return ('b', self.id, s, sub)


class Builder:
    def __init__(self, nc, dbg=None):
        self.nc = nc
        self.p = Prog(nc)
        self.arena = Arena(nc, 206 * 1024)
        self.ps = nc.alloc_psum_tensor('ps', [128, 4096], F32)
        self.psi = -1
        self.dr = {}
        self.dbg = dbg
        self.did = 0

    def dram_in(self, name, shape, dt=F32):
        t = self.nc.dram_tensor(name, list(shape), dt, kind="ExternalInput").ap()
        self.dr[name] = t
        return t

    def dram_out(self, name, shape, dt=F32):
        t = self.nc.dram_tensor(name, list(shape), dt, kind="ExternalOutput").ap()
        self.dr[name] = t
        return t

    def dram_scr(self, name, shape, dt=BF16):
        kind = "ExternalOutput" if (self.dbg and name in self.dbg) else "Internal"
        t = self.nc.dram_tensor(name, list(shape), dt, kind=kind).ap()
        self.dr[name] = t
        return t

    def bank(self, b):
        return self.ps[:, b * 512:(b + 1) * 512]

    def pk(self, b):
        return ('ps', b)

    def dma(self, out, in_, reads, writes, q='sync'):
        self.p.add(q, lambda e, o=out, i=in_: e.dma_start(out=o, in_=i), reads, writes, dma=True)

    def mm(self, out, lhsT, rhs, start, stop, reads, writes, skip=False):
        def f(e, o=out, l=lhsT, r=rhs, s=start, t=stop, k=skip):
            if k:
                return e.matmul(o, l, r, start=s, stop=t, skip_group_check=True)
            return e.matmul(o, l, r, start=s, stop=t)
        self.p.add('tensor', f, reads, writes)

    def act(self, out, in_, func, reads, writes, scale=None, bias=None, accum=None):
        def f(e, o=out, i=in_, fn=func, sc=scale, bi=bias, ac=accum):
            kw = {}
            if sc is not None:
                kw['scale'] = sc
            if bi is not None:
                kw['bias'] = bi
            if ac is not None:
                kw['accum_out'] = ac
            return e.activation(o, i, fn, **kw)
        self.p.add('scalar', f, reads, writes)

    def tt(self, eng, out, in0, in1, op, reads, writes):
        self.p.add(eng, lambda e, o=out, a=in0, b=in1, p=op: e.tensor_tensor(o, a, b, p), reads, writes)

    def ts(self, eng, out, in0, s1, s2, op0, op1, reads, writes):
        def f(e, o=out, a=in0, x=s1, y=s2, p=op0, q=op1):
            if q is None:
                return e.tensor_scalar(o, a, x, None, p)
            return e.tensor_scalar(o, a, x, y, p, q)
        self.p.add(eng, f, reads, writes)

    def stt(self, out, in0, scalar, in1, op0, op1, reads, writes):
        self.p.add('vector', lambda e, o=out, a=in0, s=scalar, b=in1, p=op0, q=op1:
                   e.scalar_tensor_tensor(o, a, s, b, p, q), reads, writes)

    def copy(self, eng, out, in_, reads, writes):
        if eng == 'scalar':
            self.act(out, in_, AF.Copy, reads, writes)
        else:
            self.p.add(eng, lambda e, o=out, i=in_: e.tensor_copy(o, i), reads, writes)

    def memset(self, eng, ap, val, writes):
        self.p.add(eng, lambda e, a=ap, v=val: e.memset(a, v), (), writes)

    def recip(self, out, in_, reads, writes):
        self.p.add('vector', lambda e, o=out, i=in_: e.reciprocal(o, i), reads, writes)

    def transpose(self, out, in_, ident, reads, writes):
        self.p.add('tensor', lambda e, o=out, i=in_, d=ident: e.transpose(o, i, d), reads, writes)


def _perm_local(h):
    own = np.arange(h * OWN, (h + 1) * OWN)
    par = np.arange((1 - h) * OWN, (2 - h) * OWN)
    return np.concatenate([own, par])


class Layer(Builder):
    def nb(self):
        self.psi = (self.psi + 1) % 8
        return self.psi

    def setup_consts(self):
        a = self.arena
        d = self.dr
        self.ident = a.alloc(128, 128, BF16)
        self.ones = a.alloc(128, 128, BF16)
        self.epsA = a.alloc(128, 1, F32)
        stage = a.alloc(128, 128, F32)
        self.dma(stage, d['ident'], [], ['c_stage'])
        self.copy('vector', self.ident, stage, ['c_stage'], ['ident'])
        self.memset('vector', self.ones, 1.0, ['ones'])
        self.memset('vector', self.epsA, EPS, ['eps'])
        self.onescol = a.alloc(128, 12 * 2, BF16)
        self.memset('vector', self.onescol, 1.0, ['onescol'])
        self.valB = a.alloc(128, 48, F32)
        self.valOne = a.alloc(128, 64, F32)
        self.memset('vector', self.valOne, 1.0, ['valOne'])

    def begin_pass(self, L, cs, xsrc, swap, out_raw, final):
        self.L, self.cs, self.xsrc, self.swap, self.out_raw, self.final = L, cs, xsrc, swap, out_raw, final
        self.dma(self.valB, self.dr['valB' + cs], [], ['valB'])
        self.p.phase_barrier()

    def xrow(self, u):
        return (u + OWN) % S if self.swap else u

    def norm_transpose(self, xt_ap, xt_key, gbc, gkey, hT_dst, hT_key, bufs, defer=False):
        junk, ss, xn = bufs['junk'], bufs['ss'], bufs['xn']
        sj = junk.next()
        s1 = ss.next()
        self.act(junk.ap(sj), xt_ap, AF.Square, [xt_key], [junk.key(sj), ss.key(s1, 'a')],
                 accum=ss.ap(s1)[:, 0:1])
        self.act(ss.ap(s1)[:, 1:2], ss.ap(s1)[:, 0:1], AF.Sqrt, [ss.key(s1, 'a'), 'eps'], [ss.key(s1, 'b')],
                 scale=1.0 / D, bias=self.epsA[:, 0:1])
        self.recip(ss.ap(s1)[:, 2:3], ss.ap(s1)[:, 1:2], [ss.key(s1, 'b')], [ss.key(s1, 'c')])
        sx = xn.next()
        self.stt(xn.ap(sx), xt_ap, ss.ap(s1)[:, 2:3], gbc, ALU.mult, ALU.mult,
                 [xt_key, ss.key(s1, 'c'), gkey], [xn.key(sx)])
        def part2():
            b = self.nb()
            pb = self.bank(b).bitcast(BF16)
            for c in range(8):
                self.transpose(pb[:, c * 128:(c + 1) * 128], xn.ap(sx)[:, c * 128:(c + 1) * 128], self.ident,
                               [xn.key(sx), 'ident'], [self.pk(b)])
            self.copy('vector', hT_dst, pb.rearrange('p (c t) -> p c t', c=8), [self.pk(b)], [hT_key])
        if defer:
            return part2
        part2()

    def norm_bufs(self):
        a = self.arena
        return dict(junk=Buf(a, 128, D, BF16, 1), ss=Buf(a, 128, 4, F32, 4), xn=Buf(a, 128, D, BF16, 2))

    def load_w(self, src, kc, ncols, wst, wbf):
        s = wst.next()
        st = wst.ap(s)[:, 0:kc * ncols].rearrange('p (c n) -> p c n', c=kc)
        if kc == 1:
            self.dma(wst.ap(s)[:, 0:ncols], src, [], [wst.key(s)])
        else:
            self.dma(st, src.rearrange('(c p) n -> p c n', p=128), [], [wst.key(s)])
        t = wbf.next()
        wb = wbf.ap(t)[:, 0:kc * ncols].rearrange('p (c n) -> p c n', c=kc)
        self.copy('scalar', wbf.ap(t)[:, 0:kc * ncols], wst.ap(s)[:, 0:kc * ncols], [wst.key(s)], [wbf.key(t)])
        return wb, wbf.key(t)

    def load_w_res(self, src, kc, ncols, wst):
        dst = self.arena.alloc(128, kc * ncols, BF16)
        key = ('wres', self.arena.n)
        done = 0
        per = max(1, (wst.aps[0].shape[1]) // ncols)
        while done < kc:
            k = min(per, kc - done)
            s = wst.next()
            st = wst.ap(s)[:, 0:k * ncols].rearrange('p (c n) -> p c n', c=k)
            self.dma(st, src[done * 128:(done + k) * 128, :].rearrange('(c p) n -> p c n', p=128), [], [wst.key(s)])
            self.copy('scalar', dst[:, done * ncols:(done + k) * ncols], wst.ap(s)[:, 0:k * ncols],
                      [wst.key(s)], [key + (done,)])
            done += k
        keys = [key + (i,) for i in range(0, kc, per)]
        return dst.rearrange('p (c n) -> p c n', c=kc), keys

    def fm_norm(self, banks, nch, gcol, gkey, nfeat, t):
        raw, sq, sd, out = t['raw'], t['sq'], t['sd'], t['cn']
        rs, qs = [], []
        for c in range(nch):
            r = raw.next()
            q = sq.next()
            self.copy('scalar', raw.ap(r), self.bank(banks[c]), [self.pk(banks[c])], [raw.key(r)])
            self.act(sq.ap(q), self.bank(banks[c]), AF.Square, [self.pk(banks[c])], [sq.key(q)])
            rs.append(r)
            qs.append(q)
        b = self.nb()
        for c in range(nch):
            self.mm(self.bank(b), self.ones, sq.ap(qs[c]), c == 0, c == nch - 1,
                    ['ones', sq.key(qs[c])], [self.pk(b)])
        s = sd.next()
        self.act(sd.ap(s), self.bank(b), AF.Sqrt, [self.pk(b), 'eps'], [sd.key(s, 'a')],
                 scale=1.0 / nfeat, bias=self.epsA[:, 0:1])
        self.recip(sd.ap(s), sd.ap(s), [sd.key(s, 'a')], [sd.key(s, 'a')])
        outs = []
        for c in range(nch):
            o = out.next()
            self.stt(out.ap(o), raw.ap(rs[c]), gcol[:, c:c + 1], sd.ap(s), ALU.mult, ALU.mult,
                     [raw.key(rs[c]), sd.key(s, 'a'), gkey], [out.key(o)])
            outs.append(o)
        return outs

    def rope(self, bA, bB, rows, cosap, sinap, tkeys, t, outbuf):
        t1, t2 = t['r1'], t['r2']
        s1 = t1.next()
        s2 = t2.next()
        self.tt('vector', t1.ap(s1)[0:rows], self.bank(bB)[0:rows], sinap, ALU.mult,
                [self.pk(bB)] + tkeys, [t1.key(s1)])
        self.tt('vector', t2.ap(s2)[0:rows], self.bank(bA)[0:rows], cosap, ALU.mult,
                [self.pk(bA)] + tkeys, [t2.key(s2)])
        o = outbuf.next()
        self.tt('gpsimd', outbuf.ap(o)[0:rows], t1.ap(s1)[0:rows], t2.ap(s2)[0:rows], ALU.add,
                [t1.key(s1), t2.key(s2)], [outbuf.key(o)])
        return o

    def vtok(self, lhs_fn, nk, w, wkeys, col0, nh, val, valkey, dst, tile_idx, hd0, t, lkeys):
        vs = t['vst']
        b = self.nb()
        ncol = nh * 64
        for k in range(nk):
            self.mm(self.bank(b)[:, 0:ncol], lhs_fn(k), w[:, k, col0:col0 + ncol], k == 0, k == nk - 1,
                    lkeys + wkeys, [self.pk(b)])
        s = vs.next()
        st = vs.ap(s)[:, 0:nh * 66].rearrange('p (h d) -> p h d', h=nh)
        self.ts('vector', st[:, :, 0:64], self.bank(b)[:, 0:ncol].rearrange('p (h d) -> p h d', h=nh),
                val, None, ALU.mult, None, [self.pk(b), valkey], [vs.key(s, 'v')])
        self.ts('vector', st[:, :, 64:66], self.onescol[:, 0:nh * 2].rearrange('p (h d) -> p h d', h=nh),
                val, None, ALU.mult, None, ['onescol', valkey], [vs.key(s, 'o')])
        self.dma(dst[:, tile_idx, hd0:hd0 + nh, :], st, [vs.key(s, 'v'), vs.key(s, 'o')],
                 [('scr', id(dst), tile_idx, hd0)], q='gpsimd')

    def phase_A(self, L, own):
        a = self.arena
        d = self.dr
        m0 = a.mark()
        ubase = 0 if own else OWN
        cs = self.cs
        hT = a.alloc(128, 8 * OWN, BF16).rearrange('p (c t) -> p c t', c=8)
        gbc = a.alloc(128, D, F32)
        self.dma(gbc, d['g_mix%d' % L], [], ['gmix'])
        m1 = a.mark()
        nbufs = self.norm_bufs()
        xt = Buf(a, 128, D, F32, 3)
        for ti in range(32):
            s = xt.next()
            r0 = self.xrow(ubase + ti * 128)
            self.dma(xt.ap(s), self.xsrc[r0:r0 + 128, :], [], [xt.key(s)])
            self.norm_transpose(xt.ap(s), xt.key(s), gbc, 'gmix', hT[:, :, ti * 128:(ti + 1) * 128],
                                ('hT', ti // 4, ti % 4), nbufs)
        self.p.phase_barrier()
        a.release(m1)
        hkeys = lambda tb: []
        wst = Buf(a, 128, 8 * 256, F32, 4)
        wbf = Buf(a, 128, 8 * 256, BF16, 4)
        wrs = Buf(a, 128, 2048, F32, 2)
        t = dict(raw=Buf(a, 128, 512, F32, 3), sq=Buf(a, 128, 512, BF16, 3), sd=Buf(a, 128, 512, F32, 2),
                 cn=Buf(a, 128, 512, BF16, 4), r1=Buf(a, 128, 512, F32, 2), r2=Buf(a, 128, 512, F32, 2),
                 vst=Buf(a, 128, 8 * 66, BF16, 3))
        ob = Buf(a, 128, 512, BF16, 4)
        tab = Buf(a, 128, 2 * 512, F32, 3)
        gq = a.alloc(128, 2, F32)
        gkv = a.alloc(128, 1, F32)
        self.dma(gq, d['g_q%d' % L], [], ['gq'])
        self.dma(gkv, d['g_kv%d' % L], [], ['gkv'])
        w_in = d['w_in%d' % L]
        w_sw = d['w_sw%d' % L]
        blocks = list(range(8))

        def proj_fm(b, w, wk, c0, m, tb):
            for c in range(8):
                self.mm(self.bank(b)[0:m], w[:, c, c0:c0 + m], hT[:, c, tb * 512:(tb + 1) * 512],
                        c == 0, c == 7, [wk] + hkeys(tb), [self.pk(b)])

        def load_tab(cname, sname, rows, u0, n=512):
            s = tab.next()
            self.dma(tab.ap(s)[0:rows, 0:n], d[cname + cs][:, u0:u0 + n], [], [tab.key(s, 'c')])
            self.dma(tab.ap(s)[0:rows, 512:512 + n], d[sname + cs][:, u0:u0 + n], [], [tab.key(s, 's')])
            return tab.ap(s)[0:rows, 0:n], tab.ap(s)[0:rows, 512:512 + n], [tab.key(s, 'c'), tab.key(s, 's')]

        groups = []
        if own:
            wuq, kuq = self.load_w_res(d['w_uq%d' % L], 2, 768, wrs)
            wuqs, kuqs = self.load_w_res(d['w_uq_sw%d' % L], 2, 768, wrs)

            def g_cq(ws, hook):
                (w, wk), = ws
                def s1(tb):
                    bs = []
                    for c in range(2):
                        b = self.nb()
                        proj_fm(b, w, wk, c * 128, 128, tb)
                        bs.append(b)
                    return self.fm_norm(bs, 2, gq, 'gq', 256, t)
                cns = {0: s1(0)}
                for tb in blocks:
                    if tb == 4:
                        hook()
                    if tb + 1 < 8:
                        cns[tb + 1] = s1(tb + 1)
                    cn = cns.pop(tb)
                    cosap, sinap, tk = load_tab('CA', 'SA', 96, tb * 512)
                    for h in range(8):
                        bA = self.nb()
                        bB = self.nb()
                        for (bb, ww, kk) in ((bA, wuq, kuq), (bB, wuqs, kuqs)):
                            for c in range(2):
                                self.mm(self.bank(bb)[0:96], ww[:, c, h * 96:(h + 1) * 96], t['cn'].ap(cn[c]),
                                        c == 0, c == 1, kk + [t['cn'].key(cn[c])], [self.pk(bb)])
                        o = self.rope(bA, bB, 96, cosap, sinap, tk, t, ob)
                        self.dma(d['S_qA'][h * 96:(h + 1) * 96, tb * 512:(tb + 1) * 512], ob.ap(o)[0:96],
                                 [ob.key(o)], [('sqa', h, tb)], q='gpsimd')
            groups.append(([(w_in[:, 0:256], 8, 256)], g_cq))
        wuk, kuk = self.load_w_res(d['w_uk%d' % L], 1, 512, wrs)
        wuv, kuv = self.load_w_res(d['w_uv%d' % L], 1, 512, wrs)

        def g_ckv(ws, hook):
            (w, wk), (wsw, wswk) = ws
            def s1(tb):
                b = self.nb()
                proj_fm(b, w, wk, 0, 128, tb)
                return self.fm_norm([b], 1, gkv, 'gkv', 128, t)
            cns = {0: s1(0)}
            for tb in blocks:
                if tb == 4:
                    hook()
                if tb + 1 < 8:
                    cns[tb + 1] = s1(tb + 1)
                u0 = ubase + tb * 512
                cn = cns.pop(tb)
                ckvn = t['cn'].ap(cn[0])
                ckey = t['cn'].key(cn[0])
                for ch in range(4):
                    b2 = self.nb()
                    self.mm(self.bank(b2), wuk[:, 0, ch * 128:(ch + 1) * 128], ckvn, True, True, kuk + [ckey], [self.pk(b2)])
                    o = ob.next()
                    self.copy('scalar' if ch % 2 else 'vector', ob.ap(o), self.bank(b2), [self.pk(b2)], [ob.key(o)])
                    self.dma(d['S_kA'][ch * 128:(ch + 1) * 128, u0:u0 + 512], ob.ap(o), [ob.key(o)], [('ska', ch, u0)], q='gpsimd')
                for sub in range(4):
                    self.vtok(lambda k, sub=sub, ckvn=ckvn: ckvn[:, sub * 128:(sub + 1) * 128], 1, wuv, kuv, 0, 8,
                              self.valOne[:, 0:1], 'valOne', d['S_vA'], u0 // 128 + sub, 0, t, [ckey])
                bA = self.nb()
                bB = self.nb()
                proj_fm(bA, w, wk, 128, 32, tb)
                proj_fm(bB, wsw, wswk, 0, 32, tb)
                cosap, sinap, tk = load_tab('Ck', 'Sk', 32, u0)
                o = self.rope(bA, bB, 32, cosap, sinap, tk, t, ob)
                self.dma(d['S_kr'][0:32, u0:u0 + 512], ob.ap(o)[0:32], [ob.key(o)], [('skr', u0)], q='gpsimd')
        if not self.swap:
            groups.append(([(w_in[:, 256:416], 8, 160), (w_sw[:, 0:32], 8, 32)], g_ckv))

        def extB(tb):
            if own:
                return 1024 + tb * 512
            return {0: 5120, 1: 5632, 6: 0, 7: 512}[tb]

        def extC(tb):
            if own:
                return 512 + tb * 512
            return {0: 4608, 7: 0}[tb]
        own_ext = lambda tb: tb * 512
        bB_blocks = blocks if own else [0, 1, 6, 7]
        bC_blocks = blocks if own else [0, 7]

        def rope_group(col0, sw0, dst, row0, blks, extf):
            def run(ws, hook):
                (w, wk), (ws_, wsk) = ws
                for bi_, tb in enumerate(blks):
                    if bi_ == len(blks) // 2:
                        hook()
                    u0 = ubase + tb * 512
                    cosap, sinap, tk = load_tab('cosB', 'sinB', 128, u0)
                    for c in range(2):
                        bA = self.nb()
                        bBk = self.nb()
                        proj_fm(bA, w, wk, c * 128, 128, tb)
                        proj_fm(bBk, ws_, wsk, c * 128, 128, tb)
                        o = self.rope(bA, bBk, 128, cosap, sinap, tk, t, ob)
                        e0 = extf(tb)
                        self.dma(dst[row0 + c * 128: row0 + (c + 1) * 128, e0:e0 + 512], ob.ap(o),
                                 [ob.key(o)], [('sr', id(dst), row0 + c, e0)], q='gpsimd')
            groups.append(([(w_in[:, col0:col0 + 256], 8, 256), (w_sw[:, sw0:sw0 + 256], 8, 256)], run))

        def plain_group(col0, ncols, dst, row0, blks, extf, func):
            def run(ws, hook):
                (w, wk), = ws
                for bi_, tb in enumerate(blks):
                    if bi_ == len(blks) // 2:
                        hook()
                    for c in range(ncols // 128):
                        b = self.nb()
                        proj_fm(b, w, wk, c * 128, 128, tb)
                        o = ob.next()
                        if func is None and c % 2 == 0:
                            self.copy('vector', ob.ap(o), self.bank(b), [self.pk(b)], [ob.key(o)])
                        else:
                            self.act(ob.ap(o), self.bank(b), AF.Copy if func is None else func, [self.pk(b)], [ob.key(o)])
                        e0 = extf(tb)
                        self.dma(dst[row0 + c * 128: row0 + (c + 1) * 128, e0:e0 + 512], ob.ap(o),
                                 [ob.key(o)], [('sp', id(dst), row0 + c, e0)], q='gpsimd')
            groups.append(([(w_in[:, col0:col0 + ncols], 8, ncols)], run))

        def vtok_group(col0, nh, dst, hd0, blks, extf, val, valkey, per_tile_val):
            def run(ws, hook):
                (w, wk), = ws
                for bi_, tb in enumerate(blks):
                    if bi_ == len(blks) // 2:
                        hook()
                    for sub in range(4):
                        et = extf(tb) // 128 + sub
                        v = val[:, et:et + 1] if per_tile_val else val[:, 0:1]
                        self.vtok(lambda k, tb=tb, sub=sub: hT[:, k, tb * 512 + sub * 128: tb * 512 + (sub + 1) * 128],
                                  8, w, [wk], 0, nh, v, valkey, dst, et, hd0, t, hkeys(tb))
            groups.append(([(w_in[:, col0:col0 + nh * 64], 8, nh * 64)], run))

        QB, KB, VB = 416, 1184, 1952
        QC, KC, VC, G0 = 2720, 3232, 3744, 4256
        if own:
            for g in range(3):
                rope_group(QB + g * 256, 32 + g * 256, d['S_qb'], g * 256, blocks, own_ext)
        for g in range(3):
            rope_group(KB + g * 256, 32 + 768 + g * 256, d['S_kb'], g * 256, bB_blocks, extB)
        for g in range(3):
            vtok_group(VB + g * 256, 4, d['S_vb'], g * 4, bB_blocks, extB, self.valB, 'valB', True)
        if own:
            for g in range(2):
                plain_group(QC + g * 256, 256, d['S_qc'], g * 256, blocks, own_ext, None)
        for g in range(2):
            plain_group(KC + g * 256, 256, d['S_kc'], g * 256, bC_blocks, extC, None)
        for g in range(2):
            vtok_group(VC + g * 256, 4, d['S_vc'], g * 4, bC_blocks, extC, self.valOne, 'valOne', False)
        if own:
            for g in range(12):
                plain_group(G0 + g * 256, 256, d['S_gate'], g * 256, blocks, own_ext, AF.Sigmoid)

        def do_loads(g):
            return [self.load_w(src, kc, n, wst, wbf) for (src, kc, n) in g[0]]
        cur = do_loads(groups[0])
        for i, g in enumerate(groups):
            box = {}

            def hook(i=i, box=box):
                if i + 1 < len(groups):
                    box['n'] = do_loads(groups[i + 1])
            g[1](cur, hook)
            cur = box.get('n')
        a.release(m0)
        self.p.phase_barrier()

    def attn_heads(self, heads, nslot=4, nkmax=S):
        a = self.arena
        Qb = Buf(a, 128, OWN, BF16, nslot)
        Kb = Buf(a, 128, nkmax, BF16, nslot)
        dk0 = heads[0]['dk']
        if dk0 == 64:
            for s_ in range(nslot):
                self.memset('vector', Qb.ap(s_)[64:128, :], 0.0, [Qb.key(s_, 'z')])
                self.memset('gpsimd', Kb.ap(s_)[64:128, :], 0.0, [Kb.key(s_, 'z')])
            self.p.phase_barrier()
        dkp = 128 if dk0 == 64 else dk0
        Vb = Buf(a, 128, (nkmax // 128) * 66, BF16, nslot)
        Pb = Buf(a, 128, 512, BF16, 7)
        Osb = Buf(a, 65, 512, F32, 2)
        r32 = Buf(a, 65, 512, F32, 2)
        rhi = Buf(a, 65, 512, BF16, 2)
        rlo = Buf(a, 65, 512, BF16, 2)
        yb = Buf(a, 64, 512, BF16, 3)
        LA = 4
        sbank = [0, 1, 2, 6, 7]
        ucount = 0
        qcount = 0
        if heads[0].get('pre'):
            heads[0]['pre']()
        for hi, hd in enumerate(heads):
            dk, scale = hd['dk'], hd['scale']
            loaded = []
            for src in hd['srcs']:
                sq, sk, sv = Qb.next(), Kb.next(), Vb.next()
                scr, r0 = src['Q']
                self.dma(Qb.ap(sq)[0:dk, :], scr[r0:r0 + dk, :], [], [Qb.key(sq)])
                nkeys = src['nkeys']
                for (scr, r0, rows, dst0) in src['K']:
                    self.dma(Kb.ap(sk)[dst0:dst0 + rows, 0:nkeys], scr[r0:r0 + rows, 0:nkeys], [], [Kb.key(sk, dst0)])
                kkeys = [Kb.key(sk, x[3]) for x in src['K']]
                scr, hidx = src['V']
                nkt = nkeys // 128
                vv = Vb.ap(sv)[:, 0:nkt * 66].rearrange('p (k d) -> p k d', d=66)
                self.dma(vv, scr[:, 0:nkt, hidx, :], [], [Vb.key(sv)])
                loaded.append(dict(Q=Qb.ap(sq), Qk=Qb.key(sq), K=Kb.ap(sk), Kk=kkeys, V=vv, Vk=Vb.key(sv)))
            if hi + 1 < len(heads) and heads[hi + 1].get('pre'):
                heads[hi + 1]['pre']()
            units = []
            for qb in range(NB):
                ul = hd['units'](qb)
                for i, un in enumerate(ul):
                    si, kt, mask, mkey = un[:4]
                    c0, c1 = (un[4], un[5]) if len(un) > 4 else (0, 512)
                    units.append((qb, si, kt, mask, mkey, i == 0, i == len(ul) - 1, c0, c1))
            n = len(units)
            pend = []
            pslots = {}

            def fin_pe(qb, ob_, so, sr):
                bc = 5
                self.mm(self.bank(bc)[0:64], self.ones[64:65, 0:64], rhi.ap(sr)[64:65, :], True, False,
                        ['ones', rhi.key(sr)], [self.pk(bc)])
                self.mm(self.bank(bc)[0:64], self.ones[64:65, 0:64], rlo.ap(sr)[64:65, :], False, True,
                        ['ones', rlo.key(sr)], [self.pk(bc)])
                sy = yb.next()
                self.tt('vector', yb.ap(sy), Osb.ap(so)[0:64], self.bank(bc)[0:64], ALU.mult,
                        [Osb.key(so), self.pk(bc)], [yb.key(sy)])
                scr, row0 = hd['out']
                self.dma(scr[row0:row0 + 64, qb * 512:(qb + 1) * 512], yb.ap(sy), [yb.key(sy)],
                         [('y', id(scr), row0, qb)], q='gpsimd')

            for u in range(n + LA):
                if u < n:
                    qb, si, kt, mask, mkey, first, last, c0, c1 = units[u]
                    L_ = loaded[si]
                    nq = c1 - c0
                    sb = sbank[ucount % 5]
                    sp = Pb.next()
                    pslots[u] = (sb, sp)
                    ucount += 1
                    self.mm(self.bank(sb)[:, 0:nq], L_['K'][0:dkp, kt * 128:(kt + 1) * 128],
                            L_['Q'][0:dkp, qb * 512 + c0:qb * 512 + c1], True, True,
                            L_['Kk'] + [L_['Qk']], [self.pk(sb)])
                    self.act(Pb.ap(sp)[:, 0:nq], self.bank(sb)[:, 0:nq], AF.Exp, [self.pk(sb)], [Pb.key(sp)], scale=scale)
                    if mask is not None:
                        self.tt('vector', Pb.ap(sp)[:, 0:nq], Pb.ap(sp)[:, 0:nq], mask[:, c0:c1], ALU.mult,
                                [Pb.key(sp)] + list(mkey or []), [Pb.key(sp)])
                if u >= LA:
                    v = u - LA
                    qb, si, kt, mask, mkey, first, last, c0, c1 = units[v]
                    L_ = loaded[si]
                    sb, sp = pslots.pop(v)
                    ob_ = 3 + (qb % 2)
                    self.mm(self.bank(ob_)[0:65, c0:c1], L_['V'][:, kt, 0:65], Pb.ap(sp)[:, 0:c1 - c0], first, last,
                            [L_['Vk'], Pb.key(sp)], [self.pk(ob_)], skip=True)
                    if last:
                        so = Osb.next()
                        sr = r32.next()
                        self.copy('vector', Osb.ap(so), self.bank(ob_)[0:65], [self.pk(ob_)], [Osb.key(so)])
                        self.act(r32.ap(sr)[64:65], Osb.ap(so)[64:65], AF.Ln, [Osb.key(so)], [r32.key(sr)])
                        self.act(r32.ap(sr)[64:65], r32.ap(sr)[64:65], AF.Exp, [r32.key(sr)], [r32.key(sr)], scale=-1.0)
                        self.copy('gpsimd', rhi.ap(sr)[64:65], r32.ap(sr)[64:65], [r32.key(sr)], [rhi.key(sr)])
                        self.tt('gpsimd', rlo.ap(sr)[64:65], r32.ap(sr)[64:65], rhi.ap(sr)[64:65], ALU.subtract,
                                [r32.key(sr), rhi.key(sr)], [rlo.key(sr)])
                        pend.append((u, qb, ob_, so, sr))
                while pend and (u - pend[0][0] >= 3 or u == n + LA - 1):
                    _, qb, ob_, so, sr = pend.pop(0)
                    fin_pe(qb, ob_, so, sr)

    def phase_M(self, L):
        a = self.arena
        d = self.dr
        m0 = a.mark()
        heads = []
        for h in range(8):
            src = dict(Q=(d['S_qA'], h * 96), K=[(d['S_kA'], h * 64, 64, 0), (d['S_kr'], 0, 32, 64)],
                       V=(d['S_vA'], h), nkeys=S)
            heads.append(dict(dk=96, scale=96 ** -0.5, srcs=[src], out=(d['S_ya'], h * 64),
                              units=lambda qb: [(0, kt, None, None) for kt in range(64)]))
        self.attn_heads(heads)
        a.release(m0)
        self.p.phase_barrier()

    def load_masks(self, name, n):
        a = self.arena
        d = self.dr
        mk = a.alloc(128, n * 512, BF16).rearrange('p (n f) -> p n f', n=n)
        m = a.mark()
        st = Buf(a, 128, 4 * 512, F32, 2)
        i = 0
        while i < n:
            k = min(4, n - i)
            s = st.next()
            self.dma(st.ap(s)[:, 0:k * 512].rearrange('p (n f) -> p n f', n=k),
                     d[name][i:i + k].rearrange('n p f -> p n f'), [], [st.key(s)])
            self.copy('scalar', mk[:, i:i + k, :], st.ap(s)[:, 0:k * 512].rearrange('p (n f) -> p n f', n=k),
                      [st.key(s)], [(name, i)])
            i += k
        self.p.phase_barrier()
        a.release(m)
        return mk

    def phase_B(self, L):
        a = self.arena
        d = self.dr
        m0 = a.mark()
        mk = self.load_masks('Bmask', 34)
        rels = [list(range(-1, 5)), list(range(-2, 6)), list(range(-8, 12))]
        offs = [0, 6, 14]
        heads = []
        for j in range(4):
            srcs = []
            for g in range(3):
                hh = g * 4 + j
                srcs.append(dict(Q=(d['S_qb'], hh * 64), K=[(d['S_kb'], hh * 64, 64, 0)], V=(d['S_vb'], hh), nkeys=EXTB))

            def units(qb, rels=rels, offs=offs):
                ul = []
                for g in (2, 1, 0):
                    reach = 64 * (1, 4, 16)[g]
                    for ri, rel in enumerate(rels[g]):
                        c0 = max(0, 128 * rel - reach)
                        c1 = min(512, 128 * rel + 128 + reach)
                        ul.append((g, 8 + 4 * qb + rel, mk[:, offs[g] + ri, :], None, c0, c1))
                ul.sort(key=lambda x: 0 if (x[4] == 0 and x[5] == 512) else 1)
                return ul
            heads.append(dict(dk=64, scale=0.125, srcs=srcs, out=(d['S_yb'], j * 64), units=units))
        self.attn_heads(heads, nslot=4, nkmax=EXTB)
        a.release(m0)
        self.p.phase_barrier()

    def phase_C(self, L):
        a = self.arena
        d = self.dr
        m0 = a.mark()
        nav = self.load_masks('NAvalid' + self.cs, 24)
        NS = 24 * 64
        ebst = Buf(a, 128, 2 * NS, F32, 2)
        EF = Buf(a, 128, 2 * NS, BF16, 2)
        T = Buf(a, 128, 16 * 512, BF16, 2)
        heads = []
        for h in range(8):
            state = {}

            def pre(h=h, state=state):
                s = ebst.next()
                self.dma(ebst.ap(s)[:, 0:NS], d['ebias%d' % L][h], [], [ebst.key(s, 'f')])
                self.dma(ebst.ap(s)[:, NS:2 * NS], d['ebiasz%d' % L][h], [], [ebst.key(s, 'z')])
                se = EF.next()
                self.act(EF.ap(se), ebst.ap(s), AF.Exp, [ebst.key(s, 'f'), ebst.key(s, 'z')], [EF.key(se)])
                ef = EF.ap(se)[:, 0:NS].rearrange('p (s c) -> p s c', c=64)
                st_ = T.next()
                tt_ = T.ap(st_).rearrange('p (n f) -> p n f', n=16)
                for e_, ty in enumerate((0, 2)):
                    for j in range(8):
                        w0 = 15 - 2 * j
                        self.tt('gpsimd', tt_[:, e_ * 8 + j, :].rearrange('p (r c) -> p r c', c=64),
                                ef[:, w0:w0 + 8, :],
                                nav[:, ty * 8 + j, :].rearrange('p (r c) -> p r c', c=64),
                                ALU.mult, [EF.key(se)], [T.key(st_, (e_, j))])
                state['T'] = tt_
                state['k'] = st_
                state['EZ'] = EF.ap(se)[:, NS:2 * NS]
                state['ek'] = EF.key(se)

            def units(qb, state=state):
                ul = []
                for j in range(8):
                    w0 = 15 - 2 * j
                    if qb == 0 or qb == 7:
                        e_ = 0 if qb == 0 else 1
                        ul.append((0, 4 * qb + 2 + j, state['T'][:, e_ * 8 + j, :], [T.key(state['k'], (e_, j))]))
                    else:
                        r0 = max(0, 2 * j - 7)
                        r1 = min(7, 2 * j + 1)
                        ul.append((0, 4 * qb + 2 + j, state['EZ'][:, w0 * 64:(w0 + 8) * 64], [state['ek']],
                                   r0 * 64, (r1 + 1) * 64))
                ul.sort(key=lambda x: 0 if (len(x) < 5 or (x[4] == 0 and x[5] == 512)) else 1)
                return ul
            src = dict(Q=(d['S_qc'], h * 64), K=[(d['S_kc'], h * 64, 64, 0)], V=(d['S_vc'], h), nkeys=EXTC)
            heads.append(dict(dk=64, scale=0.125, srcs=[src], out=(d['S_yc'], h * 64), units=units, pre=pre,
                              tkeys=state))
        self.attn_heads(heads, nslot=3, nkmax=EXTC)
        a.release(m0)
        self.p.phase_barrier()

    def phase_G(self, L):
        a = self.arena
        d = self.dr
        self.h2T = a.alloc(128, 8 * OWN, BF16).rearrange('p (c t) -> p c t', c=8)
        self.mG = a.mark()
        wst = Buf(a, 128, 2048, F32, 2)
        wpa, kpa = self.load_w_res(d['w_pa%d' % L], 4, 1024, wst)
        wpb, kpb = self.load_w_res(d['w_pb%d' % L], 2, 1024, wst)
        wpc, kpc = self.load_w_res(d['w_pc%d' % L], 4, 1024, wst)
        wo, ko = self.load_w_res(d['w_o%d' % L], 8, 1024, wst)
        gbc = a.alloc(128, D, F32)
        self.dma(gbc, d['g_ffn%d' % L], [], ['gffn'])
        ya = Buf(a, 128, 4 * 512, BF16, 2)
        yb = Buf(a, 128, 2 * 512, BF16, 2)
        yc = Buf(a, 128, 4 * 512, BF16, 2)
        gt = Buf(a, 128, 3 * 512, BF16, 3)
        mg = Buf(a, 128, 8 * 512, BF16, 2)
        tf = Buf(a, 128, 512, F32, 5)
        xt = Buf(a, 128, D, F32, 3)
        nbufs = self.norm_bufs()
        def stage1(tb):
            t0 = tb * 512
            sa, sb_, sc = ya.next(), yb.next(), yc.next()
            self.dma(ya.ap(sa).rearrange('p (c t) -> p c t', c=4),
                     d['S_ya'][:, t0:t0 + 512].rearrange('(c p) t -> p c t', p=128), [], [ya.key(sa)])
            self.dma(yb.ap(sb_).rearrange('p (c t) -> p c t', c=2),
                     d['S_yb'][:, t0:t0 + 512].rearrange('(c p) t -> p c t', p=128), [], [yb.key(sb_)])
            self.dma(yc.ap(sc).rearrange('p (c t) -> p c t', c=4),
                     d['S_yc'][:, t0:t0 + 512].rearrange('(c p) t -> p c t', p=128), [], [yc.key(sc)])
            yA = ya.ap(sa).rearrange('p (c t) -> p c t', c=4)
            yB = yb.ap(sb_).rearrange('p (c t) -> p c t', c=2)
            yC = yc.ap(sc).rearrange('p (c t) -> p c t', c=4)
            sm = mg.next()
            mgv = mg.ap(sm).rearrange('p (c t) -> p c t', c=8)
            for oc in range(8):
                sg = gt.next()
                gv = gt.ap(sg).rearrange('p (c t) -> p c t', c=3)
                self.dma(gv, d['S_gate'].rearrange('(b c p) t -> c p b t', b=3, p=128)[oc][:, :, t0:t0 + 512],
                         [], [gt.key(sg)])
                ts_ = []
                for (y, yk, w, wk, nk, bi) in ((yA, ya.key(sa), wpa, kpa, 4, 0), (yB, yb.key(sb_), wpb, kpb, 2, 1),
                                               (yC, yc.key(sc), wpc, kpc, 4, 2)):
                    b = self.nb()
                    for c in range(nk):
                        self.mm(self.bank(b), w[:, c, oc * 128:(oc + 1) * 128], y[:, c, :], c == 0, c == nk - 1,
                                wk + [yk], [self.pk(b)])
                    s = tf.next()
                    self.tt('vector', tf.ap(s), self.bank(b), gv[:, bi, :], ALU.mult, [self.pk(b), gt.key(sg)], [tf.key(s)])
                    ts_.append(s)
                s4 = tf.next()
                self.tt('gpsimd', tf.ap(s4), tf.ap(ts_[0]), tf.ap(ts_[1]), ALU.add, [tf.key(ts_[0]), tf.key(ts_[1])], [tf.key(s4)])
                self.tt('gpsimd', mgv[:, oc, :], tf.ap(s4), tf.ap(ts_[2]), ALU.add, [tf.key(s4), tf.key(ts_[2])],
                        [mg.key(sm, oc)])
            mkeys = [mg.key(sm, oc) for oc in range(8)]
            return mgv, mkeys

        def stage2(tb, mgv, mkeys):
            t0 = tb * 512
            late = None
            for sub in range(4):
                sx = xt.next()
                r0 = t0 + sub * 128
                rx = self.xrow(r0)
                self.dma(xt.ap(sx), self.xsrc[rx:rx + 128, :], [], [xt.key(sx)])
                for half in range(2):
                    b = self.nb()
                    for c in range(8):
                        self.mm(self.bank(b), mgv[:, c, sub * 128:(sub + 1) * 128], wo[:, c, half * 512:(half + 1) * 512],
                                c == 0, c == 7, mkeys + ko, [self.pk(b)])
                    self.tt('vector', xt.ap(sx)[:, half * 512:(half + 1) * 512], xt.ap(sx)[:, half * 512:(half + 1) * 512],
                            self.bank(b), ALU.add, [xt.key(sx), self.pk(b)], [xt.key(sx)])
                self.dma(d['S_x1'][r0:r0 + 128, :], xt.ap(sx), [xt.key(sx)], [('x1', r0)], q='gpsimd')
                ti = tb * 4 + sub
                nxt_late = self.norm_transpose(xt.ap(sx), xt.key(sx), gbc, 'gffn',
                                               self.h2T[:, :, ti * 128:(ti + 1) * 128], ('h2T', tb, sub), nbufs,
                                               defer=True)
                if late is not None:
                    late()
                late = nxt_late
            late()
        pend = {0: stage1(0)}
        for tb in range(NB):
            if tb + 1 < NB:
                pend[tb + 1] = stage1(tb + 1)
            stage2(tb, *pend.pop(tb))
        a.release(self.mG)
        self.p.phase_barrier()

    def phase_F(self, L, final):
        a = self.arena
        d = self.dr
        h2T = self.h2T
        m1 = a.mark()
        wst = Buf(a, 128, 8 * 256, F32, 4)
        wbf = Buf(a, 128, 8 * 256, BF16, 4)
        sgb = Buf(a, 128, 512, F32, 3)
        ob = Buf(a, 128, 512, BF16, 4)
        cols = list(range(0, DFF, 256))

        def f_loads(col):
            return (self.load_w(d['w1%d' % L][:, col:col + 256], 8, 256, wst, wbf),
                    self.load_w(d['w3%d' % L][:, col:col + 256], 8, 256, wst, wbf))
        cur = f_loads(cols[0])
        for gi, col in enumerate(cols):
            nxt = f_loads(cols[gi + 1]) if gi + 1 < len(cols) else None
            (w1, k1), (w3, k3) = cur
            for tb in range(NB):
                for ch in range(2):
                    b1 = self.nb()
                    b3 = self.nb()
                    for (bb, ww, kk) in ((b1, w1, k1), (b3, w3, k3)):
                        for c in range(8):
                            self.mm(self.bank(bb), ww[:, c, ch * 128:(ch + 1) * 128], h2T[:, c, tb * 512:(tb + 1) * 512],
                                    c == 0, c == 7, [kk], [self.pk(bb)])
                    s = sgb.next()
                    self.act(sgb.ap(s), self.bank(b1), AF.Silu, [self.pk(b1)], [sgb.key(s)])
                    o = ob.next()
                    self.tt('vector', ob.ap(o), sgb.ap(s), self.bank(b3), ALU.mult, [sgb.key(s), self.pk(b3)], [ob.key(o)])
                    f = col // 128 + ch
                    self.dma(d['S_act'][f * 128:(f + 1) * 128, tb * 512:(tb + 1) * 512], ob.ap(o), [ob.key(o)],
                             [('sact', f, tb)], q='gpsimd')
            cur = nxt
        a.release(m1)
        self.p.phase_barrier()
        a.release(self.mG)
        a.off = 0 + self._const_end
        wst = Buf(a, 128, 4096, F32, 2)
        w2, k2 = self.load_w_res(d['w2%d' % L], 22, 1024, wst)
        gfin = a.alloc(128, D, F32)
        self.dma(gfin, d['g_final'], [], ['gfin'])
        at = Buf(a, 128, 22 * 512, BF16, 2)
        xt = Buf(a, 128, D, F32, 3)
        yn = Buf(a, 128, D, F32, 2)
        junk = Buf(a, 128, D, BF16, 1)
        ss = Buf(a, 128, 4, F32, 4)
        for tb in range(NB):
            t0 = tb * 512
            sa = at.next()
            av = at.ap(sa).rearrange('p (f t) -> p f t', f=22)
            self.dma(av, d['S_act'][:, t0:t0 + 512].rearrange('(f p) t -> p f t', p=128), [], [at.key(sa)])
            for sub in range(4):
                r0 = t0 + sub * 128
                sx = xt.next()
                self.dma(xt.ap(sx), d['S_x1'][r0:r0 + 128, :], [], [xt.key(sx)])
                for half in range(2):
                    b = self.nb()
                    for f in range(22):
                        self.mm(self.bank(b), av[:, f, sub * 128:(sub + 1) * 128], w2[:, f, half * 512:(half + 1) * 512],
                                f == 0, f == 21, [at.key(sa)] + k2, [self.pk(b)])
                    self.tt('vector', xt.ap(sx)[:, half * 512:(half + 1) * 512], xt.ap(sx)[:, half * 512:(half + 1) * 512],
                            self.bank(b), ALU.add, [xt.key(sx), self.pk(b)], [xt.key(sx)])
                self.dma(self.out_raw[r0:r0 + 128, :], xt.ap(sx), [xt.key(sx)], [('yraw', r0)], q='gpsimd')
                if final:
                    sj = junk.next()
                    s1 = ss.next()
                    self.act(junk.ap(sj), xt.ap(sx), AF.Square, [xt.key(sx)], [junk.key(sj), ss.key(s1, 'a')],
                             accum=ss.ap(s1)[:, 0:1])
                    self.act(ss.ap(s1)[:, 1:2], ss.ap(s1)[:, 0:1], AF.Sqrt, [ss.key(s1, 'a'), 'eps'], [ss.key(s1, 'b')],
                             scale=1.0 / D, bias=self.epsA[:, 0:1])
                    self.recip(ss.ap(s1)[:, 2:3], ss.ap(s1)[:, 1:2], [ss.key(s1, 'b')], [ss.key(s1, 'c')])
                    sy = yn.next()
                    self.stt(yn.ap(sy), xt.ap(sx), ss.ap(s1)[:, 2:3], gfin, ALU.mult, ALU.mult,
                             [xt.key(sx), ss.key(s1, 'c'), 'gfin'], [yn.key(sy)])
                    self.dma(d['y_norm'][r0:r0 + 128, :], yn.ap(sy), [yn.key(sy)], [('ynorm', r0)], q='gpsimd')
        a.off = self._const_end
        self.p.phase_barrier()


W_SHAPES = dict(w_in=(1024, IN_COLS), w_sw=(1024, 1568), w_uq=(256, 768), w_uq_sw=(256, 768), w_uk=(128, 512),
                w_uv=(128, 512), g_mix=(128, D), g_q=(128, 2), g_kv=(128, 1), ebias=(8, 128, 24 * 64), ebiasz=(8, 128, 24 * 64),
                w_pa=(512, D), w_pb=(256, D), w_pc=(512, D), w_o=(D, D), g_ffn=(128, D), w1=(D, DFF), w3=(D, DFF),
                w2=(DFF, D))
C_SHAPES = dict(ident=(128, 128), valB=(128, 48), CA=(96, OWN), SA=(96, OWN), Ck=(32, S), Sk=(32, S),
                cosB=(128, S), sinB=(128, S), Bmask=(34, 128, 512), NAvalid=(24, 128, 512), g_final=(128, D))
SCR = dict(S_qA=((768, OWN), BF16), S_kA=((512, S), BF16), S_kr=((32, S), BF16), S_vA=((128, 64, 8, 66), BF16),
           S_qb=((768, OWN), BF16), S_kb=((768, EXTB), BF16), S_vb=((128, 48, 12, 66), BF16),
           S_qc=((512, OWN), BF16), S_kc=((512, EXTC), BF16), S_vc=((128, 40, 8, 66), BF16),
           S_gate=((3072, OWN), BF16), S_ya=((512, OWN), BF16), S_yb=((256, OWN), BF16), S_yc=((512, OWN), BF16),
           S_x1=((OWN, D), F32), S_act=((DFF, OWN), BF16))


PC_NAMES = ('valB', 'CA', 'SA', 'Ck', 'Sk', 'cosB', 'sinB', 'NAvalid')


def build_nc(fused=True, dbg=None, upto=99):
    nc = bass.Bass("TRN2", target_bir_lowering=False)
    B = Layer(nc, dbg)
    B.dram_in('xs', (S, D))
    sets = ('', '_p') if fused else ('',)
    for k, shp in C_SHAPES.items():
        if k in PC_NAMES:
            for cs in sets:
                B.dram_in(k + cs, shp)
        else:
            B.dram_in(k, shp)
    for l in range(2 if fused else 1):
        for k, shp in W_SHAPES.items():
            B.dram_in(k + str(l), shp)
    B.dram_out('y_raw', (OWN, D))
    B.dram_out('y_norm', (OWN, D))
    for k, (shp, dt) in SCR.items():
        B.dram_scr(k, shp, dt)
    B.setup_consts()
    B._const_end = B.arena.off
    B.p.phase_barrier()

    def run_pass(L, cs, xsrc, swap, out_raw, final):
        B.begin_pass(L, cs, xsrc, swap, out_raw, final)
        phs = [lambda: B.phase_A(L, True), lambda: B.phase_A(L, False), lambda: B.phase_M(L), lambda: B.phase_B(L),
               lambda: B.phase_C(L), lambda: B.phase_G(L), lambda: B.phase_F(L, final)]
        for ph in phs[:upto]:
            ph()
    if fused:
        X1 = B.dram_scr('X1', (S, D), F32)
        run_pass(0, '', B.dr['xs'], False, X1[0:OWN], False)
        run_pass(0, '_p', B.dr['xs'], True, X1[OWN:S], False)
        run_pass(1, '', X1, False, B.dr['y_raw'], True)
    else:
        run_pass(0, '', B.dr['xs'], False, B.dr['y_raw'], True)
    B.p.emit()
    return nc, B


def _rope_tabs(pos, dim):
    inv = np.power(np.float32(10000.0), -np.arange(0, dim, 2, dtype=np.float32) / np.float32(dim)).astype(np.float32)
    ang = pos.astype(np.float32)[:, None] * inv[None, :]
    return np.cos(ang).astype(np.float32), np.sin(ang).astype(np.float32)


def core_consts(h):
    perm = _perm_local(h)
    c = {}
    c['ident'] = np.eye(128, dtype=np.float32)
    e = np.arange(EXTB)
    t = h * OWN - 1024 + e
    valid = ((t >= 0) & (t < S)).astype(np.float32)
    c['valB'] = np.ascontiguousarray(valid.reshape(48, 128).T)
    cb, sb = _rope_tabs(perm, 64)
    f = np.arange(128)
    sign = np.where((f % 64) < 32, -1.0, 1.0).astype(np.float32)
    c['cosB'] = np.ascontiguousarray(cb[:, f % 32].T)
    c['sinB'] = np.ascontiguousarray((sb[:, f % 32] * sign[None, :]).T)
    ca, sa = _rope_tabs(perm, 32)
    j = np.arange(32)
    sgn = np.where(j < 16, -1.0, 1.0).astype(np.float32)
    c['Ck'] = np.ascontiguousarray(ca[:, j % 16].T)
    c['Sk'] = np.ascontiguousarray((sa[:, j % 16] * sgn[None, :]).T)
    CA = np.ones((96, OWN), np.float32)
    SA = np.zeros((96, OWN), np.float32)
    CA[64:96] = c['Ck'][:, :OWN]
    SA[64:96] = c['Sk'][:, :OWN]
    c['CA'], c['SA'] = CA, SA
    masks = []
    ii = np.arange(128)[:, None]
    jj = np.arange(512)[None, :]
    for dil, rels in ((1, range(-1, 5)), (4, range(-2, 6)), (16, range(-8, 12))):
        for rel in rels:
            dlt = 128 * rel + ii - jj
            masks.append(((dlt % dil == 0) & (np.abs(dlt) <= 64 * dil)).astype(np.float32))
    c['Bmask'] = np.stack(masks)
    nav = np.zeros((24, 128, 512), np.float32)
    for ty, i in enumerate((0, 1, 7)):
        for jx in range(8):
            for a_ in range(2):
                for rho in range(8):
                    Rr = 64 * h + 8 * i + rho
                    KR = 64 * h + 8 * i - 4 + 2 * jx + a_
                    rs = min(max(Rr - 4, 0), 120)
                    ok = (0 <= KR < 128) and (rs <= KR < rs + 8)
                    if ok:
                        nav[ty * 8 + jx, a_ * 64:(a_ + 1) * 64, rho * 64:(rho + 1) * 64] = 1.0
    c['NAvalid'] = nav
    return c


def layer_weights(inp, l):
    w = {}
    w_in = np.asarray(inp['w_in'][l], np.float32)
    w['w_in'] = w_in

    def swap_heads(cols, nh, hd):
        half = hd // 2
        parts = []
        for hh in range(nh):
            b = hh * hd
            parts += [cols[:, b + half:b + hd], cols[:, b:b + half]]
        return np.concatenate(parts, axis=1)
    w['w_sw'] = np.ascontiguousarray(np.concatenate([swap_heads(w_in[:, 384:416], 1, 32), swap_heads(w_in[:, 416:1184], 12, 64),
                                                     swap_heads(w_in[:, 1184:1952], 12, 64)], axis=1))
    wuq = np.asarray(inp['w_uq'][l], np.float32)
    w['w_uq'] = wuq
    sw = wuq.copy()
    for hh in range(8):
        b = hh * 96 + 64
        sw[:, b:b + 16] = wuq[:, b + 16:b + 32]
        sw[:, b + 16:b + 32] = wuq[:, b:b + 16]
    w['w_uq_sw'] = sw
    wukv = np.asarray(inp['w_ukv'][l], np.float32).reshape(128, 8, 128)
    w['w_uk'] = np.ascontiguousarray(wukv[:, :, :64].reshape(128, 512))
    w['w_uv'] = np.ascontiguousarray(wukv[:, :, 64:].reshape(128, 512))
    w['g_mix'] = np.ascontiguousarray(np.broadcast_to(np.asarray(inp['g_mix'][l], np.float32)[None, :], (128, D)))
    w['g_ffn'] = np.ascontiguousarray(np.broadcast_to(np.asarray(inp['g_ffn'][l], np.float32)[None, :], (128, D)))
    w['g_q'] = np.ascontiguousarray(np.asarray(inp['g_q'][l], np.float32).reshape(2, 128).T)
    w['g_kv'] = np.ascontiguousarray(np.asarray(inp['g_kv'][l], np.float32).reshape(128, 1))
    rpb = np.asarray(inp['rpb'][l], np.float32)
    kc = np.arange(64)[:, None]
    cc = np.arange(64)[None, :]
    cs = np.clip(cc - 8, 0, 48)
    colok = (kc >= cs) & (kc < cs + 16)
    dc = np.clip(kc - cc + 15, 0, 30)
    NEG = np.float32(-30000.0)
    ebf = np.full((8, 64, 23, 64), NEG, np.float32)
    ebz = np.full((8, 64, 23, 64), NEG, np.float32)
    for slot in range(23):
        dr = 18 - slot
        if 0 <= dr <= 14:
            vals = np.where(colok[None], rpb[:, dr][:, dc], NEG)
            ebf[:, :, slot, :] = vals
            if 3 <= dr <= 10:
                ebz[:, :, slot, :] = vals

    def shifted(e):
        pad = np.full((8, 64, 1, 64), NEG, np.float32)
        lo = np.concatenate([e, pad], axis=2)
        hi = np.concatenate([pad, e], axis=2)
        return np.ascontiguousarray(np.concatenate([lo, hi], axis=1).reshape(8, 128, 24 * 64))
    w['ebias'] = shifted(ebf)
    w['ebiasz'] = shifted(ebz)
    for k in ('w_pa', 'w_pb', 'w_pc', 'w_o', 'w1', 'w3', 'w2'):
        w[k] = np.ascontiguousarray(np.asarray(inp[k][l], np.float32))
    return w


_NC_CACHE = {}


def kernel(**inputs):
    x = np.asarray(inputs['x'], np.float32)
    gfin_bc = np.ascontiguousarray(np.broadcast_to(np.asarray(inputs['g_final'], np.float32)[None, :], (128, D)))
    consts = [core_consts(0), core_consts(1)]
    if 'f' not in _NC_CACHE:
        _NC_CACHE['f'] = build_nc(True)
    nc, B = _NC_CACHE['f']
    ws = [layer_weights(inputs, l) for l in range(2)]
    in_maps = []
    for c in range(8):
        b, h = c // 2, c % 2
        m = {'xs': np.ascontiguousarray(x[b][_perm_local(h)]), 'g_final': gfin_bc}
        for k, v in consts[h].items():
            m[k] = v
        for k in PC_NAMES:
            m[k + '_p'] = consts[1 - h][k]
        for l in range(2):
            for k, v in ws[l].items():
                m[k + str(l)] = v
        in_maps.append(m)
    res = run_bass_kernel_spmd(nc, in_maps, core_ids=list(range(8)))
    out = np.empty_like(x)
    for c in range(8):
        b, h = c // 2, c % 2
        out[b, h * OWN:(h + 1) * OWN] = res.results[c]['y_norm']
    return out
```

```python
import numpy as np
import concourse.bass as bass
import concourse.mybir as mybir
from concourse.bass_utils import run_bass_kernel_spmd

F32 = mybir.dt.float32
BF16 = mybir.dt.bfloat16
U8 = mybir.dt.uint8
AF = mybir.ActivationFunctionType
ALU = mybir.AluOpType

D = 1024
S = 8192
OWN = 4096
NB = 8
EPS = 1e-6
IN_COLS = 7328
DFF = 2816
EXTB = 6144
EXTC = 5120
ISZ = {F32: 4, BF16: 2, U8: 1}
ENGS = ['tensor', 'vector', 'scalar', 'gpsimd', 'sync']
CH = 16000
DMAK = 12


class Prog:
    def __init__(self, nc):
        self.nc = nc
        self.ops = []
        self.lastw = {}
        self.readers = {}
        self.barrier = {e: None for e in ENGS}

    def add(self, eng, fn, reads=(), writes=(), dma=False):
        i = len(self.ops)
        deps = set()
        for r in reads:
            j = self.lastw.get(r)
            if j is not None:
                deps.add(j)
        for w in writes:
            j = self.lastw.get(w)
            if j is not None:
                deps.add(j)
            deps.update(self.readers.get(w, ()))
        if self.barrier[eng] is not None:
            deps.update(self.barrier[eng])
            self.barrier[eng] = None
        for r in reads:
            self.readers.setdefault(r, []).append(i)
        for w in writes:
            self.lastw[w] = i
            self.readers[w] = []
        self.ops.append(dict(eng=eng, fn=fn, deps=deps, dma=dma))
        return i

    def phase_barrier(self):
        last = set()
        seen_c = set()
        seen_d = {}
        for i in range(len(self.ops) - 1, -1, -1):
            o = self.ops[i]
            if o['dma']:
                c = seen_d.get(o['eng'], 0)
                if c < DMAK:
                    last.add(i)
                    seen_d[o['eng']] = c + 1
            elif o['eng'] not in seen_c:
                seen_c.add(o['eng'])
                last.add(i)
            if len(seen_c) >= 4 and all(seen_d.get(e, 0) >= DMAK for e in ('sync', 'gpsimd')):
                break
        for e in ENGS:
            self.barrier[e] = set(last)
        self.lastw = {}
        self.readers = {}

    def emit(self):
        nc = self.nc
        ops = self.ops
        n = len(ops)
        needed = [False] * n
        for o in ops:
            for j in o['deps']:
                oj = ops[j]
                if oj['eng'] == 'tensor' and o['eng'] == 'tensor' and not oj['dma'] and not o['dma']:
                    continue
                needed[j] = True
        sig = [None] * n
        ccount = {e: 0 for e in ENGS}
        dcount = {e: 0 for e in ENGS}
        csems = {e: [] for e in ENGS}
        dsems = {e: [] for e in ENGS}
        dprev = [None] * n
        for i, o in enumerate(ops):
            e = o['eng']
            if o['dma']:
                k = dcount[e]
                dcount[e] += 1
                slot = k % DMAK
                if slot >= len(dsems[e]):
                    dsems[e].append(nc.alloc_semaphore('d_%s_%d' % (e, slot)))
                val = 16 * (k // DMAK + 1)
                sig[i] = (dsems[e][slot], val)
                if val > 16:
                    dprev[i] = (dsems[e][slot], val - 16)
            elif needed[i]:
                k = ccount[e]
                ccount[e] += 1
                si = k // CH
                if si >= len(csems[e]):
                    csems[e].append(nc.alloc_semaphore('c_%s_%d' % (e, si)))
                sig[i] = (csems[e][si], k % CH + 1)
        per = {e: [] for e in ENGS}
        for i, o in enumerate(ops):
            per[o['eng']].append(i)
        finals = []
        for e in ENGS:
            k = dcount[e]
            for slot in range(min(k, DMAK)):
                cnt = (k - 1 - slot) // DMAK + 1
                finals.append((dsems[e][slot], 16 * cnt))

        def run(eng_name, e):
            waited = {}

            def wait(sem, val):
                key = sem.num
                if waited.get(key, 0) < val:
                    e.wait_ge(sem, val)
                    waited[key] = val
            for i in per[eng_name]:
                o = ops[i]
                for j in sorted(o['deps']):
                    if sig[j] is None:
                        continue
                    oj = ops[j]
                    if oj['eng'] == 'tensor' and eng_name == 'tensor' and not oj['dma'] and not o['dma']:
                        continue
                    wait(*sig[j])
                if dprev[i] is not None:
                    wait(*dprev[i])
                ins = o['fn'](e)
                if sig[i] is not None:
                    ins.then_inc(sig[i][0], 16 if o['dma'] else 1)
            if eng_name == 'sync':
                for sem, val in finals:
                    wait(sem, val)

        with nc.Block() as block:
            @block.tensor
            def _(e):
                run('tensor', e)

            @block.vector
            def _(e):
                run('vector', e)

            @block.scalar
            def _(e):
                run('scalar', e)

            @block.gpsimd
            def _(e):
                run('gpsimd', e)

            @block.sync
            def _(e):
                run('sync', e)


class Arena:
    def __init__(self, nc, nbytes):
        self.t = nc.alloc_sbuf_tensor('arena', [128, nbytes], U8)
        self.cap = nbytes
        self.off = 0
        self.n = 0

    def alloc(self, parts, elems, dt, p0=0):
        size = elems * ISZ[dt]
        size = (size + 63) // 64 * 64
        assert self.off + size <= self.cap, ('SBUF overflow', self.off, size)
        ap = self.t[p0:p0 + parts, self.off:self.off + elems * ISZ[dt]].bitcast(dt)
        self.off += size
        self.n += 1
        return ap

    def mark(self):
        return self.off

    def release(self, m):
        self.off = m


class Buf:
    _cnt = [0]

    def __init__(self, arena, parts, elems, dt, nslot=1, p0=0):
        Buf._cnt[0] += 1
        self.id = Buf._cnt[0]
        self.aps = [arena.alloc(parts, elems, dt, p0) for _ in range(nslot)]
        self.nslot = nslot
        self.i = -1

    def next(self):
        self.i += 1
        return self.i % self.nslot

    def ap(self, s):
        return self.aps[s]

    def key(self, s, sub=None):
        return ('b', self.id, s, sub)


class Builder:
    def __init__(self, nc, dbg=None):
        self.nc = nc
        self.p = Prog(nc)
        self.arena = Arena(nc, 206 * 1024)
        self.ps = nc.alloc_psum_tensor('ps', [128, 4096], F32)
        self.psi = -1
        self.dr = {}
        self.dbg = dbg
        self.did = 0

    def dram_in(self, name, shape, dt=F32):
        t = self.nc.dram_tensor(name, list(shape), dt, kind="ExternalInput").ap()
        self.dr[name] = t
        return t

    def dram_out(self, name, shape, dt=F32):
        t = self.nc.dram_tensor(name, list(shape), dt, kind="ExternalOutput").ap()
        self.dr[name] = t
        return t

    def dram_scr(self, name, shape, dt=BF16):
        kind = "ExternalOutput" if (self.dbg and name in self.dbg) else "Internal"
        t = self.nc.dram_tensor(name, list(shape), dt, kind=kind).ap()
        self.dr[name] = t
        return t

    def bank(self, b):
        return self.ps[:, b * 512:(b + 1) * 512]

    def pk(self, b):
        return ('ps', b)

    def dma(self, out, in_, reads, writes, q='sync'):
        self.p.add(q, lambda e, o=out, i=in_: e.dma_start(out=o, in_=i), reads, writes, dma=True)

    def mm(self, out, lhsT, rhs, start, stop, reads, writes, skip=False):
        def f(e, o=out, l=lhsT, r=rhs, s=start, t=stop, k=skip):
            if k:
                return e.matmul(o, l, r, start=s, stop=t, skip_group_check=True)
            return e.matmul(o, l, r, start=s, stop=t)
        self.p.add('tensor', f, reads, writes)

    def act(self, out, in_, func, reads, writes, scale=None, bias=None, accum=None):
        def f(e, o=out, i=in_, fn=func, sc=scale, bi=bias, ac=accum):
            kw = {}
            if sc is not None:
                kw['scale'] = sc
            if bi is not None:
                kw['bias'] = bi
            if ac is not None:
                kw['accum_out'] = ac
            return e.activation(o, i, fn, **kw)
        self.p.add('scalar', f, reads, writes)

    def tt(self, eng, out, in0, in1, op, reads, writes):
        self.p.add(eng, lambda e, o=out, a=in0, b=in1, p=op: e.tensor_tensor(o, a, b, p), reads, writes)

    def ts(self, eng, out, in0, s1, s2, op0, op1, reads, writes):
        def f(e, o=out, a=in0, x=s1, y=s2, p=op0, q=op1):
            if q is None:
                return e.tensor_scalar(o, a, x, None, p)
            return e.tensor_scalar(o, a, x, y, p, q)
        self.p.add(eng, f, reads, writes)

    def stt(self, out, in0, scalar, in1, op0, op1, reads, writes):
        self.p.add('vector', lambda e, o=out, a=in0, s=scalar, b=in1, p=op0, q=op1:
                   e.scalar_tensor_tensor(o, a, s, b, p, q), reads, writes)

    def copy(self, eng, out, in_, reads, writes):
        if eng == 'scalar':
            self.act(out, in_, AF.Copy, reads, writes)
        else:
            self.p.add(eng, lambda e, o=out, i=in_: e.tensor_copy(o, i), reads, writes)

    def memset(self, eng, ap, val, writes):
        self.p.add(eng, lambda e, a=ap, v=val: e.memset(a, v), (), writes)

    def recip(self, out, in_, reads, writes):
        self.p.add('vector', lambda e, o=out, i=in_: e.reciprocal(o, i), reads, writes)

    def transpose(self, out, in_, ident, reads, writes):
        self.p.add('tensor', lambda e, o=out, i=in_, d=ident: e.transpose(o, i, d), reads, writes)


def _perm_local(h):
    own = np.arange(h * OWN, (h + 1) * OWN)
    par = np.arange((1 - h) * OWN, (2 - h) * OWN)
    return np.concatenate([own, par])


class Layer(Builder):
    def nb(self):
        self.psi = (self.psi + 1) % 8
        return self.psi

    def setup_consts(self):
        a = self.arena
        d = self.dr
        self.ident = a.alloc(128, 128, BF16)
        self.ones = a.alloc(128, 128, BF16)
        self.epsA = a.alloc(128, 1, F32)
        stage = a.alloc(128, 128, F32)
        self.dma(stage, d['ident'], [], ['c_stage'])
        self.copy('vector', self.ident, stage, ['c_stage'], ['ident'])
        self.memset('vector', self.ones, 1.0, ['ones'])
        self.memset('vector', self.epsA, EPS, ['eps'])
        self.onescol = a.alloc(128, 12 * 2, BF16)
        self.memset('vector', self.onescol, 1.0, ['onescol'])
        self.valB = a.alloc(128, 48, F32)
        self.valOne = a.alloc(128, 64, F32)
        self.memset('vector', self.valOne, 1.0, ['valOne'])

    def begin_pass(self, L, cs, xsrc, swap, out_raw, final):
        self.L, self.cs, self.xsrc, self.swap, self.out_raw, self.final = L, cs, xsrc, swap, out_raw, final
        self.dma(self.valB, self.dr['valB' + cs], [], ['valB'])
        self.p.phase_barrier()

    def xrow(self, u):
        return (u + OWN) % S if self.swap else u

    def norm_transpose(self, xt_ap, xt_key, gbc, gkey, hT_dst, hT_key, bufs, defer=False):
        junk, ss, xn = bufs['junk'], bufs['ss'], bufs['xn']
        sj = junk.next()
        s1 = ss.next()
        self.act(junk.ap(sj), xt_ap, AF.Square, [xt_key], [junk.key(sj), ss.key(s1, 'a')],
                 accum=ss.ap(s1)[:, 0:1])
        self.act(ss.ap(s1)[:, 1:2], ss.ap(s1)[:, 0:1], AF.Sqrt, [ss.key(s1, 'a'), 'eps'], [ss.key(s1, 'b')],
                 scale=1.0 / D, bias=self.epsA[:, 0:1])
        self.recip(ss.ap(s1)[:, 2:3], ss.ap(s1)[:, 1:2], [ss.key(s1, 'b')], [ss.key(s1, 'c')])
        sx = xn.next()
        self.stt(xn.ap(sx), xt_ap, ss.ap(s1)[:, 2:3], gbc, ALU.mult, ALU.mult,
                 [xt_key, ss.key(s1, 'c'), gkey], [xn.key(sx)])
        def part2():
            b = self.nb()
            pb = self.bank(b).bitcast(BF16)
            for c in range(8):
                self.transpose(pb[:, c * 128:(c + 1) * 128], xn.ap(sx)[:, c * 128:(c + 1) * 128], self.ident,
                               [xn.key(sx), 'ident'], [self.pk(b)])
            self.copy('vector', hT_dst, pb.rearrange('p (c t) -> p c t', c=8), [self.pk(b)], [hT_key])
        if defer:
            return part2
        part2()

    def norm_bufs(self):
        a = self.arena
        return dict(junk=Buf(a, 128, D, BF16, 1), ss=Buf(a, 128, 4, F32, 4), xn=Buf(a, 128, D, BF16, 2))

    def load_w(self, src, kc, ncols, wst, wbf):
        s = wst.next()
        st = wst.ap(s)[:, 0:kc * ncols].rearrange('p (c n) -> p c n', c=kc)
        if kc == 1:
            self.dma(wst.ap(s)[:, 0:ncols], src, [], [wst.key(s)])
        else:
            self.dma(st, src.rearrange('(c p) n -> p c n', p=128), [], [wst.key(s)])
        t = wbf.next()
        wb = wbf.ap(t)[:, 0:kc * ncols].rearrange('p (c n) -> p c n', c=kc)
        self.copy('scalar', wbf.ap(t)[:, 0:kc * ncols], wst.ap(s)[:, 0:kc * ncols], [wst.key(s)], [wbf.key(t)])
        return wb, wbf.key(t)

    def load_w_res(self, src, kc, ncols, wst):
        dst = self.arena.alloc(128, kc * ncols, BF16)
        key = ('wres', self.arena.n)
        done = 0
        per = max(1, (wst.aps[0].shape[1]) // ncols)
        while done < kc:
            k = min(per, kc - done)
            s = wst.next()
            st = wst.ap(s)[:, 0:k * ncols].rearrange('p (c n) -> p c n', c=k)
            self.dma(st, src[done * 128:(done + k) * 128, :].rearrange('(c p) n -> p c n', p=128), [], [wst.key(s)])
            self.copy('scalar', dst[:, done * ncols:(done + k) * ncols], wst.ap(s)[:, 0:k * ncols],
                      [wst.key(s)], [key + (done,)])
            done += k
        keys = [key + (i,) for i in range(0, kc, per)]
        return dst.rearrange('p (c n) -> p c n', c=kc), keys

    def fm_norm(self, banks, nch, gcol, gkey, nfeat, t):
        raw, sq, sd, out = t['raw'], t['sq'], t['sd'], t['cn']
        rs, qs = [], []
        for c in range(nch):
            r = raw.next()
            q = sq.next()
            self.copy('scalar', raw.ap(r), self.bank(banks[c]), [self.pk(banks[c])], [raw.key(r)])
            self.act(sq.ap(q), self.bank(banks[c]), AF.Square, [self.pk(banks[c])], [sq.key(q)])
            rs.append(r)
            qs.append(q)
        b = self.nb()
        for c in range(nch):
            self.mm(self.bank(b), self.ones, sq.ap(qs[c]), c == 0, c == nch - 1,
                    ['ones', sq.key(qs[c])], [self.pk(b)])
        s = sd.next()
        self.act(sd.ap(s), self.bank(b), AF.Sqrt, [self.pk(b), 'eps'], [sd.key(s, 'a')],
                 scale=1.0 / nfeat, bias=self.epsA[:, 0:1])
        self.recip(sd.ap(s), sd.ap(s), [sd.key(s, 'a')], [sd.key(s, 'a')])
        outs = []
        for c in range(nch):
            o = out.next()
            self.stt(out.ap(o), raw.ap(rs[c]), gcol[:, c:c + 1], sd.ap(s), ALU.mult, ALU.mult,
                     [raw.key(rs[c]), sd.key(s, 'a'), gkey], [out.key(o)])
            outs.append(o)
        return outs

    def rope(self, bA, bB, rows, cosap, sinap, tkeys, t, outbuf):
        t1, t2 = t['r1'], t['r2']
        s1 = t1.next()
        s2 = t2.next()
        self.tt('vector', t1.ap(s1)[0:rows], self.bank(bB)[0:rows], sinap, ALU.mult,
                [self.pk(bB)] + tkeys, [t1.key(s1)])
        self.tt('vector', t2.ap(s2)[0:rows], self.bank(bA)[0:rows], cosap, ALU.mult,
                [self.pk(bA)] + tkeys, [t2.key(s2)])
        o = outbuf.next()
        self.tt('gpsimd', outbuf.ap(o)[0:rows], t1.ap(s1)[0:rows], t2.ap(s2)[0:rows], ALU.add,
                [t1.key(s1), t2.key(s2)], [outbuf.key(o)])
        return o

    def vtok(self, lhs_fn, nk, w, wkeys, col0, nh, val, valkey, dst, tile_idx, hd0, t, lkeys):
        vs = t['vst']
        b = self.nb()
        ncol = nh * 64
        for k in range(nk):
            self.mm(self.bank(b)[:, 0:ncol], lhs_fn(k), w[:, k, col0:col0 + ncol], k == 0, k == nk - 1,
                    lkeys + wkeys, [self.pk(b)])
        s = vs.next()
        st = vs.ap(s)[:, 0:nh * 66].rearrange('p (h d) -> p h d', h=nh)
        self.ts('vector', st[:, :, 0:64], self.bank(b)[:, 0:ncol].rearrange('p (h d) -> p h d', h=nh),
                val, None, ALU.mult, None, [self.pk(b), valkey], [vs.key(s, 'v')])
        self.ts('vector', st[:, :, 64:66], self.onescol[:, 0:nh * 2].rearrange('p (h d) -> p h d', h=nh),
                val, None, ALU.mult, None, ['onescol', valkey], [vs.key(s, 'o')])
        self.dma(dst[:, tile_idx, hd0:hd0 + nh, :], st, [vs.key(s, 'v'), vs.key(s, 'o')],
                 [('scr', id(dst), tile_idx, hd0)], q='gpsimd')

    def phase_A(self, L, own):
        a = self.arena
        d = self.dr
        m0 = a.mark()
        ubase = 0 if own else OWN
        cs = self.cs
        hT = a.alloc(128, 8 * OWN, BF16).rearrange('p (c t) -> p c t', c=8)
        gbc = a.alloc(128, D, F32)
        self.dma(gbc, d['g_mix%d' % L], [], ['gmix'])
        m1 = a.mark()
        nbufs = self.norm_bufs()
        xt = Buf(a, 128, D, F32, 3)
        for ti in range(32):
            s = xt.next()
            r0 = self.xrow(ubase + ti * 128)
            self.dma(xt.ap(s), self.xsrc[r0:r0 + 128, :], [], [xt.key(s)])
            nl_ = self.norm_transpose(xt.ap(s), xt.key(s), gbc, 'gmix', hT[:, :, ti * 128:(ti + 1) * 128],
                                      ('hT', ti // 4, ti % 4), nbufs, defer=True)
            if ti > 0:
                late_()
            late_ = nl_
        late_()
        self.p.phase_barrier()
        a.release(m1)
        hkeys = lambda tb: []
        wst = Buf(a, 128, 8 * 256, F32, 4)
        wbf = Buf(a, 128, 8 * 256, BF16, 4)
        wrs = Buf(a, 128, 2048, F32, 2)
        t = dict(raw=Buf(a, 128, 512, F32, 3), sq=Buf(a, 128, 512, BF16, 3), sd=Buf(a, 128, 512, F32, 2),
                 cn=Buf(a, 128, 512, BF16, 4), r1=Buf(a, 128, 512, F32, 2), r2=Buf(a, 128, 512, F32, 2),
                 vst=Buf(a, 128, 8 * 66, BF16, 3))
        ob = Buf(a, 128, 512, BF16, 4)
        tab = Buf(a, 128, 2 * 512, F32, 3)
        gq = a.alloc(128, 2, F32)
        gkv = a.alloc(128, 1, F32)
        self.dma(gq, d['g_q%d' % L], [], ['gq'])
        self.dma(gkv, d['g_kv%d' % L], [], ['gkv'])
        w_in = d['w_in%d' % L]
        w_sw = d['w_sw%d' % L]
        blocks = list(range(8))

        def proj_fm(b, w, wk, c0, m, tb):
            for c in range(8):
                self.mm(self.bank(b)[0:m], w[:, c, c0:c0 + m], hT[:, c, tb * 512:(tb + 1) * 512],
                        c == 0, c == 7, [wk] + hkeys(tb), [self.pk(b)])

        def load_tab(cname, sname, rows, u0, n=512):
            s = tab.next()
            self.dma(tab.ap(s)[0:rows, 0:n], d[cname + cs][:, u0:u0 + n], [], [tab.key(s, 'c')])
            self.dma(tab.ap(s)[0:rows, 512:512 + n], d[sname + cs][:, u0:u0 + n], [], [tab.key(s, 's')])
            return tab.ap(s)[0:rows, 0:n], tab.ap(s)[0:rows, 512:512 + n], [tab.key(s, 'c'), tab.key(s, 's')]

        groups = []
        if own:
            wuq, kuq = self.load_w_res(d['w_uq%d' % L], 2, 768, wrs)
            wuqs, kuqs = self.load_w_res(d['w_uq_sw%d' % L], 2, 768, wrs)

            def g_cq(ws, hook):
                (w, wk), = ws
                def s1(tb):
                    bs = []
                    for c in range(2):
                        b = self.nb()
                        proj_fm(b, w, wk, c * 128, 128, tb)
                        bs.append(b)
                    return self.fm_norm(bs, 2, gq, 'gq', 256, t)
                cns = {0: s1(0)}
                for tb in blocks:
                    if tb == 4:
                        hook()
                    if tb + 1 < 8:
                        cns[tb + 1] = s1(tb + 1)
                    cn = cns.pop(tb)
                    cosap, sinap, tk = load_tab('CA', 'SA', 96, tb * 512)
                    for h in range(8):
                        bA = self.nb()
                        bB = self.nb()
                        for (bb, ww, kk) in ((bA, wuq, kuq), (bB, wuqs, kuqs)):
                            for c in range(2):
                                self.mm(self.bank(bb)[0:96], ww[:, c, h * 96:(h + 1) * 96], t['cn'].ap(cn[c]),
                                        c == 0, c == 1, kk + [t['cn'].key(cn[c])], [self.pk(bb)])
                        o = self.rope(bA, bB, 96, cosap, sinap, tk, t, ob)
                        self.dma(d['S_qA'][h * 96:(h + 1) * 96, tb * 512:(tb + 1) * 512], ob.ap(o)[0:96],
                                 [ob.key(o)], [('sqa', h, tb)], q='gpsimd')
            groups.append(([(w_in[:, 0:256], 8, 256)], g_cq))
        wuk, kuk = self.load_w_res(d['w_uk%d' % L], 1, 512, wrs)
        wuv, kuv = self.load_w_res(d['w_uv%d' % L], 1, 512, wrs)

        def g_ckv(ws, hook):
            (w, wk), (wsw, wswk) = ws
            def s1(tb):
                b = self.nb()
                proj_fm(b, w, wk, 0, 128, tb)
                return self.fm_norm([b], 1, gkv, 'gkv', 128, t)
            cns = {0: s1(0)}
            for tb in blocks:
                if tb == 4:
                    hook()
                if tb + 1 < 8:
                    cns[tb + 1] = s1(tb + 1)
                u0 = ubase + tb * 512
                cn = cns.pop(tb)
                ckvn = t['cn'].ap(cn[0])
                ckey = t['cn'].key(cn[0])
                for ch in range(4):
                    b2 = self.nb()
                    self.mm(self.bank(b2), wuk[:, 0, ch * 128:(ch + 1) * 128], ckvn, True, True, kuk + [ckey], [self.pk(b2)])
                    o = ob.next()
                    self.copy('scalar' if ch % 2 else 'vector', ob.ap(o), self.bank(b2), [self.pk(b2)], [ob.key(o)])
                    self.dma(d['S_kA'][ch * 128:(ch + 1) * 128, u0:u0 + 512], ob.ap(o), [ob.key(o)], [('ska', ch, u0)], q='gpsimd')
                for sub in range(4):
                    self.vtok(lambda k, sub=sub, ckvn=ckvn: ckvn[:, sub * 128:(sub + 1) * 128], 1, wuv, kuv, 0, 8,
                              self.valOne[:, 0:1], 'valOne', d['S_vA'], u0 // 128 + sub, 0, t, [ckey])
                bA = self.nb()
                bB = self.nb()
                proj_fm(bA, w, wk, 128, 32, tb)
                proj_fm(bB, wsw, wswk, 0, 32, tb)
                cosap, sinap, tk = load_tab('Ck', 'Sk', 32, u0)
                o = self.rope(bA, bB, 32, cosap, sinap, tk, t, ob)
                self.dma(d['S_kr'][0:32, u0:u0 + 512], ob.ap(o)[0:32], [ob.key(o)], [('skr', u0)], q='gpsimd')
        if not self.swap:
            groups.append(([(w_in[:, 256:416], 8, 160), (w_sw[:, 0:32], 8, 32)], g_ckv))

        def extB(tb):
            if own:
                return 1024 + tb * 512
            return {0: 5120, 1: 5632, 6: 0, 7: 512}[tb]

        def extC(tb):
            if own:
                return 512 + tb * 512
            return {0: 4608, 7: 0}[tb]
        own_ext = lambda tb: tb * 512
        bB_blocks = blocks if own else [0, 1, 6, 7]
        bC_blocks = blocks if own else [0, 7]

        def rope_group(col0, sw0, dst, row0, blks, extf):
            def run(ws, hook):
                (w, wk), (ws_, wsk) = ws
                for bi_, tb in enumerate(blks):
                    if bi_ == len(blks) // 2:
                        hook()
                    u0 = ubase + tb * 512
                    cosap, sinap, tk = load_tab('cosB', 'sinB', 128, u0)
                    for c in range(2):
                        bA = self.nb()
                        bBk = self.nb()
                        proj_fm(bA, w, wk, c * 128, 128, tb)
                        proj_fm(bBk, ws_, wsk, c * 128, 128, tb)
                        o = self.rope(bA, bBk, 128, cosap, sinap, tk, t, ob)
                        e0 = extf(tb)
                        self.dma(dst[row0 + c * 128: row0 + (c + 1) * 128, e0:e0 + 512], ob.ap(o),
                                 [ob.key(o)], [('sr', id(dst), row0 + c, e0)], q='gpsimd')
            groups.append(([(w_in[:, col0:col0 + 256], 8, 256), (w_sw[:, sw0:sw0 + 256], 8, 256)], run))

        def plain_group(col0, ncols, dst, row0, blks, extf, func):
            def run(ws, hook):
                (w, wk), = ws
                for bi_, tb in enumerate(blks):
                    if bi_ == len(blks) // 2:
                        hook()
                    for c in range(ncols // 128):
                        b = self.nb()
                        proj_fm(b, w, wk, c * 128, 128, tb)
                        o = ob.next()
                        if func is None and c % 2 == 0:
                            self.copy('vector', ob.ap(o), self.bank(b), [self.pk(b)], [ob.key(o)])
                        else:
                            self.act(ob.ap(o), self.bank(b), AF.Copy if func is None else func, [self.pk(b)], [ob.key(o)])
                        e0 = extf(tb)
                        self.dma(dst[row0 + c * 128: row0 + (c + 1) * 128, e0:e0 + 512], ob.ap(o),
                                 [ob.key(o)], [('sp', id(dst), row0 + c, e0)], q='gpsimd')
            groups.append(([(w_in[:, col0:col0 + ncols], 8, ncols)], run))

        def vtok_group(col0, nh, dst, hd0, blks, extf, val, valkey, per_tile_val):
            def run(ws, hook):
                (w, wk), = ws
                for bi_, tb in enumerate(blks):
                    if bi_ == len(blks) // 2:
                        hook()
                    for sub in range(4):
                        et = extf(tb) // 128 + sub
                        v = val[:, et:et + 1] if per_tile_val else val[:, 0:1]
                        self.vtok(lambda k, tb=tb, sub=sub: hT[:, k, tb * 512 + sub * 128: tb * 512 + (sub + 1) * 128],
                                  8, w, [wk], 0, nh, v, valkey, dst, et, hd0, t, hkeys(tb))
            groups.append(([(w_in[:, col0:col0 + nh * 64], 8, nh * 64)], run))

        QB, KB, VB = 416, 1184, 1952
        QC, KC, VC, G0 = 2720, 3232, 3744, 4256
        if own:
            for g in range(3):
                rope_group(QB + g * 256, 32 + g * 256, d['S_qb'], g * 256, blocks, own_ext)
        for g in range(3):
            rope_group(KB + g * 256, 32 + 768 + g * 256, d['S_kb'], g * 256, bB_blocks, extB)
        for g in range(3):
            vtok_group(VB + g * 256, 4, d['S_vb'], g * 4, bB_blocks, extB, self.valB, 'valB', True)
        if own:
            for g in range(2):
                plain_group(QC + g * 256, 256, d['S_qc'], g * 256, blocks, own_ext, None)
        for g in range(2):
            plain_group(KC + g * 256, 256, d['S_kc'], g * 256, bC_blocks, extC, None)
        for g in range(2):
            vtok_group(VC + g * 256, 4, d['S_vc'], g * 4, bC_blocks, extC, self.valOne, 'valOne', False)
        if own:
            for g in range(12):
                plain_group(G0 + g * 256, 256, d['S_gate'], g * 256, blocks, own_ext, AF.Sigmoid)

        def do_loads(g):
            return [self.load_w(src, kc, n, wst, wbf) for (src, kc, n) in g[0]]
        cur = do_loads(groups[0])
        for i, g in enumerate(groups):
            box = {}

            def hook(i=i, box=box):
                if i + 1 < len(groups):
                    box['n'] = do_loads(groups[i + 1])
            g[1](cur, hook)
            cur = box.get('n')
        a.release(m0)
        self.p.phase_barrier()

    def attn_heads(self, heads, nslot=4, nkmax=S):
        a = self.arena
        Qb = Buf(a, 128, OWN, BF16, nslot)
        Kb = Buf(a, 128, nkmax, BF16, nslot)
        dk0 = heads[0]['dk']
        if dk0 == 64:
            for s_ in range(nslot):
                self.memset('vector', Qb.ap(s_)[64:128, :], 0.0, [Qb.key(s_, 'z')])
                self.memset('gpsimd', Kb.ap(s_)[64:128, :], 0.0, [Kb.key(s_, 'z')])
            self.p.phase_barrier()
        dkp = 128 if dk0 == 64 else dk0
        Vb = Buf(a, 128, (nkmax // 128) * 66, BF16, nslot)
        Pb = Buf(a, 128, 512, BF16, 7)
        Osb = Buf(a, 65, 512, F32, 2)
        r32 = Buf(a, 65, 512, F32, 2)
        rhi = Buf(a, 65, 512, BF16, 2)
        rlo = Buf(a, 65, 512, BF16, 2)
        yb = Buf(a, 64, 512, BF16, 3)
        LA = 4
        sbank = [0, 1, 2, 6, 7]
        ucount = 0
        qcount = 0
        if heads[0].get('pre'):
            heads[0]['pre']()
        for hi, hd in enumerate(heads):
            dk, scale = hd['dk'], hd['scale']
            loaded = []
            for src in hd['srcs']:
                sq, sk, sv = Qb.next(), Kb.next(), Vb.next()
                scr, r0 = src['Q']
                self.dma(Qb.ap(sq)[0:dk, :], scr[r0:r0 + dk, :], [], [Qb.key(sq)])
                nkeys = src['nkeys']
                for (scr, r0, rows, dst0) in src['K']:
                    self.dma(Kb.ap(sk)[dst0:dst0 + rows, 0:nkeys], scr[r0:r0 + rows, 0:nkeys], [], [Kb.key(sk, dst0)])
                kkeys = [Kb.key(sk, x[3]) for x in src['K']]
                scr, hidx = src['V']
                nkt = nkeys // 128
                vv = Vb.ap(sv)[:, 0:nkt * 66].rearrange('p (k d) -> p k d', d=66)
                self.dma(vv, scr[:, 0:nkt, hidx, :], [], [Vb.key(sv)])
                loaded.append(dict(Q=Qb.ap(sq), Qk=Qb.key(sq), K=Kb.ap(sk), Kk=kkeys, V=vv, Vk=Vb.key(sv)))
            if hi + 1 < len(heads) and heads[hi + 1].get('pre'):
                heads[hi + 1]['pre']()
            units = []
            for qb in range(NB):
                ul = hd['units'](qb)
                for i, un in enumerate(ul):
                    si, kt, mask, mkey = un[:4]
                    c0, c1 = (un[4], un[5]) if len(un) > 4 else (0, 512)
                    units.append((qb, si, kt, mask, mkey, i == 0, i == len(ul) - 1, c0, c1))
            n = len(units)
            pend = []
            pslots = {}

            def fin_pe(qb, ob_, so, sr):
                bc = 5
                self.mm(self.bank(bc)[0:64], self.ones[64:65, 0:64], rhi.ap(sr)[64:65, :], True, False,
                        ['ones', rhi.key(sr)], [self.pk(bc)])
                self.mm(self.bank(bc)[0:64], self.ones[64:65, 0:64], rlo.ap(sr)[64:65, :], False, True,
                        ['ones', rlo.key(sr)], [self.pk(bc)])
                sy = yb.next()
                self.tt('vector', yb.ap(sy), Osb.ap(so)[0:64], self.bank(bc)[0:64], ALU.mult,
                        [Osb.key(so), self.pk(bc)], [yb.key(sy)])
                scr, row0 = hd['out']
                self.dma(scr[row0:row0 + 64, qb * 512:(qb + 1) * 512], yb.ap(sy), [yb.key(sy)],
                         [('y', id(scr), row0, qb)], q='gpsimd')

            for u in range(n + LA):
                if u < n:
                    qb, si, kt, mask, mkey, first, last, c0, c1 = units[u]
                    L_ = loaded[si]
                    nq = c1 - c0
                    sb = sbank[ucount % 5]
                    sp = Pb.next()
                    pslots[u] = (sb, sp)
                    ucount += 1
                    self.mm(self.bank(sb)[:, 0:nq], L_['K'][0:dkp, kt * 128:(kt + 1) * 128],
                            L_['Q'][0:dkp, qb * 512 + c0:qb * 512 + c1], True, True,
                            L_['Kk'] + [L_['Qk']], [self.pk(sb)])
                    self.act(Pb.ap(sp)[:, 0:nq], self.bank(sb)[:, 0:nq], AF.Exp, [self.pk(sb)], [Pb.key(sp)], scale=scale)
                    if mask is not None:
                        self.tt('vector', Pb.ap(sp)[:, 0:nq], Pb.ap(sp)[:, 0:nq], mask[:, c0:c1], ALU.mult,
                                [Pb.key(sp)] + list(mkey or []), [Pb.key(sp)])
                if u >= LA:
                    v = u - LA
                    qb, si, kt, mask, mkey, first, last, c0, c1 = units[v]
                    L_ = loaded[si]
                    sb, sp = pslots.pop(v)
                    ob_ = 3 + (qb % 2)
                    self.mm(self.bank(ob_)[0:65, c0:c1], L_['V'][:, kt, 0:65], Pb.ap(sp)[:, 0:c1 - c0], first, last,
                            [L_['Vk'], Pb.key(sp)], [self.pk(ob_)], skip=True)
                    if last:
                        so = Osb.next()
                        sr = r32.next()
                        self.copy('vector', Osb.ap(so), self.bank(ob_)[0:65], [self.pk(ob_)], [Osb.key(so)])
                        self.act(r32.ap(sr)[64:65], Osb.ap(so)[64:65], AF.Ln, [Osb.key(so)], [r32.key(sr)])
                        self.act(r32.ap(sr)[64:65], r32.ap(sr)[64:65], AF.Exp, [r32.key(sr)], [r32.key(sr)], scale=-1.0)
                        self.copy('gpsimd', rhi.ap(sr)[64:65], r32.ap(sr)[64:65], [r32.key(sr)], [rhi.key(sr)])
                        self.tt('gpsimd', rlo.ap(sr)[64:65], r32.ap(sr)[64:65], rhi.ap(sr)[64:65], ALU.subtract,
                                [r32.key(sr), rhi.key(sr)], [rlo.key(sr)])
                        pend.append((u, qb, ob_, so, sr))
                while pend and (u - pend[0][0] >= 3 or u == n + LA - 1):
                    _, qb, ob_, so, sr = pend.pop(0)
                    fin_pe(qb, ob_, so, sr)

    def phase_M(self, L):
        a = self.arena
        d = self.dr
        m0 = a.mark()
        heads = []
        for h in range(8):
            src = dict(Q=(d['S_qA'], h * 96), K=[(d['S_kA'], h * 64, 64, 0), (d['S_kr'], 0, 32, 64)],
                       V=(d['S_vA'], h), nkeys=S)
            heads.append(dict(dk=96, scale=96 ** -0.5, srcs=[src], out=(d['S_ya'], h * 64),
                              units=lambda qb: [(0, kt, None, None) for kt in range(64)]))
        self.attn_heads(heads)
        a.release(m0)
        self.p.phase_barrier()

    def load_masks(self, name, n):
        a = self.arena
        d = self.dr
        mk = a.alloc(128, n * 512, BF16).rearrange('p (n f) -> p n f', n=n)
        m = a.mark()
        st = Buf(a, 128, 4 * 512, F32, 2)
        i = 0
        while i < n:
            k = min(4, n - i)
            s = st.next()
            self.dma(st.ap(s)[:, 0:k * 512].rearrange('p (n f) -> p n f', n=k),
                     d[name][i:i + k].rearrange('n p f -> p n f'), [], [st.key(s)])
            self.copy('scalar', mk[:, i:i + k, :], st.ap(s)[:, 0:k * 512].rearrange('p (n f) -> p n f', n=k),
                      [st.key(s)], [(name, i)])
            i += k
        self.p.phase_barrier()
        a.release(m)
        return mk

    def phase_B(self, L):
        a = self.arena
        d = self.dr
        m0 = a.mark()
        mk = self.load_masks('Bmask', 34)
        rels = [list(range(-1, 5)), list(range(-2, 6)), list(range(-8, 12))]
        offs = [0, 6, 14]
        heads = []
        for j in range(4):
            srcs = []
            for g in range(3):
                hh = g * 4 + j
                srcs.append(dict(Q=(d['S_qb'], hh * 64), K=[(d['S_kb'], hh * 64, 64, 0)], V=(d['S_vb'], hh), nkeys=EXTB))

            def units(qb, rels=rels, offs=offs):
                ul = []
                for g in (2, 1, 0):
                    reach = 64 * (1, 4, 16)[g]
                    for ri, rel in enumerate(rels[g]):
                        c0 = max(0, 128 * rel - reach)
                        c1 = min(512, 128 * rel + 128 + reach)
                        ul.append((g, 8 + 4 * qb + rel, mk[:, offs[g] + ri, :], None, c0, c1))
                ul.sort(key=lambda x: 0 if (x[4] == 0 and x[5] == 512) else 1)
                return ul
            heads.append(dict(dk=64, scale=0.125, srcs=srcs, out=(d['S_yb'], j * 64), units=units))
        self.attn_heads(heads, nslot=4, nkmax=EXTB)
        a.release(m0)
        self.p.phase_barrier()

    def phase_C(self, L):
        a = self.arena
        d = self.dr
        m0 = a.mark()
        nav = self.load_masks('NAvalid' + self.cs, 24)
        NS = 24 * 64
        ebst = Buf(a, 128, 2 * NS, F32, 2)
        EF = Buf(a, 128, 2 * NS, BF16, 2)
        T = Buf(a, 128, 16 * 512, BF16, 2)
        heads = []
        for h in range(8):
            state = {}

            def pre(h=h, state=state):
                s = ebst.next()
                self.dma(ebst.ap(s)[:, 0:NS], d['ebias%d' % L][h], [], [ebst.key(s, 'f')])
                self.dma(ebst.ap(s)[:, NS:2 * NS], d['ebiasz%d' % L][h], [], [ebst.key(s, 'z')])
                se = EF.next()
                self.act(EF.ap(se), ebst.ap(s), AF.Exp, [ebst.key(s, 'f'), ebst.key(s, 'z')], [EF.key(se)])
                ef = EF.ap(se)[:, 0:NS].rearrange('p (s c) -> p s c', c=64)
                st_ = T.next()
                tt_ = T.ap(st_).rearrange('p (n f) -> p n f', n=16)
                for e_, ty in enumerate((0, 2)):
                    for j in range(8):
                        w0 = 15 - 2 * j
                        self.tt('gpsimd', tt_[:, e_ * 8 + j, :].rearrange('p (r c) -> p r c', c=64),
                                ef[:, w0:w0 + 8, :],
                                nav[:, ty * 8 + j, :].rearrange('p (r c) -> p r c', c=64),
                                ALU.mult, [EF.key(se)], [T.key(st_, (e_, j))])
                state['T'] = tt_
                state['k'] = st_
                state['EZ'] = EF.ap(se)[:, NS:2 * NS]
                state['ek'] = EF.key(se)

            def units(qb, state=state):
                ul = []
                for j in range(8):
                    w0 = 15 - 2 * j
                    if qb == 0 or qb == 7:
                        e_ = 0 if qb == 0 else 1
                        ul.append((0, 4 * qb + 2 + j, state['T'][:, e_ * 8 + j, :], [T.key(state['k'], (e_, j))]))
                    else:
                        r0 = max(0, 2 * j - 7)
                        r1 = min(7, 2 * j + 1)
                        ul.append((0, 4 * qb + 2 + j, state['EZ'][:, w0 * 64:(w0 + 8) * 64], [state['ek']],
                                   r0 * 64, (r1 + 1) * 64))
                ul.sort(key=lambda x: 0 if (len(x) < 5 or (x[4] == 0 and x[5] == 512)) else 1)
                return ul
            src = dict(Q=(d['S_qc'], h * 64), K=[(d['S_kc'], h * 64, 64, 0)], V=(d['S_vc'], h), nkeys=EXTC)
            heads.append(dict(dk=64, scale=0.125, srcs=[src], out=(d['S_yc'], h * 64), units=units, pre=pre,
                              tkeys=state))
        self.attn_heads(heads, nslot=3, nkmax=EXTC)
        a.release(m0)
        self.p.phase_barrier()

    def phase_G(self, L):
        a = self.arena
        d = self.dr
        self.h2T = a.alloc(128, 8 * OWN, BF16).rearrange('p (c t) -> p c t', c=8)
        self.mG = a.mark()
        wst = Buf(a, 128, 2048, F32, 2)
        wpa, kpa = self.load_w_res(d['w_pa%d' % L], 4, 1024, wst)
        wpb, kpb = self.load_w_res(d['w_pb%d' % L], 2, 1024, wst)
        wpc, kpc = self.load_w_res(d['w_pc%d' % L], 4, 1024, wst)
        wo, ko = self.load_w_res(d['w_o%d' % L], 8, 1024, wst)
        gbc = a.alloc(128, D, F32)
        self.dma(gbc, d['g_ffn%d' % L], [], ['gffn'])
        ya = Buf(a, 128, 4 * 512, BF16, 2)
        yb = Buf(a, 128, 2 * 512, BF16, 2)
        yc = Buf(a, 128, 4 * 512, BF16, 2)
        gt = Buf(a, 128, 3 * 512, BF16, 3)
        mg = Buf(a, 128, 8 * 512, BF16, 2)
        tf = Buf(a, 128, 512, F32, 5)
        xt = Buf(a, 128, D, F32, 3)
        nbufs = self.norm_bufs()
        def stage1(tb):
            t0 = tb * 512
            sa, sb_, sc = ya.next(), yb.next(), yc.next()
            self.dma(ya.ap(sa).rearrange('p (c t) -> p c t', c=4),
                     d['S_ya'][:, t0:t0 + 512].rearrange('(c p) t -> p c t', p=128), [], [ya.key(sa)])
            self.dma(yb.ap(sb_).rearrange('p (c t) -> p c t', c=2),
                     d['S_yb'][:, t0:t0 + 512].rearrange('(c p) t -> p c t', p=128), [], [yb.key(sb_)])
            self.dma(yc.ap(sc).rearrange('p (c t) -> p c t', c=4),
                     d['S_yc'][:, t0:t0 + 512].rearrange('(c p) t -> p c t', p=128), [], [yc.key(sc)])
            yA = ya.ap(sa).rearrange('p (c t) -> p c t', c=4)
            yB = yb.ap(sb_).rearrange('p (c t) -> p c t', c=2)
            yC = yc.ap(sc).rearrange('p (c t) -> p c t', c=4)
            sm = mg.next()
            mgv = mg.ap(sm).rearrange('p (c t) -> p c t', c=8)
            for oc in range(8):
                sg = gt.next()
                gv = gt.ap(sg).rearrange('p (c t) -> p c t', c=3)
                self.dma(gv, d['S_gate'].rearrange('(b c p) t -> c p b t', b=3, p=128)[oc][:, :, t0:t0 + 512],
                         [], [gt.key(sg)])
                ts_ = []
                for (y, yk, w, wk, nk, bi) in ((yA, ya.key(sa), wpa, kpa, 4, 0), (yB, yb.key(sb_), wpb, kpb, 2, 1),
                                               (yC, yc.key(sc), wpc, kpc, 4, 2)):
                    b = self.nb()
                    for c in range(nk):
                        self.mm(self.bank(b), w[:, c, oc * 128:(oc + 1) * 128], y[:, c, :], c == 0, c == nk - 1,
                                wk + [yk], [self.pk(b)])
                    s = tf.next()
                    self.tt('vector', tf.ap(s), self.bank(b), gv[:, bi, :], ALU.mult, [self.pk(b), gt.key(sg)], [tf.key(s)])
                    ts_.append(s)
                s4 = tf.next()
                self.tt('gpsimd', tf.ap(s4), tf.ap(ts_[0]), tf.ap(ts_[1]), ALU.add, [tf.key(ts_[0]), tf.key(ts_[1])], [tf.key(s4)])
                self.tt('gpsimd', mgv[:, oc, :], tf.ap(s4), tf.ap(ts_[2]), ALU.add, [tf.key(s4), tf.key(ts_[2])],
                        [mg.key(sm, oc)])
            mkeys = [mg.key(sm, oc) for oc in range(8)]
            return mgv, mkeys

        def stage2(tb, mgv, mkeys):
            t0 = tb * 512
            late = None
            for sub in range(4):
                sx = xt.next()
                r0 = t0 + sub * 128
                rx = self.xrow(r0)
                self.dma(xt.ap(sx), self.xsrc[rx:rx + 128, :], [], [xt.key(sx)])
                for half in range(2):
                    b = self.nb()
                    for c in range(8):
                        self.mm(self.bank(b), mgv[:, c, sub * 128:(sub + 1) * 128], wo[:, c, half * 512:(half + 1) * 512],
                                c == 0, c == 7, mkeys + ko, [self.pk(b)])
                    self.tt('vector', xt.ap(sx)[:, half * 512:(half + 1) * 512], xt.ap(sx)[:, half * 512:(half + 1) * 512],
                            self.bank(b), ALU.add, [xt.key(sx), self.pk(b)], [xt.key(sx)])
                self.dma(d['S_x1'][r0:r0 + 128, :], xt.ap(sx), [xt.key(sx)], [('x1', r0)], q='gpsimd')
                ti = tb * 4 + sub
                nxt_late = self.norm_transpose(xt.ap(sx), xt.key(sx), gbc, 'gffn',
                                               self.h2T[:, :, ti * 128:(ti + 1) * 128], ('h2T', tb, sub), nbufs,
                                               defer=True)
                if late is not None:
                    late()
                late = nxt_late
            late()
        pend = {0: stage1(0)}
        for tb in range(NB):
            if tb + 1 < NB:
                pend[tb + 1] = stage1(tb + 1)
            stage2(tb, *pend.pop(tb))
        a.release(self.mG)
        self.p.phase_barrier()

    def phase_F(self, L, final):
        a = self.arena
        d = self.dr
        h2T = self.h2T
        m1 = a.mark()
        wst = Buf(a, 128, 8 * 256, F32, 4)
        wbf = Buf(a, 128, 8 * 256, BF16, 4)
        sgb = Buf(a, 128, 512, F32, 3)
        ob = Buf(a, 128, 512, BF16, 4)
        cols = list(range(0, DFF, 256))

        def f_loads(col):
            return (self.load_w(d['w1%d' % L][:, col:col + 256], 8, 256, wst, wbf),
                    self.load_w(d['w3%d' % L][:, col:col + 256], 8, 256, wst, wbf))
        cur = f_loads(cols[0])
        for gi, col in enumerate(cols):
            nxt = f_loads(cols[gi + 1]) if gi + 1 < len(cols) else None
            (w1, k1), (w3, k3) = cur
            for tb in range(NB):
                for ch in range(2):
                    b1 = self.nb()
                    b3 = self.nb()
                    for (bb, ww, kk) in ((b1, w1, k1), (b3, w3, k3)):
                        for c in range(8):
                            self.mm(self.bank(bb), ww[:, c, ch * 128:(ch + 1) * 128], h2T[:, c, tb * 512:(tb + 1) * 512],
                                    c == 0, c == 7, [kk], [self.pk(bb)])
                    s = sgb.next()
                    self.act(sgb.ap(s), self.bank(b1), AF.Silu, [self.pk(b1)], [sgb.key(s)])
                    o = ob.next()
                    self.tt('vector', ob.ap(o), sgb.ap(s), self.bank(b3), ALU.mult, [sgb.key(s), self.pk(b3)], [ob.key(o)])
                    f = col // 128 + ch
                    self.dma(d['S_act'][f * 128:(f + 1) * 128, tb * 512:(tb + 1) * 512], ob.ap(o), [ob.key(o)],
                             [('sact', f, tb)], q='gpsimd')
            cur = nxt
        a.release(m1)
        self.p.phase_barrier()
        a.release(self.mG)
        a.off = 0 + self._const_end
        wst = Buf(a, 128, 4096, F32, 2)
        w2, k2 = self.load_w_res(d['w2%d' % L], 22, 1024, wst)
        gfin = a.alloc(128, D, F32)
        self.dma(gfin, d['g_final'], [], ['gfin'])
        at = Buf(a, 128, 22 * 512, BF16, 2)
        xt = Buf(a, 128, D, F32, 3)
        yn = Buf(a, 128, D, F32, 2)
        junk = Buf(a, 128, D, BF16, 1)
        ss = Buf(a, 128, 4, F32, 4)
        for tb in range(NB):
            t0 = tb * 512
            sa = at.next()
            av = at.ap(sa).rearrange('p (f t) -> p f t', f=22)
            self.dma(av, d['S_act'][:, t0:t0 + 512].rearrange('(f p) t -> p f t', p=128), [], [at.key(sa)])
            for sub in range(4):
                r0 = t0 + sub * 128
                sx = xt.next()
                self.dma(xt.ap(sx), d['S_x1'][r0:r0 + 128, :], [], [xt.key(sx)])
                for half in range(2):
                    b = self.nb()
                    for f in range(22):
                        self.mm(self.bank(b), av[:, f, sub * 128:(sub + 1) * 128], w2[:, f, half * 512:(half + 1) * 512],
                                f == 0, f == 21, [at.key(sa)] + k2, [self.pk(b)])
                    self.tt('vector', xt.ap(sx)[:, half * 512:(half + 1) * 512], xt.ap(sx)[:, half * 512:(half + 1) * 512],
                            self.bank(b), ALU.add, [xt.key(sx), self.pk(b)], [xt.key(sx)])
                self.dma(self.out_raw[r0:r0 + 128, :], xt.ap(sx), [xt.key(sx)], [('yraw', r0)], q='gpsimd')
                if final:
                    sj = junk.next()
                    s1 = ss.next()
                    self.act(junk.ap(sj), xt.ap(sx), AF.Square, [xt.key(sx)], [junk.key(sj), ss.key(s1, 'a')],
                             accum=ss.ap(s1)[:, 0:1])
                    self.act(ss.ap(s1)[:, 1:2], ss.ap(s1)[:, 0:1], AF.Sqrt, [ss.key(s1, 'a'), 'eps'], [ss.key(s1, 'b')],
                             scale=1.0 / D, bias=self.epsA[:, 0:1])
                    self.recip(ss.ap(s1)[:, 2:3], ss.ap(s1)[:, 1:2], [ss.key(s1, 'b')], [ss.key(s1, 'c')])
                    sy = yn.next()
                    self.stt(yn.ap(sy), xt.ap(sx), ss.ap(s1)[:, 2:3], gfin, ALU.mult, ALU.mult,
                             [xt.key(sx), ss.key(s1, 'c'), 'gfin'], [yn.key(sy)])
                    self.dma(d['y_norm'][r0:r0 + 128, :], yn.ap(sy), [yn.key(sy)], [('ynorm', r0)], q='gpsimd')
        a.off = self._const_end
        self.p.phase_barrier()


W_SHAPES = dict(w_in=(1024, IN_COLS), w_sw=(1024, 1568), w_uq=(256, 768), w_uq_sw=(256, 768), w_uk=(128, 512),
                w_uv=(128, 512), g_mix=(128, D), g_q=(128, 2), g_kv=(128, 1), ebias=(8, 128, 24 * 64), ebiasz=(8, 128, 24 * 64),
                w_pa=(512, D), w_pb=(256, D), w_pc=(512, D), w_o=(D, D), g_ffn=(128, D), w1=(D, DFF), w3=(D, DFF),
                w2=(DFF, D))
C_SHAPES = dict(ident=(128, 128), valB=(128, 48), CA=(96, OWN), SA=(96, OWN), Ck=(32, S), Sk=(32, S),
                cosB=(128, S), sinB=(128, S), Bmask=(34, 128, 512), NAvalid=(24, 128, 512), g_final=(128, D))
SCR = dict(S_qA=((768, OWN), BF16), S_kA=((512, S), BF16), S_kr=((32, S), BF16), S_vA=((128, 64, 8, 66), BF16),
           S_qb=((768, OWN), BF16), S_kb=((768, EXTB), BF16), S_vb=((128, 48, 12, 66), BF16),
           S_qc=((512, OWN), BF16), S_kc=((512, EXTC), BF16), S_vc=((128, 40, 8, 66), BF16),
           S_gate=((3072, OWN), BF16), S_ya=((512, OWN), BF16), S_yb=((256, OWN), BF16), S_yc=((512, OWN), BF16),
           S_x1=((OWN, D), F32), S_act=((DFF, OWN), BF16))


PC_NAMES = ('valB', 'CA', 'SA', 'Ck', 'Sk', 'cosB', 'sinB', 'NAvalid')


def build_nc(fused=True, dbg=None, upto=99):
    nc = bass.Bass("TRN2", target_bir_lowering=False)
    B = Layer(nc, dbg)
    B.dram_in('xs', (S, D))
    sets = ('', '_p') if fused else ('',)
    for k, shp in C_SHAPES.items():
        if k in PC_NAMES:
            for cs in sets:
                B.dram_in(k + cs, shp)
        else:
            B.dram_in(k, shp)
    for l in range(2 if fused else 1):
        for k, shp in W_SHAPES.items():
            B.dram_in(k + str(l), shp)
    B.dram_out('y_raw', (OWN, D))
    B.dram_out('y_norm', (OWN, D))
    for k, (shp, dt) in SCR.items():
        B.dram_scr(k, shp, dt)
    B.setup_consts()
    B._const_end = B.arena.off
    B.p.phase_barrier()

    def run_pass(L, cs, xsrc, swap, out_raw, final):
        B.begin_pass(L, cs, xsrc, swap, out_raw, final)
        phs = [lambda: B.phase_A(L, True), lambda: B.phase_A(L, False), lambda: B.phase_M(L), lambda: B.phase_B(L),
               lambda: B.phase_C(L), lambda: B.phase_G(L), lambda: B.phase_F(L, final)]
        for ph in phs[:upto]:
            ph()
    if fused:
        X1 = B.dram_scr('X1', (S, D), F32)
        run_pass(0, '', B.dr['xs'], False, X1[0:OWN], False)
        run_pass(0, '_p', B.dr['xs'], True, X1[OWN:S], False)
        run_pass(1, '', X1, False, B.dr['y_raw'], True)
    else:
        run_pass(0, '', B.dr['xs'], False, B.dr['y_raw'], True)
    B.p.emit()
    return nc, B


def _rope_tabs(pos, dim):
    inv = np.power(np.float32(10000.0), -np.arange(0, dim, 2, dtype=np.float32) / np.float32(dim)).astype(np.float32)
    ang = pos.astype(np.float32)[:, None] * inv[None, :]
    return np.cos(ang).astype(np.float32), np.sin(ang).astype(np.float32)


def core_consts(h):
    perm = _perm_local(h)
    c = {}
    c['ident'] = np.eye(128, dtype=np.float32)
    e = np.arange(EXTB)
    t = h * OWN - 1024 + e
    valid = ((t >= 0) & (t < S)).astype(np.float32)
    c['valB'] = np.ascontiguousarray(valid.reshape(48, 128).T)
    cb, sb = _rope_tabs(perm, 64)
    f = np.arange(128)
    sign = np.where((f % 64) < 32, -1.0, 1.0).astype(np.float32)
    c['cosB'] = np.ascontiguousarray(cb[:, f % 32].T)
    c['sinB'] = np.ascontiguousarray((sb[:, f % 32] * sign[None, :]).T)
    ca, sa = _rope_tabs(perm, 32)
    j = np.arange(32)
    sgn = np.where(j < 16, -1.0, 1.0).astype(np.float32)
    c['Ck'] = np.ascontiguousarray(ca[:, j % 16].T)
    c['Sk'] = np.ascontiguousarray((sa[:, j % 16] * sgn[None, :]).T)
    CA = np.ones((96, OWN), np.float32)
    SA = np.zeros((96, OWN), np.float32)
    CA[64:96] = c['Ck'][:, :OWN]
    SA[64:96] = c['Sk'][:, :OWN]
    c['CA'], c['SA'] = CA, SA
    masks = []
    ii = np.arange(128)[:, None]
    jj = np.arange(512)[None, :]
    for dil, rels in ((1, range(-1, 5)), (4, range(-2, 6)), (16, range(-8, 12))):
        for rel in rels:
            dlt = 128 * rel + ii - jj
            masks.append(((dlt % dil == 0) & (np.abs(dlt) <= 64 * dil)).astype(np.float32))
    c['Bmask'] = np.stack(masks)
    nav = np.zeros((24, 128, 512), np.float32)
    for ty, i in enumerate((0, 1, 7)):
        for jx in range(8):
            for a_ in range(2):
                for rho in range(8):
                    Rr = 64 * h + 8 * i + rho
                    KR = 64 * h + 8 * i - 4 + 2 * jx + a_
                    rs = min(max(Rr - 4, 0), 120)
                    ok = (0 <= KR < 128) and (rs <= KR < rs + 8)
                    if ok:
                        nav[ty * 8 + jx, a_ * 64:(a_ + 1) * 64, rho * 64:(rho + 1) * 64] = 1.0
    c['NAvalid'] = nav
    return c


def layer_weights(inp, l):
    w = {}
    w_in = np.asarray(inp['w_in'][l], np.float32)
    w['w_in'] = w_in

    def swap_heads(cols, nh, hd):
        half = hd // 2
        parts = []
        for hh in range(nh):
            b = hh * hd
            parts += [cols[:, b + half:b + hd], cols[:, b:b + half]]
        return np.concatenate(parts, axis=1)
    w['w_sw'] = np.ascontiguousarray(np.concatenate([swap_heads(w_in[:, 384:416], 1, 32), swap_heads(w_in[:, 416:1184], 12, 64),
                                                     swap_heads(w_in[:, 1184:1952], 12, 64)], axis=1))
    wuq = np.asarray(inp['w_uq'][l], np.float32)
    w['w_uq'] = wuq
    sw = wuq.copy()
    for hh in range(8):
        b = hh * 96 + 64
        sw[:, b:b + 16] = wuq[:, b + 16:b + 32]
        sw[:, b + 16:b + 32] = wuq[:, b:b + 16]
    w['w_uq_sw'] = sw
    wukv = np.asarray(inp['w_ukv'][l], np.float32).reshape(128, 8, 128)
    w['w_uk'] = np.ascontiguousarray(wukv[:, :, :64].reshape(128, 512))
    w['w_uv'] = np.ascontiguousarray(wukv[:, :, 64:].reshape(128, 512))
    w['g_mix'] = np.ascontiguousarray(np.broadcast_to(np.asarray(inp['g_mix'][l], np.float32)[None, :], (128, D)))
    w['g_ffn'] = np.ascontiguousarray(np.broadcast_to(np.asarray(inp['g_ffn'][l], np.float32)[None, :], (128, D)))
    w['g_q'] = np.ascontiguousarray(np.asarray(inp['g_q'][l], np.float32).reshape(2, 128).T)
    w['g_kv'] = np.ascontiguousarray(np.asarray(inp['g_kv'][l], np.float32).reshape(128, 1))
    rpb = np.asarray(inp['rpb'][l], np.float32)
    kc = np.arange(64)[:, None]
    cc = np.arange(64)[None, :]
    cs = np.clip(cc - 8, 0, 48)
    colok = (kc >= cs) & (kc < cs + 16)
    dc = np.clip(kc - cc + 15, 0, 30)
    NEG = np.float32(-30000.0)
    ebf = np.full((8, 64, 23, 64), NEG, np.float32)
    ebz = np.full((8, 64, 23, 64), NEG, np.float32)
    for slot in range(23):
        dr = 18 - slot
        if 0 <= dr <= 14:
            vals = np.where(colok[None], rpb[:, dr][:, dc], NEG)
            ebf[:, :, slot, :] = vals
            if 3 <= dr <= 10:
                ebz[:, :, slot, :] = vals

    def shifted(e):
        pad = np.full((8, 64, 1, 64), NEG, np.float32)
        lo = np.concatenate([e, pad], axis=2)
        hi = np.concatenate([pad, e], axis=2)
        return np.ascontiguousarray(np.concatenate([lo, hi], axis=1).reshape(8, 128, 24 * 64))
    w['ebias'] = shifted(ebf)
    w['ebiasz'] = shifted(ebz)
    for k in ('w_pa', 'w_pb', 'w_pc', 'w_o', 'w1', 'w3', 'w2'):
        w[k] = np.ascontiguousarray(np.asarray(inp[k][l], np.float32))
    return w


_NC_CACHE = {}


def kernel(**inputs):
    x = np.asarray(inputs['x'], np.float32)
    gfin_bc = np.ascontiguousarray(np.broadcast_to(np.asarray(inputs['g_final'], np.float32)[None, :], (128, D)))
    consts = [core_consts(0), core_consts(1)]
    if 'f' not in _NC_CACHE:
        _NC_CACHE['f'] = build_nc(True)
    nc, B = _NC_CACHE['f']
    ws = [layer_weights(inputs, l) for l in range(2)]
    in_maps = []
    for c in range(8):
        b, h = c // 2, c % 2
        m = {'xs': np.ascontiguousarray(x[b][_perm_local(h)]), 'g_final': gfin_bc}
        for k, v in consts[h].items():
            m[k] = v
        for k in PC_NAMES:
            m[k + '_p'] = consts[1 - h][k]
        for l in range(2):
            for k, v in ws[l].items():
                m[k + str(l)] = v
        in_maps.append(m)
    res = run_bass_kernel_spmd(nc, in_maps, core_ids=list(range(8)))
    out = np.empty_like(x)
    for c in range(8):
        b, h = c // 2, c % 2
        out[b, h * OWN:(h + 1) * OWN] = res.results[c]['y_norm']
    return out
```
